# Optimizing a Trainium2 kernel written in Bass

```python
import math
import jax, jax.numpy as jnp
from jax import lax
import numpy as np

D_MODEL = 1024
BATCH = 2
SEQ = 8192
DEPTH = 1

D_FF = 2816
NORM_EPS = 1e-6
SSD_HEAD_DIM = 64
SSD_HEADS = D_MODEL // SSD_HEAD_DIM
SSD_WIDTH = SSD_HEADS * SSD_HEAD_DIM
SSD_GROUPS = 2
SSD_STATE = 128
SSD_CONV = 4
SSD_CHUNK = 128
ATT_QK_DIM = 64
ATT_V_DIM = 2 * ATT_QK_DIM
ATT_HEADS = D_MODEL // ATT_V_DIM
ATT_WIDTH = ATT_HEADS * ATT_V_DIM
ATT_BLOCK = 128
MIX_WIDTH = SSD_WIDTH + ATT_WIDTH
CONV_CH = SSD_WIDTH + 2 * SSD_GROUPS * SSD_STATE
QK_WIDTH = ATT_HEADS * 2 * ATT_QK_DIM
IN_SIZES = (SSD_WIDTH, CONV_CH, SSD_HEADS, QK_WIDTH, QK_WIDTH, ATT_WIDTH)
IN_WIDTH = SSD_WIDTH + CONV_CH + SSD_HEADS + 2 * QK_WIDTH + ATT_WIDTH

kernel_name = "hymba_ssd_diffattn_macaron"


def rms_norm(x, w):
    xf = x.astype(jnp.float32)
    y = xf * lax.rsqrt(jnp.mean(xf * xf, axis=-1, keepdims=True) + NORM_EPS)
    return (y * w.astype(jnp.float32)).astype(x.dtype)


def swiglu(h, w_gate, w_up, w_down):
    return (jax.nn.silu(h @ w_gate) * (h @ w_up)) @ w_down


def causal_dwconv(u, w, b):
    k = w.shape[0]
    out = lax.conv_general_dilated(
        u, w[:, None, :].astype(u.dtype), window_strides=(1,), padding=[(k - 1, 0)],
        dimension_numbers=("NWC", "WIO", "NWC"), feature_group_count=u.shape[-1])
    return out + b.astype(u.dtype)


def ssd_chunked(xs, dt, a, bm, cm, d_skip):
    bsz, s, nh, hp = xs.shape
    g, n = bm.shape[2], bm.shape[3]
    hg = nh // g
    L = SSD_CHUNK
    nc = s // L
    xs = xs.reshape(bsz, nc, L, g, hg, hp)
    dt = dt.reshape(bsz, nc, L, g, hg)
    bm = bm.reshape(bsz, nc, L, g, n)
    cm = cm.reshape(bsz, nc, L, g, n)
    a_cs = jnp.cumsum(dt * a.reshape(g, hg), axis=2)
    xdt = xs * dt[..., None]
    seg = a_cs[:, :, :, None] - a_cs[:, :, None, :]
    causal = jnp.tril(jnp.ones((L, L), dtype=bool))[None, None, :, :, None, None]
    decay = jnp.where(causal, jnp.exp(jnp.where(causal, seg, 0.0)), 0.0)
    cb = jnp.einsum("bcign,bcjgn->bcijg", cm, bm)
    y_diag = jnp.einsum("bcijgh,bcjghp->bcighp", cb[..., None] * decay, xdt)
    decay_states = jnp.exp(a_cs[:, :, -1:] - a_cs)
    states = jnp.einsum("bclgn,bclgh,bclghp->bcghpn", bm, decay_states, xdt)
    chunk_decay = jnp.exp(a_cs[:, :, -1])

    def step(h, inp):
        st, dec = inp
        return dec[..., None, None] * h + st, h

    h0 = jnp.zeros((bsz, g, hg, hp, n), dtype=states.dtype)
    _, prev = lax.scan(step, h0, (jnp.moveaxis(states, 1, 0), jnp.moveaxis(chunk_decay, 1, 0)))
    prev = jnp.moveaxis(prev, 0, 1)
    y_off = jnp.einsum("bclgn,bcghpn,bclgh->bclghp", cm, prev, jnp.exp(a_cs))
    y = y_diag + y_off + xs * d_skip.astype(jnp.float32).reshape(g, hg)[:, :, None]
    return y.reshape(bsz, s, nh * hp)


def diff_attention(q, k, v, lam):
    bsz, s, nh = q.shape[0], q.shape[1], q.shape[2]
    nb = s // ATT_BLOCK
    scale = ATT_QK_DIM ** -0.5
    kf = k.astype(jnp.float32)
    vf = v.astype(jnp.float32)
    qb = jnp.moveaxis(q.astype(jnp.float32).reshape(bsz, nb, ATT_BLOCK, nh, 2, ATT_QK_DIM), 1, 0)
    kpos = jnp.arange(s)

    def block(args):
        qi, i = args
        sc = jnp.einsum("bqhtd,bkhtd->bhtqk", qi, kf) * scale
        qpos = i * ATT_BLOCK + jnp.arange(ATT_BLOCK)
        mask = qpos[:, None] >= kpos[None, :]
        p = jax.nn.softmax(jnp.where(mask, sc, -jnp.inf), axis=-1)
        att = p[:, :, 0] - lam * p[:, :, 1]
        return jnp.einsum("bhqk,bkhd->bqhd", att, vf)

    out = lax.map(block, (qb, jnp.arange(nb)))
    return jnp.moveaxis(out, 0, 1).reshape(bsz, s, nh, ATT_V_DIM)


def setup_inputs(seed: int = 0) -> dict:
    key = jax.random.key(seed)
    ks = jax.random.split(key, 24)
    f32 = jnp.float32

    def nrm(k, shape, scale):
        return jax.random.normal(k, shape, f32) * scale

    def gain(k, shape):
        return 1.0 + 0.02 * jax.random.normal(k, shape, f32)

    dt0 = jnp.exp(jax.random.uniform(ks[9], (DEPTH, SSD_HEADS), f32, math.log(1e-3), math.log(1e-1)))
    dt_bias = dt0 + jnp.log(-jnp.expm1(-dt0))
    return {
        "x": jax.random.normal(ks[0], (BATCH, SEQ, D_MODEL), f32),
        "ffn1_norm_w": gain(ks[1], (DEPTH, D_MODEL)),
        "ffn1_w_gate": nrm(ks[2], (DEPTH, D_MODEL, D_FF), D_MODEL ** -0.5),
        "ffn1_w_up": nrm(ks[3], (DEPTH, D_MODEL, D_FF), D_MODEL ** -0.5),
        "ffn1_w_down": nrm(ks[4], (DEPTH, D_FF, D_MODEL), D_FF ** -0.5),
        "mix_norm_w": gain(ks[5], (DEPTH, D_MODEL)),
        "w_in": nrm(ks[6], (DEPTH, D_MODEL, IN_WIDTH), D_MODEL ** -0.5),
        "conv_w": nrm(ks[7], (DEPTH, SSD_CONV, CONV_CH), SSD_CONV ** -0.5),
        "conv_b": nrm(ks[8], (DEPTH, CONV_CH), 0.02),
        "dt_bias": dt_bias,
        "a_log": jnp.log(jax.random.uniform(ks[10], (DEPTH, SSD_HEADS), f32, 1.0, 16.0)),
        "d_skip": gain(ks[11], (DEPTH, SSD_HEADS)),
        "ssd_norm_w": gain(ks[12], (DEPTH, SSD_WIDTH)),
        "q_norm_w": gain(ks[13], (DEPTH, ATT_QK_DIM)),
        "k_norm_w": gain(ks[14], (DEPTH, ATT_QK_DIM)),
        "lambda_q1": nrm(ks[15], (DEPTH, ATT_QK_DIM), 0.1),
        "lambda_k1": nrm(ks[16], (DEPTH, ATT_QK_DIM), 0.1),
        "lambda_q2": nrm(ks[17], (DEPTH, ATT_QK_DIM), 0.1),
        "lambda_k2": nrm(ks[18], (DEPTH, ATT_QK_DIM), 0.1),
        "attn_subln_w": gain(ks[19], (DEPTH, ATT_V_DIM)),
        "w_out": nrm(ks[20], (DEPTH, MIX_WIDTH, D_MODEL), MIX_WIDTH ** -0.5),
        "ffn2_norm_w": gain(ks[21], (DEPTH, D_MODEL)),
        "ffn2_w_gate": nrm(ks[22], (DEPTH, D_MODEL, D_FF), D_MODEL ** -0.5),
        "ffn2_w_up": nrm(ks[23], (DEPTH, D_MODEL, D_FF), D_MODEL ** -0.5),
        "ffn2_w_down": nrm(jax.random.fold_in(key, 99), (DEPTH, D_FF, D_MODEL), D_FF ** -0.5),
    }


def reference(x, ffn1_norm_w, ffn1_w_gate, ffn1_w_up, ffn1_w_down, mix_norm_w, w_in, conv_w,
              conv_b, dt_bias, a_log, d_skip, ssd_norm_w, q_norm_w, k_norm_w, lambda_q1,
              lambda_k1, lambda_q2, lambda_k2, attn_subln_w, w_out, ffn2_norm_w, ffn2_w_gate,
              ffn2_w_up, ffn2_w_down):
    bsz, s, _ = x.shape
    f32 = jnp.float32
    split_idx = [int(v) for v in np.cumsum(IN_SIZES)[:-1]]
    for l in range(DEPTH):
        lambda_init = 0.8 - 0.6 * math.exp(-0.3 * l)
        x = x + 0.5 * swiglu(rms_norm(x, ffn1_norm_w[l]), ffn1_w_gate[l], ffn1_w_up[l], ffn1_w_down[l])
        h = rms_norm(x, mix_norm_w[l])
        proj = h @ w_in[l]
        z, xbc, dt_raw, q, k, v = jnp.split(proj, split_idx, axis=-1)
        xbc = jax.nn.silu(causal_dwconv(xbc, conv_w[l], conv_b[l]))
        xs, bm, cm = jnp.split(xbc, [SSD_WIDTH, SSD_WIDTH + SSD_GROUPS * SSD_STATE], axis=-1)
        dt = jax.nn.softplus(dt_raw.astype(f32) + dt_bias[l].astype(f32))
        a = -jnp.exp(a_log[l].astype(f32))
        y_ssd = ssd_chunked(
            xs.astype(f32).reshape(bsz, s, SSD_HEADS, SSD_HEAD_DIM), dt, a,
            bm.astype(f32).reshape(bsz, s, SSD_GROUPS, SSD_STATE),
            cm.astype(f32).reshape(bsz, s, SSD_GROUPS, SSD_STATE), d_skip[l])
        gated = (y_ssd * jax.nn.silu(z.astype(f32))).reshape(bsz, s, SSD_GROUPS, SSD_WIDTH // SSD_GROUPS)
        y_ssd = rms_norm(gated, ssd_norm_w[l].reshape(SSD_GROUPS, -1)).reshape(bsz, s, SSD_WIDTH)
        q = rms_norm(q.reshape(bsz, s, ATT_HEADS, 2, ATT_QK_DIM), q_norm_w[l])
        k = rms_norm(k.reshape(bsz, s, ATT_HEADS, 2, ATT_QK_DIM), k_norm_w[l])
        v = v.reshape(bsz, s, ATT_HEADS, ATT_V_DIM)
        lam = (jnp.exp(jnp.sum(lambda_q1[l].astype(f32) * lambda_k1[l].astype(f32)))
               - jnp.exp(jnp.sum(lambda_q2[l].astype(f32) * lambda_k2[l].astype(f32)))
               + lambda_init)
        o = diff_attention(q, k, v, lam)
        o = rms_norm(o, attn_subln_w[l]) * (1.0 - lambda_init)
        mixed = jnp.concatenate([y_ssd.astype(x.dtype), o.reshape(bsz, s, ATT_WIDTH).astype(x.dtype)], axis=-1)
        x = x + (mixed @ w_out[l]).astype(x.dtype)
        x = x + 0.5 * swiglu(rms_norm(x, ffn2_norm_w[l]), ffn2_w_gate[l], ffn2_w_up[l], ffn2_w_down[l])
    return x
```

```python
import numpy as np
from contextlib import ExitStack
import concourse.bass as bass
import concourse.mybir as mybir
from concourse.bass_utils import run_bass_kernel_spmd

F32 = mybir.dt.float32
BF16 = mybir.dt.bfloat16
ALU = mybir.AluOpType
AF = mybir.ActivationFunctionType
AX = mybir.AxisListType

D = 1024
FF = 2816
NF = FF // 128
T = 2048
S = 8192
EPS = 1e-6
WIN_C = 1540
GROUPS = [[0, 1, 2, 3], [4, 5, 6, 7]]
NEG = -30000.0


class Buf:
    __slots__ = ("name", "w", "rc", "rd")

    def __init__(self, name):
        self.name = name
        self.w = None
        self.rc = {}
        self.rd = []


class Sched:
    CE = ("pe", "act", "dve", "pool")

    def __init__(self, nc, sems, dma_sems):
        self.nc = nc
        self.sem = sems
        self.dma_sems = dma_sems
        self.dma_n = {k: 0 for k in dma_sems}
        self.ops = {e: [] for e in ("pe", "act", "dve", "pool", "sp")}
        self.seq = {e: 0 for e in self.ops}
        self.flushed = {e: 0 for e in self.ops}
        self.inc_total = {e: 0 for e in self.CE}
        self.phase_end_rank = {e: 0 for e in self.CE}
        self.waited = {e: {} for e in self.ops}

    def _deps_and_mark(self, ev, reads, writes):
        deps = []
        for b in reads:
            if b.w is not None:
                deps.append(b.w)
        for b in writes:
            if b.w is not None:
                deps.append(b.w)
            for e, s in b.rc.items():
                deps.append(("c", e, s))
            deps.extend(b.rd)
        for b in reads:
            if ev[0] == "c":
                if b.rc.get(ev[1], 0) < ev[2]:
                    b.rc[ev[1]] = ev[2]
            else:
                b.rd.append(ev)
        for b in writes:
            b.w = ev
            b.rc = {}
            b.rd = []
        return deps

    def op(self, eng, fn, reads=(), writes=()):
        self.seq[eng] += 1
        ev = ("c", eng, self.seq[eng])
        deps = self._deps_and_mark(ev, reads, writes)
        self.ops[eng].append(dict(seq=self.seq[eng], fn=fn, deps=deps, kind="c"))
        return ev

    def dma(self, eng, fn, reads=(), writes=(), inc=16, grp=None):
        grp = grp or eng
        sems = self.dma_sems[grp]
        n = self.dma_n[grp]
        self.dma_n[grp] += 1
        k = n % len(sems)
        val = inc * (n // len(sems) + 1)
        ev = ("d", (grp, k), val)
        self.seq[eng] += 1
        deps = self._deps_and_mark(ev, reads, writes)
        if val > inc:
            deps.append(("d", (grp, k), val - inc))
        self.ops[eng].append(dict(seq=self.seq[eng], fn=fn, deps=deps, kind="d", sem=sems[k], inc=inc))
        return ev

    def wait_event(self, eng, ev):
        self.seq[eng] += 1
        self.ops[eng].append(dict(seq=self.seq[eng], fn=None, deps=[ev], kind="w"))

    def flush(self, block):
        need_inc = {e: set() for e in self.CE}
        for e, lst in self.ops.items():
            for o in lst:
                nd = []
                for d in o["deps"]:
                    if d[0] == "c":
                        e2, s2 = d[1], d[2]
                        if e2 == e:
                            if e == "pe" or o["seq"] - s2 > (10 if e == "pool" else 3):
                                continue
                        if s2 > self.flushed[e2]:
                            need_inc[e2].add(s2)
                    nd.append(d)
                o["deps"] = nd
        rank = {}
        for e in self.CE:
            lst = self.ops[e]
            lc = [o["seq"] for o in lst if o["kind"] == "c"]
            if lc:
                need_inc[e].add(lc[-1])
            r = self.inc_total[e]
            for o in lst:
                if o["kind"] == "c" and o["seq"] in need_inc[e]:
                    r += 1
                    rank[(e, o["seq"])] = r
            self.inc_total[e] = r
        new_phase_end = {e: self.inc_total[e] for e in self.CE}

        sched = self

        def emit(e, handle):
            waited = sched.waited[e]
            for o in sched.ops[e]:
                for d in o["deps"]:
                    if d[0] == "c":
                        e2, s2 = d[1], d[2]
                        if s2 <= sched.flushed[e2]:
                            val = sched.phase_end_rank[e2]
                        else:
                            val = rank[(e2, s2)]
                        key = ("c", e2)
                        semh = sched.sem[e2]
                    else:
                        key = ("d",) + d[1]
                        semh = sched.dma_sems[d[1][0]][d[1][1]]
                        val = d[2]
                    if waited.get(key, 0) < val:
                        handle.wait_ge(semh, val)
                        waited[key] = val
                if o["fn"] is None:
                    continue
                inst = o["fn"](handle)
                if o["kind"] == "d":
                    inst.then_inc(o["sem"], o["inc"])
                elif (e, o["seq"]) in rank:
                    inst.then_inc(sched.sem[e], 1)

        if self.ops["pe"]:
            block.tensor(lambda h: emit("pe", h))
        if self.ops["act"]:
            block.scalar(lambda h: emit("act", h))
        if self.ops["dve"]:
            block.vector(lambda h: emit("dve", h))
        if self.ops["pool"]:
            block.gpsimd(lambda h: emit("pool", h))
        if self.ops["sp"]:
            block.sync(lambda h: emit("sp", h))
        for e in self.ops:
            self.flushed[e] = self.seq[e]
            self.ops[e] = []
        self.phase_end_rank = new_phase_end


def build_program(debug=False, stop_after=None):
    nc = bass.Bass("TRN2", target_bir_lowering=False)

    def din(name, shape, dt=F32):
        return nc.dram_tensor(name, list(shape), dt, kind="ExternalInput").ap()

    xT = din("xT", [D, T])
    w1g = din("w1g", [D, FF]); w1u = din("w1u", [D, FF]); w1d = din("w1d", [FF, D])
    w2g = din("w2g", [D, FF]); w2u = din("w2u", [D, FF]); w2d = din("w2d", [FF, D])
    win = din("win", [D, WIN_C])
    wout = din("wout", [2 * D, D])
    n1w = din("n1w", [128, 8]); nmw = din("nmw", [128, 8]); n2w = din("n2w", [128, 8])
    ssdw = din("ssdw", [128, 8])
    convw = din("convw", [128, 16]); convb = din("convb", [128, 4])
    dtb = din("dtb", [128, 4]); alog = din("alog", [128, 4]); dsk = din("dsk", [128, 2])
    qkw = din("qkw", [128, 2]); subw = din("subw", [128, 1])
    lamv = din("lamv", [128, 4 * 64])
    cst = din("cst", [128, 9 * 128])
    maskadd = din("maskadd", [128, 4 * 512])
    outT = nc.dram_tensor("outT", [D, T], F32, kind="ExternalOutput").ap()

    hloc = nc.dram_tensor("hloc", [4, D, 512], BF16)
    hfull = nc.dram_tensor("hfull", [4, 4 * D, 512], BF16)
    mloc = nc.dram_tensor("mloc", [8, 1024, 512], BF16)
    mfull = nc.dram_tensor("mfull", [8, 4 * 1024, 512], BF16)
    xsave = nc.dram_tensor("xsave", [D, T], F32)
    win_bf = nc.dram_tensor("win_bf", [128, 8, WIN_C], BF16)
    wout_bf = nc.dram_tensor("wout_bf", [128, 16, D], BF16)
    mk_bf = nc.dram_tensor("mk_bf", [128, 4 * 512], BF16)
    dbg = {}
    if debug:
        dbg["x1"] = nc.dram_tensor("dbg_x1", [D, T], F32, kind="ExternalOutput").ap()
        dbg["h"] = nc.dram_tensor("dbg_h", [4, D, 512], BF16, kind="ExternalOutput").ap()
        dbg["m"] = nc.dram_tensor("dbg_m", [8, 1024, 512], BF16, kind="ExternalOutput").ap()

    es = ExitStack()
    with es:
        sems = {e: es.enter_context(nc.semaphore("c_" + e)) for e in Sched.CE}
        dma_sems = {
            "sp": [es.enter_context(nc.semaphore(f"dsp{i}")) for i in range(8)],
            "pool": [es.enter_context(nc.semaphore(f"dpl{i}")) for i in range(8)],
        }
        dma_sems["cc1"] = [es.enter_context(nc.semaphore(f"cc1_{i}")) for i in range(2)]
        dma_sems["cc2"] = [es.enter_context(nc.semaphore(f"cc2_{i}")) for i in range(2)]
        K = Sched(nc, sems, dma_sems)
        tap_evs = []

        def tap(name, ap, shape, dt, reads):
            if not debug:
                return
            t = nc.dram_tensor("tap_" + name, list(shape), dt, kind="ExternalOutput").ap()
            tap_evs.append(K.dma("sp", lambda h: h.dma_start(out=t, in_=ap), reads=reads, writes=[Buf("tap")]))

        ps = es.enter_context(nc.psum_tensor("ps", [128, 8 * 512], F32))
        banks = [Buf(f"bank{i}") for i in range(8)]

        def bank_ap(i, n=512):
            return ps[:, i * 512:i * 512 + n]

        bank_rr = [0]

        def next_bank(pool=(0, 1, 2, 3, 4, 5, 6, 7)):
            i = pool[bank_rr[0] % len(pool)]
            bank_rr[0] += 1
            return i

        def mm(out_ap, lhsT, rhs, start, stop, reads, bank, tp=None):
            if tp is None:
                K.op("pe", lambda h, o=out_ap, l=lhsT, r=rhs, s=start, p=stop: h.matmul(o, lhsT=l, rhs=r, start=s, stop=p),
                     reads=reads, writes=[banks[bank]])
            else:
                K.op("pe", lambda h, o=out_ap, l=lhsT, r=rhs, s=start, p=stop, tp=tp: h.matmul(
                    o, lhsT=l, rhs=r, start=s, stop=p, tile_position=tp), reads=reads, writes=[banks[bank]])

        def rstd_from_ss(bank, n_feat, rstd_ap, rstd_buf, tmp_ap, tmp_buf):
            K.op("act", lambda h: h.activation(out=tmp_ap, in_=bank_ap(bank), func=AF.Ln, bias=eps_col[:, 0:1], scale=1.0 / n_feat),
                 reads=[banks[bank], b_const], writes=[tmp_buf])
            K.op("act", lambda h: h.activation(out=rstd_ap, in_=tmp_ap, func=AF.Exp, scale=-0.5),
                 reads=[tmp_buf], writes=[rstd_buf])

        def rmsnorm_tokens(xres, xres_b, wcol, wcol_b, dst_fn, dst_bufs, t0, scr):
            sq, sq_b, rstd, rstd_b, tmp, tmp_b = scr
            tt = t0 // 512
            K.op("act", lambda h: h.activation(out=sq[:], in_=xres[:, :, t0:t0 + 512], func=AF.Square),
                 reads=[xres_b[kt][tt] for kt in range(8)], writes=[sq_b])
            bk = next_bank()
            for kt in range(8):
                mm(bank_ap(bk), ones_bf[:], sq[:, kt, :], kt == 0, kt == 7, [sq_b, b_const], bk)
            rstd_from_ss(bk, D, rstd[:], rstd_b, tmp[:], tmp_b)
            for kt in range(8):
                K.op("dve", lambda h, kt=kt: h.scalar_tensor_tensor(
                    out=dst_fn(kt), in0=xres[:, kt, t0:t0 + 512], scalar=wcol[:, kt:kt + 1], in1=rstd[:],
                    op0=ALU.mult, op1=ALU.mult),
                    reads=[xres_b[kt][tt], rstd_b, wcol_b], writes=dst_bufs)

        def ffn(xres, xres_b, wcol, wcol_b, wg, wu, wd, bufs, after_st=None, mid_st=None):
            (hT, hT_b, actT, actT_b, wgu, wgu_b, wdb, wdb_b, sg, sg_b, scr) = bufs
            for st in range(2):
                for tl in range(2):
                    rmsnorm_tokens(xres, xres_b, wcol, wcol_b,
                                   lambda kt, tl=tl: hT[:, kt, tl * 512:(tl + 1) * 512], [hT_b[tl]],
                                   st * 1024 + tl * 512, scr)

                def load_gu(blk):
                    slot = blk % 2
                    for j, w in enumerate((wg, wu)):
                        src = w[:, blk * 256:(blk + 1) * 256].rearrange("(kt p) c -> p kt c", p=128)
                        K.dma("pool", lambda h, j=j, slot=slot, src=src: h.dma_start(out=wgu[:, slot, j, :, :], in_=src),
                              writes=[wgu_b[slot][j]])

                def load_d(q):
                    slot = q % 2
                    src = wd[:, q * 256:(q + 1) * 256].rearrange("(f p) c -> p f c", p=128)
                    for hf in range(2):
                        K.dma("pool", lambda h, slot=slot, src=src, hf=hf: h.dma_start(
                            out=wdb[:, slot, hf * 11:(hf + 1) * 11, :], in_=src[:, hf * 11:(hf + 1) * 11, :]),
                            writes=[wdb_b[slot][hf]])

                load_gu(0)
                for blk in range(11):
                    if blk + 1 < 11:
                        load_gu(blk + 1)
                    elif True:
                        load_d(0)
                    if blk == 2 and mid_st is not None:
                        mid_st(st)
                    slot = blk % 2
                    for fl in range(2):
                        f = 2 * blk + fl
                        for tl in range(2):
                            bg = next_bank()
                            bu = next_bank()
                            for j, bk in ((0, bg), (1, bu)):
                                for kt in range(8):
                                    mm(bank_ap(bk), wgu[:, slot, j, kt, fl * 128:(fl + 1) * 128],
                                       hT[:, kt, tl * 512:(tl + 1) * 512], kt == 0, kt == 7,
                                       [wgu_b[slot][j], hT_b[tl]], bk)
                            si = (f * 2 + tl) % 2
                            K.op("act", lambda h, bg=bg, si=si: h.activation(out=sg[:, si, :], in_=bank_ap(bg), func=AF.Silu),
                                 reads=[banks[bg]], writes=[sg_b[si]])
                            K.op("dve", lambda h, bu=bu, si=si, f=f, tl=tl: h.tensor_tensor(
                                out=actT[:, f, tl * 512:(tl + 1) * 512], in0=sg[:, si, :], in1=bank_ap(bu), op=ALU.mult),
                                reads=[sg_b[si], banks[bu]], writes=[actT_b[f][tl]])
                for q in range(4):
                    if q + 1 < 4:
                        load_d(q + 1)
                    slot = q % 2
                    for dl in range(2):
                        d = 2 * q + dl
                        for tl in range(2):
                            bk = next_bank()
                            for f in range(NF):
                                mm(bank_ap(bk), wdb[:, slot, f, dl * 128:(dl + 1) * 128],
                                   actT[:, f, tl * 512:(tl + 1) * 512], f == 0, f == NF - 1,
                                   [wdb_b[slot][f // 11], actT_b[f][tl]], bk)
                            t0 = st * 1024 + tl * 512
                            tt = t0 // 512
                            K.op("dve", lambda h, bk=bk, d=d, t0=t0: h.scalar_tensor_tensor(
                                out=xres[:, d, t0:t0 + 512], in0=bank_ap(bk), scalar=0.5, in1=xres[:, d, t0:t0 + 512],
                                op0=ALU.mult, op1=ALU.add),
                                reads=[banks[bk], xres_b[d][tt]], writes=[xres_b[d][tt]])
                if after_st is not None:
                    after_st(st)

        def alloc_ffn_bufs(st_, tg):
            hT = st_.enter_context(nc.sbuf_tensor("hT" + tg, [128, 8, 1024], BF16))
            actT = st_.enter_context(nc.sbuf_tensor("actT" + tg, [128, NF, 1024], BF16))
            wgu = st_.enter_context(nc.sbuf_tensor("wgu" + tg, [128, 2, 2, 8, 256], BF16))
            wdb = st_.enter_context(nc.sbuf_tensor("wdb" + tg, [128, 2, NF, 256], BF16))
            sg = st_.enter_context(nc.sbuf_tensor("sg" + tg, [128, 2, 512], F32))
            sq = st_.enter_context(nc.sbuf_tensor("sq" + tg, [128, 8, 512], BF16))
            rstd = st_.enter_context(nc.sbuf_tensor("rstd" + tg, [128, 512], F32))
            tmp = st_.enter_context(nc.sbuf_tensor("ntmp" + tg, [128, 512], F32))
            return (hT, [Buf("hT0"), Buf("hT1")], actT, [[Buf("a"), Buf("a")] for _ in range(NF)],
                    wgu, [[Buf("w"), Buf("w")] for _ in range(2)], wdb, [[Buf("w"), Buf("w")] for _ in range(2)],
                    sg, [Buf("sg0"), Buf("sg1")], (sq, Buf("sq"), rstd, Buf("rstd"), tmp, Buf("tmp")))

        ones_bf = es.enter_context(nc.sbuf_tensor("ones_bf", [128, 128], BF16))
        ident_bf = es.enter_context(nc.sbuf_tensor("ident_bf", [128, 128], BF16))
        bones_bf = es.enter_context(nc.sbuf_tensor("bones_bf", [128, 128], BF16))
        cst_f = es.enter_context(nc.sbuf_tensor("cst_f", [128, 9, 128], F32))
        onesAB_bf = es.enter_context(nc.sbuf_tensor("onesAB_bf", [128, 2, 128], BF16))
        eps_col = es.enter_context(nc.sbuf_tensor("eps_col", [128, 1], F32))
        ncols = es.enter_context(nc.sbuf_tensor("ncols", [128, 4, 8], F32))
        b_const = Buf("const")
        b_ncols = Buf("ncols")

        with ExitStack() as sa:
            xres = sa.enter_context(nc.sbuf_tensor("xres", [128, 8, T], F32))
            xres_b = [[Buf("x") for _ in range(4)] for _ in range(8)]
            fb = alloc_ffn_bufs(sa, "A")
            actT = fb[2]
            block = sa.enter_context(nc.Block())
            K.dma("sp", lambda h: h.dma_start(out=cst_f[:], in_=cst.rearrange("p (a b) -> p a b", a=9)), writes=[b_const])
            for j, src in enumerate((n1w, nmw, n2w, ssdw)):
                K.dma("sp", lambda h, j=j, src=src: h.dma_start(out=ncols[:, j, :], in_=src), writes=[b_ncols])
            K.op("dve", lambda h: h.tensor_copy(out=ident_bf[:], in_=cst_f[:, 0, :]), reads=[b_const], writes=[b_const])
            K.op("dve", lambda h: h.tensor_copy(out=ones_bf[:], in_=cst_f[:, 1, :]), reads=[b_const], writes=[b_const])
            K.op("dve", lambda h: h.tensor_copy(out=bones_bf[:], in_=cst_f[:, 2, :]), reads=[b_const], writes=[b_const])
            K.op("dve", lambda h: h.tensor_copy(out=onesAB_bf[:], in_=cst_f[:, 5:7, :]), reads=[b_const], writes=[b_const])
            K.op("dve", lambda h: h.memset(eps_col[:], EPS), writes=[b_const])
            for hf in range(2):
                for kt in range(8):
                    K.dma("sp", lambda h, kt=kt, hf=hf: h.dma_start(
                        out=xres[:, kt, hf * 1024:(hf + 1) * 1024], in_=xT[kt * 128:(kt + 1) * 128, hf * 1024:(hf + 1) * 1024]),
                        writes=[xres_b[kt][2 * hf], xres_b[kt][2 * hf + 1]])
            hmix = sa.enter_context(nc.sbuf_tensor("hmix", [128, 8, 1024], BF16))
            hm_b = [Buf("hm") for _ in range(4)]
            b_xsave = Buf("xsave")
            b_winbf = [Buf("winbf") for _ in range(3)]
            b_woutbf = [Buf("woutbf") for _ in range(4)]
            b_mkbf = Buf("mkbf")
            b_hloc = [Buf("hloc") for _ in range(4)]
            b_hfull = [Buf("hfull") for _ in range(4)]
            pend_ag = []

            def mixnorm_st(st):
                for kt in range(8):
                    K.dma("sp", lambda h, kt=kt, st=st: h.dma_start(
                        out=xsave.ap()[kt * 128:(kt + 1) * 128, st * 1024:(st + 1) * 1024], in_=xres[:, kt, st * 1024:(st + 1) * 1024]),
                        reads=[xres_b[kt][2 * st], xres_b[kt][2 * st + 1]], writes=[b_xsave])
                for tl in range(2):
                    tt = 2 * st + tl
                    rmsnorm_tokens(xres, xres_b, ncols[:, 1, :], b_ncols,
                                   lambda kt, tl=tl: hmix[:, kt, tl * 512:(tl + 1) * 512], [hm_b[tt]],
                                   tt * 512, fb[10])
                    K.dma("sp", lambda h, tl=tl, tt=tt: h.dma_start(
                        out=hloc.ap()[tt].rearrange("(kt p) t -> p kt t", p=128),
                        in_=hmix[:, :, tl * 512:(tl + 1) * 512]),
                        reads=[hm_b[tt]], writes=[b_hloc[tt]])
                    pend_ag.append(tt)

            def issue_pending(st=None, extra=()):
                if st == 0:
                    for j, (c0, c1) in enumerate(((0, 512), (512, 1024), (1024, WIN_C))):
                        K.dma("pool", lambda h, c0=c0, c1=c1: h.dma_start(
                            out=win_bf.ap()[:, :, c0:c1], in_=win[:, c0:c1].rearrange("(kt p) c -> p kt c", p=128)),
                            writes=[b_winbf[j]])
                    K.dma("pool", lambda h: h.dma_start(out=mk_bf.ap(), in_=maskadd), writes=[b_mkbf])
                while pend_ag:
                    tt = pend_ag.pop(0)
                    K.dma("pool", lambda h, tt=tt: h.collective_compute(
                        "AllGather", ALU.bypass, replica_groups=GROUPS,
                        ins=[hloc.ap()[tt].opt()], outs=[hfull.ap()[tt].opt()]),
                        reads=[b_hloc[tt]] + list(extra), writes=[b_hfull[tt]], inc=1, grp="cc1")

            ffn(xres, xres_b, ncols[:, 0, :], b_ncols, w1g, w1u, w1d, fb, after_st=mixnorm_st, mid_st=issue_pending)
            if debug:
                evs = []
                for kt in range(8):
                    sl = slice(kt * 128, (kt + 1) * 128)
                    evs.append(K.dma("sp", lambda h, sl=sl: h.dma_start(out=dbg["x1"][sl, :], in_=xsave.ap()[sl, :]),
                                     reads=[b_xsave], writes=[Buf("d1")]))
                    if kt < 4:
                        evs.append(K.dma("sp", lambda h, kt=kt: h.dma_start(out=dbg["h"][kt], in_=hloc.ap()[kt]),
                                         reads=[b_hloc[kt]], writes=[Buf("d2")]))
                for ev in evs:
                    K.wait_event("sp", ev)
            K.flush(block)

        b_mloc = [Buf("mloc") for _ in range(8)]
        b_mfull = [Buf("mfull") for _ in range(8)]
        if stop_after in ("A", "A1", "A2"):
            return nc
        with ExitStack() as sb:
            def sbt(name, shape, dt):
                return sb.enter_context(nc.sbuf_tensor(name, shape, dt))
            KT = sbt("KT", [128, 2, S], BF16)
            Vt = sbt("Vt", [128, 2, 64, 128], BF16)
            KT_b = [[Buf("k") for _ in range(16)] for _ in range(2)]
            V_b = [Buf("v") for _ in range(16)]
            winb = sbt("winb", [128, 8, WIN_C], BF16)
            b_win = [Buf("win") for _ in range(6)]
            hB = sbt("hB", [128, 2, 8, 512], BF16)
            hB_b = [Buf("hB0"), Buf("hB1")]
            mk = sbt("mk", [128, 4, 512], BF16)
            prm = sbt("prm", [128, 32], F32)
            prm2 = sbt("prm2", [128, 8], F32)
            lam_t = sbt("lam_t", [128, 4, 64], F32)
            lam_s = sbt("lam_s", [128, 4], F32)
            b_prm = Buf("prm")
            qk_raw = sbt("qk_raw", [128, 2, 512], F32)
            qk_raw_b = [Buf("qkr") for _ in range(2)]
            sqb2 = sbt("sqb2", [128, 512], BF16); sqb2_b = Buf("sqb2")
            rstd2 = sbt("rstdB2", [128, 512], F32); rstd2_b = Buf("rstdB2")
            ntmp2 = sbt("ntmpB2", [128, 512], F32); ntmp2_b = Buf("ntmpB2")
            sqb = sbt("sqb", [128, 512], BF16); sqb_b = Buf("sqb")
            rstd = sbt("rstdB", [128, 512], F32); rstd_b = Buf("rstdB")
            ntmp = sbt("ntmpB", [128, 512], F32); ntmp_b = Buf("ntmpB")
            qn = sbt("qn", [128, 2, 2, 512], BF16); qn_b = [[Buf("qn0"), Buf("qn1")] for _ in range(2)]
            zs = sbt("zs", [128, 2, 512], F32); zs_b = [Buf("zs0"), Buf("zs1")]
            xbc = sbt("xbc", [128, 4, 515], F32); xbc_b = [Buf("xbc") for _ in range(4)]
            cacc = sbt("cacc", [128, 4, 512], F32); cacc_b = [Buf("cacc") for _ in range(4)]
            xsT = sbt("xsT", [128, 4, 512], BF16); xsT_b = [Buf("xsT") for _ in range(4)]
            dtpre2 = sbt("dtpre", [128, 2, 4, 4], F32); dt_b2 = [Buf("dt0"), Buf("dt1")]
            dtv2 = sbt("dtv", [128, 2, 4, 4], F32)
            dtA2 = sbt("dtA", [128, 2, 4, 4], F32)
            xBtok = sbt("xBtok", [128, 2, 384], BF16); xBtok_b = [Buf("xBt0"), Buf("xBt1")]
            Rall = sbt("Rall", [128, 4, 128], F32); Rall_b = Buf("Rall")
            LT = sbt("LT", [128, 4, 128], F32); LT_b = Buf("LT")
            Eall = sbt("Eall", [128, 4, 128], F32); Eall_b = Buf("Eall")
            CBm = sbt("CBm", [128, 128], F32); CBm_b = Buf("CBm")
            MT = sbt("MT", [128, 4, 128], BF16); MT_b = Buf("MT")
            CeT = sbt("CeT", [128, 4, 128], BF16); CeT_b = Buf("CeT")
            wst = sbt("wst", [128, 4], F32); wst_b = Buf("wst")
            xdtp = sbt("xdtp", [128, 4, 128], BF16); xdtp_b = Buf("xdtp")
            xdtd = sbt("xdtd", [128, 4, 64], BF16); xdtd_b = Buf("xdtd")
            hst = sbt("hst", [128, 4, 64], F32); hst_b = Buf("hst")
            hstt = sbt("hstt", [128, 4, 64], F32); hstt_b = Buf("hstt")
            hstp = sbt("hstp", [128, 4, 128], BF16); hstp_b = Buf("hstp")
            yv = sbt("yv", [128, 2, 512], F32); yv_b = [Buf("yv0"), Buf("yv1")]
            Pb = sbt("Pb", [128, 2, 4, 512], BF16); P_b = [[Buf("P") for _ in range(4)] for _ in range(2)]
            rec = sbt("rec", [128, 2, 512], F32); rec_b = [Buf("rec0"), Buf("rec1")]
            recp = sbt("recp", [128, 512], F32); recp_b = Buf("recp")
            o12 = sbt("o12", [128, 2, 512], F32); o12_b = [Buf("o1"), Buf("o2")]
            of = sbt("of", [128, 512], F32); of_b = Buf("of")
            mixT = sbt("mixT", [128, 2, 4, 512], BF16); mixT_b = [[Buf("mx") for _ in range(4)] for _ in range(2)]
            block = sb.enter_context(nc.Block())

            identf = cst_f[:, 0, :]
            onesf = cst_f[:, 1, :]
            triLE = cst_f[:, 3, :]
            SUf = cst_f[:, 4, :]

            for hk in range(2):
                K.dma("sp", lambda h, hk=hk: h.dma_start(out=winb[:, hk * 4:(hk + 1) * 4, :], in_=win_bf.ap()[:, hk * 4:(hk + 1) * 4, :]),
                      reads=b_winbf, writes=b_win)
            K.dma("sp", lambda h: h.dma_start(out=mk[:], in_=mk_bf.ap().rearrange("p (a b) -> p a b", a=4)),
                  reads=[b_mkbf], writes=[b_const])
            for src, c0, n in ((convw, 0, 16), (convb, 16, 4), (dtb, 20, 4), (alog, 24, 4), (dsk, 28, 2), (qkw, 30, 2)):
                K.dma("sp", lambda h, src=src, c0=c0, n=n: h.dma_start(out=prm[:, c0:c0 + n], in_=src), writes=[b_prm])
            K.dma("sp", lambda h: h.dma_start(out=prm2[:, 0:1], in_=subw), writes=[b_prm])
            K.dma("sp", lambda h: h.dma_start(out=lam_t[:], in_=lamv.rearrange("p (a b) -> p a b", a=4)), writes=[b_prm])
            convw_s = prm[:, 0:16]; convb_s = prm[:, 16:20]; dtb_s = prm[:, 20:24]; alog_s = prm[:, 24:28]
            dsk_s = prm[:, 28:30]; qkw_s = prm[:, 30:32]
            A_row = prm2[:, 1:5]; lamneg = prm2[:, 5:6]; sub08 = prm2[:, 6:7]
            K.op("act", lambda h: h.activation(out=A_row, in_=alog_s, func=AF.Exp), reads=[b_prm], writes=[b_prm])
            K.op("dve", lambda h: h.tensor_scalar(out=A_row, in0=A_row, scalar1=-1.0, scalar2=None, op0=ALU.mult),
                 reads=[b_prm], writes=[b_prm])
            K.op("dve", lambda h: h.tensor_tensor(out=lam_t[:, 0, :], in0=lam_t[:, 0, :], in1=lam_t[:, 1, :], op=ALU.mult),
                 reads=[b_prm], writes=[b_prm])
            K.op("dve", lambda h: h.tensor_tensor(out=lam_t[:, 2, :], in0=lam_t[:, 2, :], in1=lam_t[:, 3, :], op=ALU.mult),
                 reads=[b_prm], writes=[b_prm])
            K.op("dve", lambda h: h.reduce_sum(out=lam_s[:, 0:1], in_=lam_t[:, 0, :], axis=AX.X), reads=[b_prm], writes=[b_prm])
            K.op("dve", lambda h: h.reduce_sum(out=lam_s[:, 1:2], in_=lam_t[:, 2, :], axis=AX.X), reads=[b_prm], writes=[b_prm])
            K.op("act", lambda h: h.activation(out=lam_s[:, 2:4], in_=lam_s[:, 0:2], func=AF.Exp), reads=[b_prm], writes=[b_prm])
            K.op("dve", lambda h: h.tensor_tensor(out=lamneg, in0=lam_s[:, 3:4], in1=lam_s[:, 2:3], op=ALU.subtract),
                 reads=[b_prm], writes=[b_prm])
            K.op("dve", lambda h: h.tensor_scalar(out=lamneg, in0=lamneg, scalar1=-0.2, scalar2=None, op0=ALU.add),
                 reads=[b_prm], writes=[b_prm])
            K.op("dve", lambda h: h.tensor_scalar(out=sub08, in0=prm2[:, 0:1], scalar1=0.8, scalar2=None, op0=ALU.mult),
                 reads=[b_prm], writes=[b_prm])
            K.op("pool", lambda h: h.memset(hst[:], 0.0), writes=[hst_b])
            K.op("pool", lambda h: h.memset(hstp[:], 0.0), writes=[hstp_b])
            K.op("pool", lambda h: h.memset(xdtp[:], 0.0), writes=[xdtp_b])
            K.op("pool", lambda h: h.memset(xbc[:], 0.0), writes=xbc_b)

            O1, O2, LB = 3, 4, 5
            GP = (6, 7)
            GPA = (0, 1, 2)
            s_rr = [0]

            def load_h(i):
                slot = i % 2
                src = hfull.ap()[i % 4][(i // 4) * D:(i // 4 + 1) * D, :].rearrange("(kt p) t -> p kt t", p=128)
                K.dma("sp", lambda h, slot=slot, src=src: h.dma_start(out=hB[:, slot, :, :], in_=src),
                      reads=[b_hfull[i % 4]], writes=[hB_b[slot]])

            bankA = [0]

            def gp_bank():
                return next_bank(GP)

            def proj_tile(i, ct):
                slot = i % 2
                bk = gp_bank()
                for kt in range(8):
                    mm(bank_ap(bk), winb[:, kt, ct * 128:(ct + 1) * 128], hB[:, slot, kt, :], kt == 0, kt == 7,
                       [b_win[ct // 2], hB_b[slot]], bk)
                return bk

            def st_qk(i, ct):
                def f():
                    t0 = i * 512
                    qs = i % 2
                    bk = proj_tile(i, ct)
                    rs = ct % 2
                    K.op("dve", lambda h: h.tensor_copy(out=qk_raw[:, rs, :], in_=bank_ap(bk)),
                         reads=[banks[bk]], writes=[qk_raw_b[rs]])
                    K.op("pool", lambda h: h.tensor_tensor(out=sqb[:], in0=qk_raw[:, rs, :], in1=qk_raw[:, rs, :], op=ALU.mult),
                         reads=[qk_raw_b[rs]], writes=[sqb_b])
                    bk2 = gp_bank()
                    mm(bank_ap(bk2), bones_bf[:], sqb[:], True, True, [sqb_b, b_const], bk2)
                    rstd_from_ss(bk2, 64, rstd[:], rstd_b, ntmp[:], ntmp_b)
                    if ct < 2:
                        dst, dstb, wc = qn[:, qs, ct, :], qn_b[qs][ct], qkw_s[:, 0:1]
                    else:
                        dst, dstb, wc = KT[:, ct - 2, t0:t0 + 512], KT_b[ct - 2][i], qkw_s[:, 1:2]
                    K.op("dve", lambda h: h.scalar_tensor_tensor(
                        out=dst, in0=qk_raw[:, rs, :], scalar=wc, in1=rstd[:], op0=ALU.mult, op1=ALU.mult),
                        reads=[qk_raw_b[rs], rstd_b, b_prm], writes=[dstb])
                return f

            def st_z(i, pz):
                def f():
                    bk = proj_tile(i, 4 + pz)
                    K.op("dve", lambda h: h.tensor_copy(out=zs[:, pz, :], in_=bank_ap(bk)),
                         reads=[banks[bk]], writes=[zs_b[pz]])
                return f

            def st_xbc(i, c):
                def f():
                    bk = proj_tile(i, 6 + c)
                    K.op("dve", lambda h: h.tensor_copy(out=xbc[:, c, 3:515], in_=bank_ap(bk)),
                         reads=[banks[bk]], writes=[xbc_b[c]])
                return f

            def st_v(i, c):
                def f():
                    slot = i % 2
                    dtpre = dtpre2[:, i % 2]
                    dt_b = dt_b2[i % 2]
                    bk = gp_bank()
                    for kt in range(8):
                        mm(bank_ap(bk, 260), hB[:, slot, kt, c * 128:(c + 1) * 128], winb[:, kt, 1280:1540], kt == 0, kt == 7,
                           [b_win[5], hB_b[slot]], bk)
                    for hv in range(2):
                        K.op("dve", lambda h, hv=hv: h.tensor_copy(
                            out=Vt[:, hv, 4 * i + c, :], in_=ps[:, bk * 512 + hv * 128:bk * 512 + (hv + 1) * 128]),
                            reads=[banks[bk]], writes=[V_b[i]])
                    K.op("dve", lambda h: h.tensor_tensor(
                        out=dtpre[:, c, :], in0=ps[:, bk * 512 + 256:bk * 512 + 260], in1=dtb_s, op=ALU.add),
                        reads=[banks[bk], b_prm], writes=[dt_b])
                return f

            def st_dt(i):
                def f():
                    dtpre = dtpre2[:, i % 2]; dtv = dtv2[:, i % 2]; dtA = dtA2[:, i % 2]
                    dt_b = dt_b2[i % 2]
                    K.op("act", lambda h: h.activation(out=dtv[:], in_=dtpre[:], func=AF.Exp), reads=[dt_b], writes=[dt_b])
                    K.op("act", lambda h: h.activation(out=dtv[:], in_=dtv[:], func=AF.Ln, bias=1.0), reads=[dt_b], writes=[dt_b])
                    K.op("dve", lambda h: h.tensor_tensor(out=dtA[:], in0=dtv[:], in1=A_row.unsqueeze(1).to_broadcast([128, 4, 4]),
                                                          op=ALU.mult), reads=[dt_b, b_prm], writes=[dt_b])
                return f

            def st_conv(i, c):
                def f():
                    K.op("dve", lambda h: h.tensor_scalar(out=cacc[:, c, :], in0=xbc[:, c, 0:512],
                                                          scalar1=convw_s[:, c * 4:c * 4 + 1], scalar2=None, op0=ALU.mult),
                         reads=[xbc_b[c], b_prm], writes=[cacc_b[c]])
                    for k in range(1, 4):
                        K.op("dve", lambda h, k=k: h.scalar_tensor_tensor(
                            out=cacc[:, c, :], in0=xbc[:, c, k:k + 512], scalar=convw_s[:, c * 4 + k:c * 4 + k + 1],
                            in1=cacc[:, c, :], op0=ALU.mult, op1=ALU.add),
                            reads=[xbc_b[c], b_prm, cacc_b[c]], writes=[cacc_b[c]])
                    K.op("dve", lambda h: h.tensor_copy(out=xbc[:, c, 0:3], in_=xbc[:, c, 512:515]),
                         reads=[xbc_b[c]], writes=[xbc_b[c]])
                return f

            def st_silu(i):
                def f():
                    for c in range(4):
                        K.op("act", lambda h, c=c: h.activation(out=xsT[:, c, :], in_=cacc[:, c, :], func=AF.Silu,
                                                                bias=convb_s[:, c:c + 1]),
                             reads=[cacc_b[c], b_prm], writes=[xsT_b[c]])
                    for pz in range(2):
                        K.op("act", lambda h, pz=pz: h.activation(out=zs[:, pz, :], in_=zs[:, pz, :], func=AF.Silu),
                             reads=[zs_b[pz]], writes=[zs_b[pz]])
                return f

            def prelude_stages(i):
                L = [st_qk(i, ct) for ct in range(4)] + [st_v(i, c) for c in range(4)] + [st_dt(i)]
                for c in range(4):
                    L += [st_xbc(i, c), st_conv(i, c)]
                L += [st_z(i, pz) for pz in range(2)]
                L += [st_silu(i)]
                return L

            def ssd_stages(i):
                L = []
                ms = i % 2
                dtv = dtv2[:, i % 2]; dtA = dtA2[:, i % 2]
                dt_b = dt_b2[i % 2]
                for c in range(4):
                    cs = slice(c * 128, (c + 1) * 128)
                    ts_ = c % 2
                    st = {}

                    def s1(c=c, cs=cs, ts_=ts_, st=st):
                        bk = gp_bank()
                        for j in range(3):
                            mm(ps[:, bk * 512 + j * 128:bk * 512 + (j + 1) * 128], xsT[:, j, cs], ident_bf[:], True, True,
                               [xsT_b[j], b_const], bk)
                        K.op("dve", lambda h: h.tensor_copy(out=xBtok[:, ts_, :], in_=bank_ap(bk, 384)),
                             reads=[banks[bk]], writes=[xBtok_b[ts_]])
                        K.op("dve", lambda h: h.tensor_tensor(
                            out=Rall[:], in0=triLE.unsqueeze(1).to_broadcast([128, 4, 128]),
                            in1=dtA[:, c, :].unsqueeze(2).to_broadcast([128, 4, 128]), op=ALU.mult),
                            reads=[b_const, dt_b], writes=[Rall_b])

                    def s2(c=c, cs=cs, ts_=ts_, st=st):
                        Rflat = Rall[:].rearrange("p a b -> p (a b)")
                        bseg = gp_bank()
                        mm(bank_ap(bseg), SUf, Rflat, True, True, [b_const, Rall_b], bseg)
                        K.op("act", lambda h: h.activation(out=LT[:].rearrange("p a b -> p (a b)"), in_=bank_ap(bseg), func=AF.Exp),
                             reads=[banks[bseg]], writes=[LT_b])
                        bacs = gp_bank()
                        mm(bank_ap(bacs), onesf, Rflat, True, True, [b_const, Rall_b], bacs)
                        K.op("act", lambda h: h.activation(out=Eall[:].rearrange("p a b -> p (a b)"), in_=bank_ap(bacs), func=AF.Exp),
                             reads=[banks[bacs]], writes=[Eall_b])

                    def s2b(c=c, cs=cs, ts_=ts_, st=st):
                        bcb = gp_bank()
                        mm(bank_ap(bcb, 128), xsT[:, 2, cs], xsT[:, 3, cs], True, True, [xsT_b[2], xsT_b[3]], bcb)
                        K.op("dve", lambda h: h.tensor_tensor(out=CBm[:], in0=bank_ap(bcb, 128), in1=triLE, op=ALU.mult),
                             reads=[banks[bcb], b_const], writes=[CBm_b])

                    def s3(c=c, cs=cs, ts_=ts_, st=st):
                        K.op("dve", lambda h: h.tensor_tensor(out=MT[:], in0=LT[:], in1=CBm[:].unsqueeze(1).to_broadcast([128, 4, 128]),
                                                              op=ALU.mult), reads=[LT_b, CBm_b], writes=[MT_b])
                        K.op("pool", lambda h: h.tensor_tensor(out=CeT[:], in0=Eall[:],
                                                               in1=xsT[:, 3, cs].unsqueeze(1).to_broadcast([128, 4, 128]),
                                                               op=ALU.mult), reads=[Eall_b, xsT_b[3]], writes=[CeT_b])
                        K.op("dve", lambda h: h.tensor_tensor(out=wst[:], in0=dtv[:, c, :], in1=LT[:, :, 127], op=ALU.mult),
                             reads=[dt_b, LT_b], writes=[wst_b])
                        xt22 = xBtok[:, ts_, 0:256].rearrange("p (a b c) -> p a b c", a=2, b=2)
                        dt22 = dtv[:, c, :].rearrange("p (a b) -> p a b", a=2)
                        xdtp4 = xdtp[:].rearrange("p (a b) c -> p a b c", a=2)
                        xt4 = xBtok[:, ts_, 0:256].rearrange("p (a b) -> p a b", a=4)
                        for half in range(2):
                            K.op("dve", lambda h, half=half: h.tensor_tensor(
                                out=xdtp4[:, :, half, half * 64:(half + 1) * 64], in0=xt22[:, :, half, :],
                                in1=dt22[:, :, half:half + 1].to_broadcast([128, 2, 64]), op=ALU.mult),
                                reads=[xBtok_b[ts_], dt_b], writes=[xdtp_b])
                        K.op("dve", lambda h: h.tensor_tensor(out=xdtd[:], in0=xt4,
                                                              in1=wst[:].unsqueeze(2).to_broadcast([128, 4, 64]), op=ALU.mult),
                             reads=[xBtok_b[ts_], wst_b], writes=[xdtd_b])

                    def s4(c=c, cs=cs, ts_=ts_, st=st):
                        by = gp_bank()
                        st["by"] = by
                        for pr in range(2):
                            yo = ps[:, by * 512 + pr * 128:by * 512 + (pr + 1) * 128]
                            for hh in range(2):
                                h4 = 2 * pr + hh
                                mm(yo, xdtp[:, h4, :], MT[:, h4, :], hh == 0, False, [xdtp_b, MT_b], by)
                            for hh in range(2):
                                h4 = 2 * pr + hh
                                mm(yo, hstp[:, h4, :], CeT[:, h4, :], False, hh == 1, [hstp_b, CeT_b], by)
                        bst = gp_bank()
                        st["bst"] = bst
                        mm(bank_ap(bst, 256), xBtok[:, ts_, 256:384], xdtd[:].rearrange("p a b -> p (a b)"), True, True,
                           [xBtok_b[ts_], xdtd_b], bst)

                    def s5(c=c, cs=cs, ts_=ts_, st=st):
                        by, bst = st["by"], st["bst"]
                        for pr in range(2):
                            K.op("dve", lambda h, pr=pr: h.scalar_tensor_tensor(
                                out=yv[:, pr, cs], in0=xsT[:, pr, cs], scalar=dsk_s[:, pr:pr + 1],
                                in1=ps[:, by * 512 + pr * 128:by * 512 + (pr + 1) * 128], op0=ALU.mult, op1=ALU.add),
                                reads=[xsT_b[pr], banks[by], b_prm], writes=[yv_b[pr]])
                        K.op("dve", lambda h: h.tensor_tensor(out=hstt[:], in0=hst[:], in1=Eall[:, :, 127:128].to_broadcast([128, 4, 64]),
                                                              op=ALU.mult), reads=[hst_b, Eall_b], writes=[hstt_b])
                        K.op("dve", lambda h: h.tensor_tensor(out=hst[:], in0=hstt[:],
                                                              in1=bank_ap(bst, 256).rearrange("p (a b) -> p a b", a=4), op=ALU.add),
                             reads=[hstt_b, banks[bst]], writes=[hst_b])
                        hst22 = hst[:].rearrange("p (a b) c -> p a b c", a=2)
                        hstp4 = hstp[:].rearrange("p (a b) c -> p a b c", a=2)
                        for half in range(2):
                            K.op("pool", lambda h, half=half: h.tensor_copy(
                                out=hstp4[:, :, half, half * 64:(half + 1) * 64], in_=hst22[:, :, half, :]),
                                reads=[hst_b], writes=[hstp_b])

                    def s45(s4=s4, s5=s5):
                        s4()
                        s5()

                    L += [s1, s2, s2b, s3, s45]

                def gate():
                    for pr in range(2):
                        K.op("pool", lambda h, pr=pr: h.tensor_tensor(out=mixT[:, ms, pr, :], in0=yv[:, pr, :], in1=zs[:, pr, :],
                                                                      op=ALU.mult),
                             reads=[yv_b[pr], zs_b[pr]], writes=[mixT_b[ms][pr]])
                L.append(gate)
                return L

            def attention(i, side):
                nkt = 4 * i + 4
                qs = i % 2
                ms = i % 2
                npairs = 2 * nkt
                per = -(-len(side) // npairs) if side else 0
                for hd in range(2):
                    sbank = {}

                    def emit_S(kt):
                        diag = kt >= 4 * i
                        kti = kt // 4
                        pslot = kt % 4
                        for mp, lo in ((0, 0), (1, 64)):
                            sb_ = GPA[s_rr[0] % 3]
                            s_rr[0] += 1
                            mm(bank_ap(sb_), KT[lo:lo + 64, hd, kt * 128:(kt + 1) * 128], qn[lo:lo + 64, qs, hd, :], True, not diag,
                               [KT_b[hd][kti], qn_b[qs][hd]], sb_)
                            if diag:
                                mm(bank_ap(sb_), ident_bf[:], mk[:, kt - 4 * i, :], False, True, [b_const], sb_)
                            K.op("act", lambda h, sb_=sb_, mp=mp, pslot=pslot: h.activation(
                                out=Pb[:, mp, pslot, :], in_=bank_ap(sb_), func=AF.Exp, scale=0.125),
                                reads=[banks[sb_]], writes=[P_b[mp][pslot]])

                    def emit_PV(kt):
                        kti = kt // 4
                        pslot = kt % 4
                        for mp, ob in ((0, O1), (1, O2)):
                            mm(bank_ap(ob), Vt[:, hd, kt, :], Pb[:, mp, pslot, :], kt == 0, kt == nkt - 1,
                               [V_b[kti], P_b[mp][pslot]], ob)
                        if kt % 2 == 1:
                            j = 0
                            for k2 in (kt - 1, kt):
                                for mp in range(2):
                                    mm(ps[32 * j:32 * j + 32, LB * 512:(LB + 1) * 512], ones_bf[:, 0:32], Pb[:, mp, k2 % 4, :],
                                       k2 < 2, k2 >= nkt - 2, [b_const, P_b[mp][k2 % 4]], LB, tp=(0, 32 * j))
                                    j += 1

                    emit_S(0)
                    for kt in range(nkt):
                        if kt + 1 < nkt:
                            emit_S(kt + 1)
                        emit_PV(kt)
                        for _ in range(per):
                            if side:
                                side.pop(0)()
                    K.op("act", lambda h: h.activation(out=o12[:, 0, :], in_=bank_ap(O1), func=AF.Copy),
                         reads=[banks[O1]], writes=[o12_b[0]])
                    K.op("dve", lambda h: h.tensor_copy(out=o12[:, 1, :], in_=bank_ap(O2)), reads=[banks[O2]], writes=[o12_b[1]])
                    K.op("act", lambda h: h.activation(out=recp[:], in_=bank_ap(LB), func=AF.Copy), reads=[banks[LB]], writes=[recp_b])

                    def e1():
                        pass

                    def e2(hd=hd):
                        st = {}
                        for mp in range(2):
                            bb = gp_bank()
                            mm(bank_ap(bb), cst_f[:, 7 + mp, :], recp[:], True, True, [b_const, recp_b], bb)
                            K.op("act", lambda h, mp=mp, bb=bb: h.activation(out=rec[:, mp, :], in_=bank_ap(bb), func=AF.Ln),
                                 reads=[banks[bb]], writes=[rec_b[mp]])
                            K.op("act", lambda h, mp=mp: h.activation(out=rec[:, mp, :], in_=rec[:, mp, :], func=AF.Exp, scale=-1.0),
                                 reads=[rec_b[mp]], writes=[rec_b[mp]])
                            K.op("dve", lambda h, mp=mp: h.tensor_tensor(out=o12[:, mp, :], in0=o12[:, mp, :], in1=rec[:, mp, :],
                                                                         op=ALU.mult),
                                 reads=[rec_b[mp], o12_b[mp]], writes=[o12_b[mp]])

                    def e3():
                        K.op("dve", lambda h: h.scalar_tensor_tensor(out=of[:], in0=o12[:, 1, :], scalar=lamneg, in1=o12[:, 0, :],
                                                                     op0=ALU.mult, op1=ALU.add),
                             reads=[o12_b[0], o12_b[1], b_prm], writes=[of_b])
                        K.op("pool", lambda h: h.tensor_tensor(out=sqb2[:], in0=of[:], in1=of[:], op=ALU.mult),
                             reads=[of_b], writes=[sqb2_b])

                    def e4():
                        bk2 = gp_bank()
                        mm(bank_ap(bk2), ones_bf[:], sqb2[:], True, True, [sqb2_b, b_const], bk2)
                        rstd_from_ss(bk2, 128, rstd2[:], rstd2_b, ntmp2[:], ntmp2_b)

                    def e5(hd=hd):
                        K.op("dve", lambda h: h.scalar_tensor_tensor(
                            out=mixT[:, ms, 2 + hd, :], in0=of[:], scalar=sub08, in1=rstd2[:], op0=ALU.mult, op1=ALU.mult),
                            reads=[of_b, rstd2_b, b_prm], writes=[mixT_b[ms][2 + hd]])

                    ep = [e1, e2, e3, e4, e5]
                    if hd == 0:
                        side[0:0] = ep
                    else:
                        carry.extend(ep)
                while side:
                    side.pop(0)()

            carry = []

            def issue_ag(k):
                K.dma("pool", lambda h, k=k: h.collective_compute(
                    "AllGather", ALU.bypass, replica_groups=GROUPS,
                    ins=[mloc.ap()[k].opt()], outs=[mfull.ap()[k].opt()]),
                    reads=[b_mloc[k], hB_b[0], hB_b[1]], writes=[b_mfull[k]], inc=1, grp="cc2")

            def st_store(i):
                def f():
                    ms = i % 2
                    K.dma("sp", lambda h: h.dma_start(
                        out=mloc.ap()[i // 2][(i % 2) * 512:(i % 2 + 1) * 512, :].rearrange("(p j) t -> p j t", j=4),
                        in_=mixT[:, ms, :, :]),
                        reads=mixT_b[ms], writes=[b_mloc[i // 2]])
                    if i % 2 == 1:
                        issue_ag(i // 2)
                return f

            load_h(0)
            load_h(1)
            issue_pending(extra=list(b_win) + [b_const, b_prm, hB_b[0], hB_b[1]])
            for f in prelude_stages(0):
                f()
            for rr in range(4):
                for part, base in ((0, 256 * rr), (1, D + 256 * rr)):
                    K.dma("pool", lambda h, rr=rr, part=part, base=base: h.dma_start(
                        out=wout_bf.ap()[:, rr * 4 + part * 2:rr * 4 + part * 2 + 2, :],
                        in_=wout[base:base + 256, :].rearrange("(j p) c -> p j c", p=128)),
                        writes=[b_woutbf[rr]])
            for i in range(16):
                side = list(carry)
                del carry[:]
                if i >= 1:
                    side.append(st_store(i - 1))
                sl = ssd_stages(i)
                pl = prelude_stages(i + 1) if i + 1 < 16 else []
                while sl:
                    side += sl[:2]
                    del sl[:2]
                    if len(pl) > 3:
                        side.append(pl.pop(0))
                side += pl
                attention(i, side)
                if i + 2 < 16:
                    load_h(i + 2)
            for f in carry:
                f()
            st_store(15)()
            if debug:
                evs = []
                for k in range(8):
                    evs.append(K.dma("sp", lambda h, k=k: h.dma_start(out=dbg["m"][k], in_=mloc.ap()[k]),
                                     reads=[b_mloc[k]], writes=[Buf("d3")]))
                for ev in evs + tap_evs:
                    K.wait_event("sp", ev)
            K.flush(block)

        if stop_after == "B":
            return nc
        with ExitStack() as sc:
            xres = sc.enter_context(nc.sbuf_tensor("xresC", [128, 8, T], F32))
            xres_b = [[Buf("x") for _ in range(4)] for _ in range(8)]
            with ExitStack() as sc1:
                mx = sc1.enter_context(nc.sbuf_tensor("mx", [128, 16, T], BF16))
                mx_b = [[Buf("m") for _ in range(4)] for _ in range(16)]
                woutb = sc1.enter_context(nc.sbuf_tensor("woutb", [128, 16, D], BF16))
                wout_b = [Buf("wo") for _ in range(16)]
                sq4 = sc1.enter_context(nc.sbuf_tensor("sq4", [128, 2, 4, 512], BF16)); sq4_b = [Buf("sq4"), Buf("sq4")]
                rstd = sc1.enter_context(nc.sbuf_tensor("rstdC", [128, 2, 512], F32)); rstd_b = [Buf("r"), Buf("r")]
                ntmp = sc1.enter_context(nc.sbuf_tensor("ntmpC", [128, 2, 512], F32)); ntmp_b = [Buf("t"), Buf("t")]
                block = sc1.enter_context(nc.Block())
                bq_cache = {}

                def load_mx(h, tt, r):
                    if "bq" not in bq_cache:
                        bq_cache["bq"] = h.snap(h.partition_id() % 4)
                    bq = bq_cache["bq"]
                    src = mfull.ap().rearrange("(b k2) (r tl p j) t -> p b k2 r tl j t", b=4, r=4, tl=2, p=128)
                    return h.dma_start(out=mx[:, r * 4:(r + 1) * 4, tt * 512:(tt + 1) * 512],
                                       in_=src[:, bass.ds(bq, 1), tt // 2, r, tt % 2, :, :])

                def issue_mx(tt):
                    for r in range(4):
                        K.dma("sp", lambda h, tt=tt, r=r: load_mx(h, tt, r), reads=(b_mfull[0:7] if tt < 2 else b_mfull),
                              writes=[mx_b[r * 4 + j][tt] for j in range(4)])

                issue_mx(0)
                issue_mx(1)
                for q4 in range(4):
                    K.dma("sp", lambda h, q4=q4: h.dma_start(out=woutb[:, q4 * 4:(q4 + 1) * 4, :], in_=wout_bf.ap()[:, q4 * 4:(q4 + 1) * 4, :]),
                          reads=b_woutbf, writes=wout_b[q4 * 4:(q4 + 1) * 4])
                for kt in range(8):
                    K.dma("sp", lambda h, kt=kt: h.dma_start(out=xres[:, kt, :], in_=xsave.ap()[kt * 128:(kt + 1) * 128, :]),
                          reads=[b_xsave], writes=xres_b[kt])
                issue_mx(2)
                issue_mx(3)
                for tt in range(4):
                    tsl = slice(tt * 512, (tt + 1) * 512)
                    for g in range(2):
                        tiles = [8 * g + 0, 8 * g + 1, 8 * g + 4, 8 * g + 5]
                        for a, rj in enumerate(tiles):
                            K.op("pool", lambda h, a=a, rj=rj, tsl=tsl, g=g: h.tensor_tensor(out=sq4[:, g, a, :], in0=mx[:, rj, tsl],
                                                                                            in1=mx[:, rj, tsl], op=ALU.mult),
                                 reads=[mx_b[rj][tt]], writes=[sq4_b[g]])
                        bk = next_bank()
                        for a in range(4):
                            mm(bank_ap(bk), ones_bf[:], sq4[:, g, a, :], a == 0, a == 3, [sq4_b[g], b_const], bk)
                        rstd_from_ss(bk, 512, rstd[:, g, :], rstd_b[g], ntmp[:, g, :], ntmp_b[g])
                        for a, rj in enumerate(tiles):
                            rr, j = rj // 4, rj % 4
                            wk = 2 * rr + j
                            K.op("dve", lambda h, rj=rj, tsl=tsl, wk=wk, g=g: h.scalar_tensor_tensor(
                                out=mx[:, rj, tsl], in0=mx[:, rj, tsl], scalar=ncols[:, 3, wk:wk + 1], in1=rstd[:, g, :],
                                op0=ALU.mult, op1=ALU.mult),
                                reads=[mx_b[rj][tt], rstd_b[g], b_ncols], writes=[mx_b[rj][tt]])
                    for d in range(8):
                        bk = next_bank()
                        for rj in range(16):
                            mm(bank_ap(bk), woutb[:, rj, d * 128:(d + 1) * 128], mx[:, rj, tt * 512:(tt + 1) * 512], rj == 0, rj == 15,
                               [wout_b[rj], mx_b[rj][tt]], bk)
                        K.op("dve", lambda h, bk=bk, d=d, tt=tt: h.tensor_tensor(
                            out=xres[:, d, tt * 512:(tt + 1) * 512], in0=bank_ap(bk), in1=xres[:, d, tt * 512:(tt + 1) * 512], op=ALU.add),
                            reads=[banks[bk], xres_b[d][tt]], writes=[xres_b[d][tt]])
                K.flush(block)
            with ExitStack() as sc2:
                fb = alloc_ffn_bufs(sc2, "F2")
                block = sc2.enter_context(nc.Block())
                ffn(xres, xres_b, ncols[:, 2, :], b_ncols, w2g, w2u, w2d, fb)
                b_out = Buf("out")
                evs = []
                for kt in range(8):
                    evs.append(K.dma("sp", lambda h, kt=kt: h.dma_start(out=outT[kt * 128:(kt + 1) * 128, :], in_=xres[:, kt, :]),
                                     reads=xres_b[kt], writes=[b_out]))
                for ev in evs:
                    K.wait_event("sp", ev)
                K.flush(block)
    return nc


def _consts():
    p = np.arange(128)
    ident = np.eye(128, dtype=np.float32)
    ones = np.ones((128, 128), np.float32)
    bones = (p[:, None] // 64 == p[None, :] // 64).astype(np.float32)
    triLE = (p[:, None] <= p[None, :]).astype(np.float32)
    SU = (p[:, None] > p[None, :]).astype(np.float32)
    onesA = np.zeros((128, 128), np.float32); onesA[:, :64] = 1.0
    onesB = np.zeros((128, 128), np.float32); onesB[:, 64:] = 1.0
    selA = np.zeros((128, 128), np.float32); selA[0, :] = 1.0; selA[64, :] = 1.0
    selB = np.zeros((128, 128), np.float32); selB[32, :] = 1.0; selB[96, :] = 1.0
    cst = np.concatenate([ident, ones, bones, triLE, SU, onesA, onesB, selA, selB], axis=1)
    q = np.arange(512)
    mk = np.zeros((128, 4, 512), np.float32)
    for r in range(4):
        mk[:, r, :] = np.where(q[None, :] >= 128 * r + p[:, None], 0.0, NEG)
    return np.ascontiguousarray(cst), np.ascontiguousarray(mk.reshape(128, 2048))


def _col(v):
    return np.ascontiguousarray(np.asarray(v, np.float32).reshape(-1, 128).T)


def make_in_maps(inp):
    f = lambda a: np.asarray(a, np.float32)
    x = f(inp["x"])
    cst, mk = _consts()
    w_in = f(inp["w_in"])[0]
    conv_w = f(inp["conv_w"])[0]
    conv_b = f(inp["conv_b"])[0]
    shared = dict(
        w1g=f(inp["ffn1_w_gate"])[0], w1u=f(inp["ffn1_w_up"])[0], w1d=f(inp["ffn1_w_down"])[0],
        w2g=f(inp["ffn2_w_gate"])[0], w2u=f(inp["ffn2_w_up"])[0], w2d=f(inp["ffn2_w_down"])[0],
        wout=f(inp["w_out"])[0],
        n1w=_col(inp["ffn1_norm_w"][0]), nmw=_col(inp["mix_norm_w"][0]), n2w=_col(inp["ffn2_norm_w"][0]),
        ssdw=_col(inp["ssd_norm_w"][0]),
        qkw=np.ascontiguousarray(np.stack([np.tile(f(inp["q_norm_w"])[0], 2), np.tile(f(inp["k_norm_w"])[0], 2)], axis=1)),
        subw=np.ascontiguousarray(f(inp["attn_subln_w"])[0].reshape(128, 1)),
        lamv=np.ascontiguousarray(np.broadcast_to(np.concatenate(
            [f(inp["lambda_q1"])[0], f(inp["lambda_k1"])[0], f(inp["lambda_q2"])[0], f(inp["lambda_k2"])[0]])[None, :], (128, 256))),
        cst=cst, maskadd=mk,
    )
    maps = []
    for c in range(8):
        s, r = c // 4, c % 4
        g = r // 2
        cols = np.concatenate([
            np.arange(2576 + 256 * r, 2576 + 256 * r + 256),
            np.arange(3600 + 256 * r, 3600 + 256 * r + 256),
            np.arange(256 * r, 256 * r + 256),
            np.arange(1024 + 256 * r, 1024 + 256 * r + 256),
            np.arange(2048 + 128 * g, 2048 + 128 * g + 128),
            np.arange(2304 + 128 * g, 2304 + 128 * g + 128),
            np.arange(4624 + 256 * r, 4624 + 256 * r + 256),
            np.arange(2560 + 4 * r, 2560 + 4 * r + 4),
        ])
        cch = np.concatenate([np.arange(256 * r, 256 * r + 256), np.arange(1024 + 128 * g, 1024 + 128 * g + 128),
                              np.arange(1280 + 128 * g, 1280 + 128 * g + 128)])
        cw = conv_w[:, cch]
        convw = np.ascontiguousarray(cw.reshape(4, 4, 128).transpose(2, 1, 0).reshape(128, 16))
        convb = np.ascontiguousarray(conv_b[cch].reshape(4, 128).T)
        hs = slice(4 * r, 4 * r + 4)
        m = dict(shared)
        m.update(
            xT=np.ascontiguousarray(x[s, r * T:(r + 1) * T, :].T),
            win=np.ascontiguousarray(w_in[:, cols]),
            convw=convw, convb=convb,
            dtb=np.ascontiguousarray(np.broadcast_to(f(inp["dt_bias"])[0, hs][None, :], (128, 4))),
            alog=np.ascontiguousarray(np.broadcast_to(f(inp["a_log"])[0, hs][None, :], (128, 4))),
            dsk=np.ascontiguousarray(np.repeat(f(inp["d_skip"])[0, hs].reshape(2, 2), 64, axis=1).reshape(2, 128).T),
        )
        maps.append(m)
    return maps


_NC_CACHE = {}


def kernel(**inputs):
    if "nc" not in _NC_CACHE:
        _NC_CACHE["nc"] = build_program()
    nc = _NC_CACHE["nc"]
    maps = make_in_maps(inputs)
    res = run_bass_kernel_spmd(nc, maps, core_ids=list(range(8)))
    out = np.empty((2, S, D), np.float32)
    for c in range(8):
        s, r = c // 4, c % 4
        out[s, r * T:(r + 1) * T, :] = np.asarray(res.results[c]["outT"]).T
    return out
```

```python
import numpy as np
from contextlib import ExitStack
import concourse.bass as bass
import concourse.mybir as mybir
from concourse.bass_utils import run_bass_kernel_spmd

F32 = mybir.dt.float32
BF16 = mybir.dt.bfloat16
ALU = mybir.AluOpType
AF = mybir.ActivationFunctionType
AX = mybir.AxisListType

D = 1024
FF = 2816
NF = FF // 128
T = 2048
S = 8192
EPS = 1e-6
WIN_C = 1540
GROUPS = [[0, 1, 2, 3], [4, 5, 6, 7]]
NEG = -30000.0


class Buf:
    __slots__ = ("name", "w", "rc", "rd")

    def __init__(self, name):
        self.name = name
        self.w = None
        self.rc = {}
        self.rd = []


class Sched:
    CE = ("pe", "act", "dve", "pool")

    def __init__(self, nc, sems, dma_sems):
        self.nc = nc
        self.sem = sems
        self.dma_sems = dma_sems
        self.dma_n = {k: 0 for k in dma_sems}
        self.ops = {e: [] for e in ("pe", "act", "dve", "pool", "sp")}
        self.seq = {e: 0 for e in self.ops}
        self.flushed = {e: 0 for e in self.ops}
        self.inc_total = {e: 0 for e in self.CE}
        self.phase_end_rank = {e: 0 for e in self.CE}
        self.waited = {e: {} for e in self.ops}

    def _deps_and_mark(self, ev, reads, writes):
        deps = []
        for b in reads:
            if b.w is not None:
                deps.append(b.w)
        for b in writes:
            if b.w is not None:
                deps.append(b.w)
            for e, s in b.rc.items():
                deps.append(("c", e, s))
            deps.extend(b.rd)
        for b in reads:
            if ev[0] == "c":
                if b.rc.get(ev[1], 0) < ev[2]:
                    b.rc[ev[1]] = ev[2]
            else:
                b.rd.append(ev)
        for b in writes:
            b.w = ev
            b.rc = {}
            b.rd = []
        return deps

    def op(self, eng, fn, reads=(), writes=()):
        self.seq[eng] += 1
        ev = ("c", eng, self.seq[eng])
        deps = self._deps_and_mark(ev, reads, writes)
        self.ops[eng].append(dict(seq=self.seq[eng], fn=fn, deps=deps, kind="c"))
        return ev

    def dma(self, eng, fn, reads=(), writes=(), inc=16, grp=None):
        grp = grp or eng
        sems = self.dma_sems[grp]
        n = self.dma_n[grp]
        self.dma_n[grp] += 1
        k = n % len(sems)
        val = inc * (n // len(sems) + 1)
        ev = ("d", (grp, k), val)
        self.seq[eng] += 1
        deps = self._deps_and_mark(ev, reads, writes)
        if val > inc:
            deps.append(("d", (grp, k), val - inc))
        self.ops[eng].append(dict(seq=self.seq[eng], fn=fn, deps=deps, kind="d", sem=sems[k], inc=inc))
        return ev

    def wait_event(self, eng, ev):
        self.seq[eng] += 1
        self.ops[eng].append(dict(seq=self.seq[eng], fn=None, deps=[ev], kind="w"))

    def flush(self, block):
        need_inc = {e: set() for e in self.CE}
        for e, lst in self.ops.items():
            for o in lst:
                nd = []
                for d in o["deps"]:
                    if d[0] == "c":
                        e2, s2 = d[1], d[2]
                        if e2 == e:
                            if e == "pe" or o["seq"] - s2 > (10 if e == "pool" else 3):
                                continue
                        if s2 > self.flushed[e2]:
                            need_inc[e2].add(s2)
                    nd.append(d)
                o["deps"] = nd
        rank = {}
        for e in self.CE:
            lst = self.ops[e]
            lc = [o["seq"] for o in lst if o["kind"] == "c"]
            if lc:
                need_inc[e].add(lc[-1])
            r = self.inc_total[e]
            for o in lst:
                if o["kind"] == "c" and o["seq"] in need_inc[e]:
                    r += 1
                    rank[(e, o["seq"])] = r
            self.inc_total[e] = r
        new_phase_end = {e: self.inc_total[e] for e in self.CE}

        sched = self

        def emit(e, handle):
            waited = sched.waited[e]
            for o in sched.ops[e]:
                for d in o["deps"]:
                    if d[0] == "c":
                        e2, s2 = d[1], d[2]
                        if s2 <= sched.flushed[e2]:
                            val = sched.phase_end_rank[e2]
                        else:
                            val = rank[(e2, s2)]
                        key = ("c", e2)
                        semh = sched.sem[e2]
                    else:
                        key = ("d",) + d[1]
                        semh = sched.dma_sems[d[1][0]][d[1][1]]
                        val = d[2]
                    if waited.get(key, 0) < val:
                        handle.wait_ge(semh, val)
                        waited[key] = val
                if o["fn"] is None:
                    continue
                inst = o["fn"](handle)
                if o["kind"] == "d":
                    inst.then_inc(o["sem"], o["inc"])
                elif (e, o["seq"]) in rank:
                    inst.then_inc(sched.sem[e], 1)

        if self.ops["pe"]:
            block.tensor(lambda h: emit("pe", h))
        if self.ops["act"]:
            block.scalar(lambda h: emit("act", h))
        if self.ops["dve"]:
            block.vector(lambda h: emit("dve", h))
        if self.ops["pool"]:
            block.gpsimd(lambda h: emit("pool", h))
        if self.ops["sp"]:
            block.sync(lambda h: emit("sp", h))
        for e in self.ops:
            self.flushed[e] = self.seq[e]
            self.ops[e] = []
        self.phase_end_rank = new_phase_end


def build_program(debug=False, stop_after=None):
    nc = bass.Bass("TRN2", target_bir_lowering=False)

    def din(name, shape, dt=F32):
        return nc.dram_tensor(name, list(shape), dt, kind="ExternalInput").ap()

    xT = din("xT", [D, T])
    w1g = din("w1g", [D, FF]); w1u = din("w1u", [D, FF]); w1d = din("w1d", [FF, D])
    w2g = din("w2g", [D, FF]); w2u = din("w2u", [D, FF]); w2d = din("w2d", [FF, D])
    win = din("win", [D, WIN_C])
    wout = din("wout", [2 * D, D])
    n1w = din("n1w", [128, 8]); nmw = din("nmw", [128, 8]); n2w = din("n2w", [128, 8])
    ssdw = din("ssdw", [128, 8])
    convw = din("convw", [128, 16]); convb = din("convb", [128, 4])
    dtb = din("dtb", [128, 4]); alog = din("alog", [128, 4]); dsk = din("dsk", [128, 2])
    qkw = din("qkw", [128, 2]); subw = din("subw", [128, 1])
    lamv = din("lamv", [128, 4 * 64])
    cst = din("cst", [128, 9 * 128])
    maskadd = din("maskadd", [128, 4 * 512])
    outT = nc.dram_tensor("outT", [D, T], F32, kind="ExternalOutput").ap()

    hloc = nc.dram_tensor("hloc", [4, D, 512], BF16)
    hfull = nc.dram_tensor("hfull", [4, 4 * D, 512], BF16)
    mloc = nc.dram_tensor("mloc", [8, 512, 1024], BF16)
    mfull = nc.dram_tensor("mfull", [8, 4 * 512, 1024], BF16)
    xsave = nc.dram_tensor("xsave", [D, T], F32)
    win_bf = nc.dram_tensor("win_bf", [128, 8, WIN_C], BF16)
    wout_bf = nc.dram_tensor("wout_bf", [128, 16, D], BF16)
    mk_bf = nc.dram_tensor("mk_bf", [128, 4 * 512], BF16)
    dbg = {}
    if debug:
        dbg["x1"] = nc.dram_tensor("dbg_x1", [D, T], F32, kind="ExternalOutput").ap()
        dbg["h"] = nc.dram_tensor("dbg_h", [4, D, 512], BF16, kind="ExternalOutput").ap()
        dbg["m"] = nc.dram_tensor("dbg_m", [8, 512, 1024], BF16, kind="ExternalOutput").ap()

    es = ExitStack()
    with es:
        sems = {e: es.enter_context(nc.semaphore("c_" + e)) for e in Sched.CE}
        dma_sems = {
            "sp": [es.enter_context(nc.semaphore(f"dsp{i}")) for i in range(8)],
            "pool": [es.enter_context(nc.semaphore(f"dpl{i}")) for i in range(8)],
        }
        dma_sems["cc1"] = [es.enter_context(nc.semaphore(f"cc1_{i}")) for i in range(2)]
        dma_sems["cc2"] = [es.enter_context(nc.semaphore(f"cc2_{i}")) for i in range(2)]
        K = Sched(nc, sems, dma_sems)
        tap_evs = []

        def tap(name, ap, shape, dt, reads):
            if not debug:
                return
            t = nc.dram_tensor("tap_" + name, list(shape), dt, kind="ExternalOutput").ap()
            tap_evs.append(K.dma("sp", lambda h: h.dma_start(out=t, in_=ap), reads=reads, writes=[Buf("tap")]))

        ps = es.enter_context(nc.psum_tensor("ps", [128, 8 * 512], F32))
        banks = [Buf(f"bank{i}") for i in range(8)]

        def bank_ap(i, n=512):
            return ps[:, i * 512:i * 512 + n]

        bank_rr = [0]

        def next_bank(pool=(0, 1, 2, 3, 4, 5, 6, 7)):
            i = pool[bank_rr[0] % len(pool)]
            bank_rr[0] += 1
            return i

        def mm(out_ap, lhsT, rhs, start, stop, reads, bank, tp=None):
            if tp is None:
                K.op("pe", lambda h, o=out_ap, l=lhsT, r=rhs, s=start, p=stop: h.matmul(o, lhsT=l, rhs=r, start=s, stop=p),
                     reads=reads, writes=[banks[bank]])
            else:
                K.op("pe", lambda h, o=out_ap, l=lhsT, r=rhs, s=start, p=stop, tp=tp: h.matmul(
                    o, lhsT=l, rhs=r, start=s, stop=p, tile_position=tp), reads=reads, writes=[banks[bank]])

        def rstd_from_ss(bank, n_feat, rstd_ap, rstd_buf, tmp_ap, tmp_buf):
            K.op("act", lambda h: h.activation(out=tmp_ap, in_=bank_ap(bank), func=AF.Ln, bias=eps_col[:, 0:1], scale=1.0 / n_feat),
                 reads=[banks[bank], b_const], writes=[tmp_buf])
            K.op("act", lambda h: h.activation(out=rstd_ap, in_=tmp_ap, func=AF.Exp, scale=-0.5),
                 reads=[tmp_buf], writes=[rstd_buf])

        def rmsnorm_tokens(xres, xres_b, wcol, wcol_b, dst_fn, dst_bufs, t0, scr):
            sq, sq_b, rstd, rstd_b, tmp, tmp_b = scr
            tt = t0 // 512
            K.op("act", lambda h: h.activation(out=sq[:], in_=xres[:, :, t0:t0 + 512], func=AF.Square),
                 reads=[xres_b[kt][tt] for kt in range(8)], writes=[sq_b])
            bk = next_bank()
            for kt in range(8):
                mm(bank_ap(bk), ones_bf[:], sq[:, kt, :], kt == 0, kt == 7, [sq_b, b_const], bk)
            rstd_from_ss(bk, D, rstd[:], rstd_b, tmp[:], tmp_b)
            for kt in range(8):
                K.op("dve", lambda h, kt=kt: h.scalar_tensor_tensor(
                    out=dst_fn(kt), in0=xres[:, kt, t0:t0 + 512], scalar=wcol[:, kt:kt + 1], in1=rstd[:],
                    op0=ALU.mult, op1=ALU.mult),
                    reads=[xres_b[kt][tt], rstd_b, wcol_b], writes=dst_bufs)

        def ffn(xres, xres_b, wcol, wcol_b, wg, wu, wd, bufs, after_st=None, mid_st=None):
            (hT, hT_b, actT, actT_b, wgu, wgu_b, wdb, wdb_b, sg, sg_b, scr) = bufs
            for st in range(2):
                for tl in range(2):
                    rmsnorm_tokens(xres, xres_b, wcol, wcol_b,
                                   lambda kt, tl=tl: hT[:, kt, tl * 512:(tl + 1) * 512], [hT_b[tl]],
                                   st * 1024 + tl * 512, scr)

                def load_gu(blk):
                    slot = blk % 2
                    for j, w in enumerate((wg, wu)):
                        src = w[:, blk * 256:(blk + 1) * 256].rearrange("(kt p) c -> p kt c", p=128)
                        K.dma("pool", lambda h, j=j, slot=slot, src=src: h.dma_start(out=wgu[:, slot, j, :, :], in_=src),
                              writes=[wgu_b[slot][j]])

                def load_d(q):
                    slot = q % 2
                    src = wd[:, q * 256:(q + 1) * 256].rearrange("(f p) c -> p f c", p=128)
                    for hf in range(2):
                        K.dma("pool", lambda h, slot=slot, src=src, hf=hf: h.dma_start(
                            out=wdb[:, slot, hf * 11:(hf + 1) * 11, :], in_=src[:, hf * 11:(hf + 1) * 11, :]),
                            writes=[wdb_b[slot][hf]])

                load_gu(0)
                for blk in range(11):
                    if blk + 1 < 11:
                        load_gu(blk + 1)
                    elif True:
                        load_d(0)
                    if blk == 2 and mid_st is not None:
                        mid_st(st)
                    slot = blk % 2
                    for fl in range(2):
                        f = 2 * blk + fl
                        for tl in range(2):
                            bg = next_bank()
                            bu = next_bank()
                            for j, bk in ((0, bg), (1, bu)):
                                for kt in range(8):
                                    mm(bank_ap(bk), wgu[:, slot, j, kt, fl * 128:(fl + 1) * 128],
                                       hT[:, kt, tl * 512:(tl + 1) * 512], kt == 0, kt == 7,
                                       [wgu_b[slot][j], hT_b[tl]], bk)
                            si = (f * 2 + tl) % 2
                            K.op("act", lambda h, bg=bg, si=si: h.activation(out=sg[:, si, :], in_=bank_ap(bg), func=AF.Silu),
                                 reads=[banks[bg]], writes=[sg_b[si]])
                            K.op("dve", lambda h, bu=bu, si=si, f=f, tl=tl: h.tensor_tensor(
                                out=actT[:, f, tl * 512:(tl + 1) * 512], in0=sg[:, si, :], in1=bank_ap(bu), op=ALU.mult),
                                reads=[sg_b[si], banks[bu]], writes=[actT_b[f][tl]])
                for q in range(4):
                    if q + 1 < 4:
                        load_d(q + 1)
                    slot = q % 2
                    for dl in range(2):
                        d = 2 * q + dl
                        for tl in range(2):
                            bk = next_bank()
                            for f in range(NF):
                                mm(bank_ap(bk), wdb[:, slot, f, dl * 128:(dl + 1) * 128],
                                   actT[:, f, tl * 512:(tl + 1) * 512], f == 0, f == NF - 1,
                                   [wdb_b[slot][f // 11], actT_b[f][tl]], bk)
                            t0 = st * 1024 + tl * 512
                            tt = t0 // 512
                            K.op("dve", lambda h, bk=bk, d=d, t0=t0: h.scalar_tensor_tensor(
                                out=xres[:, d, t0:t0 + 512], in0=bank_ap(bk), scalar=0.5, in1=xres[:, d, t0:t0 + 512],
                                op0=ALU.mult, op1=ALU.add),
                                reads=[banks[bk], xres_b[d][tt]], writes=[xres_b[d][tt]])
                if after_st is not None:
                    after_st(st)

        def alloc_ffn_bufs(st_, tg):
            hT = st_.enter_context(nc.sbuf_tensor("hT" + tg, [128, 8, 1024], BF16))
            actT = st_.enter_context(nc.sbuf_tensor("actT" + tg, [128, NF, 1024], BF16))
            wgu = st_.enter_context(nc.sbuf_tensor("wgu" + tg, [128, 2, 2, 8, 256], BF16))
            wdb = st_.enter_context(nc.sbuf_tensor("wdb" + tg, [128, 2, NF, 256], BF16))
            sg = st_.enter_context(nc.sbuf_tensor("sg" + tg, [128, 2, 512], F32))
            sq = st_.enter_context(nc.sbuf_tensor("sq" + tg, [128, 8, 512], BF16))
            rstd = st_.enter_context(nc.sbuf_tensor("rstd" + tg, [128, 512], F32))
            tmp = st_.enter_context(nc.sbuf_tensor("ntmp" + tg, [128, 512], F32))
            return (hT, [Buf("hT0"), Buf("hT1")], actT, [[Buf("a"), Buf("a")] for _ in range(NF)],
                    wgu, [[Buf("w"), Buf("w")] for _ in range(2)], wdb, [[Buf("w"), Buf("w")] for _ in range(2)],
                    sg, [Buf("sg0"), Buf("sg1")], (sq, Buf("sq"), rstd, Buf("rstd"), tmp, Buf("tmp")))

        ones_bf = es.enter_context(nc.sbuf_tensor("ones_bf", [128, 128], BF16))
        ident_bf = es.enter_context(nc.sbuf_tensor("ident_bf", [128, 128], BF16))
        bones_bf = es.enter_context(nc.sbuf_tensor("bones_bf", [128, 128], BF16))
        cst_f = es.enter_context(nc.sbuf_tensor("cst_f", [128, 9, 128], F32))
        onesAB_bf = es.enter_context(nc.sbuf_tensor("onesAB_bf", [128, 2, 128], BF16))
        eps_col = es.enter_context(nc.sbuf_tensor("eps_col", [128, 1], F32))
        ncols = es.enter_context(nc.sbuf_tensor("ncols", [128, 4, 8], F32))
        b_const = Buf("const")
        b_ncols = Buf("ncols")

        with ExitStack() as sa:
            xres = sa.enter_context(nc.sbuf_tensor("xres", [128, 8, T], F32))
            xres_b = [[Buf("x") for _ in range(4)] for _ in range(8)]
            fb = alloc_ffn_bufs(sa, "A")
            actT = fb[2]
            block = sa.enter_context(nc.Block())
            K.dma("sp", lambda h: h.dma_start(out=cst_f[:], in_=cst.rearrange("p (a b) -> p a b", a=9)), writes=[b_const])
            for j, src in enumerate((n1w, nmw, n2w, ssdw)):
                K.dma("sp", lambda h, j=j, src=src: h.dma_start(out=ncols[:, j, :], in_=src), writes=[b_ncols])
            K.op("dve", lambda h: h.tensor_copy(out=ident_bf[:], in_=cst_f[:, 0, :]), reads=[b_const], writes=[b_const])
            K.op("dve", lambda h: h.tensor_copy(out=ones_bf[:], in_=cst_f[:, 1, :]), reads=[b_const], writes=[b_const])
            K.op("dve", lambda h: h.tensor_copy(out=bones_bf[:], in_=cst_f[:, 2, :]), reads=[b_const], writes=[b_const])
            K.op("dve", lambda h: h.tensor_copy(out=onesAB_bf[:], in_=cst_f[:, 5:7, :]), reads=[b_const], writes=[b_const])
            K.op("dve", lambda h: h.memset(eps_col[:], EPS), writes=[b_const])
            for hf in range(2):
                for kt in range(8):
                    K.dma("sp", lambda h, kt=kt, hf=hf: h.dma_start(
                        out=xres[:, kt, hf * 1024:(hf + 1) * 1024], in_=xT[kt * 128:(kt + 1) * 128, hf * 1024:(hf + 1) * 1024]),
                        writes=[xres_b[kt][2 * hf], xres_b[kt][2 * hf + 1]])
            hmix = sa.enter_context(nc.sbuf_tensor("hmix", [128, 8, 1024], BF16))
            hm_b = [Buf("hm") for _ in range(4)]
            b_xsave = Buf("xsave")
            b_winbf = [Buf("winbf") for _ in range(3)]
            b_woutbf = [Buf("woutbf") for _ in range(4)]
            b_mkbf = Buf("mkbf")
            b_hloc = [Buf("hloc") for _ in range(4)]
            b_hfull = [Buf("hfull") for _ in range(4)]
            pend_ag = []

            def mixnorm_st(st):
                for kt in range(8):
                    K.dma("sp", lambda h, kt=kt, st=st: h.dma_start(
                        out=xsave.ap()[kt * 128:(kt + 1) * 128, st * 1024:(st + 1) * 1024], in_=xres[:, kt, st * 1024:(st + 1) * 1024]),
                        reads=[xres_b[kt][2 * st], xres_b[kt][2 * st + 1]], writes=[b_xsave])
                for tl in range(2):
                    tt = 2 * st + tl
                    rmsnorm_tokens(xres, xres_b, ncols[:, 1, :], b_ncols,
                                   lambda kt, tl=tl: hmix[:, kt, tl * 512:(tl + 1) * 512], [hm_b[tt]],
                                   tt * 512, fb[10])
                    K.dma("sp", lambda h, tl=tl, tt=tt: h.dma_start(
                        out=hloc.ap()[tt].rearrange("(kt p) t -> p kt t", p=128),
                        in_=hmix[:, :, tl * 512:(tl + 1) * 512]),
                        reads=[hm_b[tt]], writes=[b_hloc[tt]])
                    pend_ag.append(tt)

            def issue_pending(st=None, extra=()):
                if st == 0:
                    for j, (c0, c1) in enumerate(((0, 512), (512, 1024), (1024, WIN_C))):
                        K.dma("pool", lambda h, c0=c0, c1=c1: h.dma_start(
                            out=win_bf.ap()[:, :, c0:c1], in_=win[:, c0:c1].rearrange("(kt p) c -> p kt c", p=128)),
                            writes=[b_winbf[j]])
                    K.dma("pool", lambda h: h.dma_start(out=mk_bf.ap(), in_=maskadd), writes=[b_mkbf])
                while pend_ag:
                    tt = pend_ag.pop(0)
                    K.dma("pool", lambda h, tt=tt: h.collective_compute(
                        "AllGather", ALU.bypass, replica_groups=GROUPS,
                        ins=[hloc.ap()[tt].opt()], outs=[hfull.ap()[tt].opt()]),
                        reads=[b_hloc[tt]] + list(extra), writes=[b_hfull[tt]], inc=1, grp="cc1")

            ffn(xres, xres_b, ncols[:, 0, :], b_ncols, w1g, w1u, w1d, fb, after_st=mixnorm_st, mid_st=issue_pending)
            if debug:
                evs = []
                for kt in range(8):
                    sl = slice(kt * 128, (kt + 1) * 128)
                    evs.append(K.dma("sp", lambda h, sl=sl: h.dma_start(out=dbg["x1"][sl, :], in_=xsave.ap()[sl, :]),
                                     reads=[b_xsave], writes=[Buf("d1")]))
                    if kt < 4:
                        evs.append(K.dma("sp", lambda h, kt=kt: h.dma_start(out=dbg["h"][kt], in_=hloc.ap()[kt]),
                                         reads=[b_hloc[kt]], writes=[Buf("d2")]))
                for ev in evs:
                    K.wait_event("sp", ev)
            K.flush(block)

        b_mloc = [Buf("mloc") for _ in range(8)]
        b_mfull = [Buf("mfull") for _ in range(8)]
        if stop_after in ("A", "A1", "A2"):
            return nc
        with ExitStack() as sb:
            def sbt(name, shape, dt):
                return sb.enter_context(nc.sbuf_tensor(name, shape, dt))
            KT = sbt("KT", [128, 2, S], BF16)
            Vt = sbt("Vt", [128, 2, 64, 128], BF16)
            KT_b = [[Buf("k") for _ in range(16)] for _ in range(2)]
            V_b = [Buf("v") for _ in range(16)]
            winb = sbt("winb", [128, 8, WIN_C], BF16)
            b_win = [Buf("win") for _ in range(6)]
            hB = sbt("hB", [128, 2, 8, 512], BF16)
            hB_b = [Buf("hB0"), Buf("hB1")]
            mk = sbt("mk", [128, 4, 512], BF16)
            prm = sbt("prm", [128, 32], F32)
            prm2 = sbt("prm2", [128, 8], F32)
            lam_t = sbt("lam_t", [128, 4, 64], F32)
            lam_s = sbt("lam_s", [128, 4], F32)
            b_prm = Buf("prm")
            qk_raw = sbt("qk_raw", [128, 2, 512], F32)
            qk_raw_b = [Buf("qkr") for _ in range(2)]
            sqb2 = sbt("sqb2", [128, 512], BF16); sqb2_b = Buf("sqb2")
            rstd2 = sbt("rstdB2", [128, 512], F32); rstd2_b = Buf("rstdB2")
            ntmp2 = sbt("ntmpB2", [128, 512], F32); ntmp2_b = Buf("ntmpB2")
            sqb = sbt("sqb", [128, 512], BF16); sqb_b = Buf("sqb")
            rstd = sbt("rstdB", [128, 512], F32); rstd_b = Buf("rstdB")
            ntmp = sbt("ntmpB", [128, 512], F32); ntmp_b = Buf("ntmpB")
            qn = sbt("qn", [128, 2, 2, 512], BF16); qn_b = [[Buf("qn0"), Buf("qn1")] for _ in range(2)]
            zs = sbt("zs", [128, 2, 512], F32); zs_b = [Buf("zs0"), Buf("zs1")]
            xbc = sbt("xbc", [128, 4, 515], F32); xbc_b = [Buf("xbc") for _ in range(4)]
            cacc = sbt("cacc", [128, 4, 512], F32); cacc_b = [Buf("cacc") for _ in range(4)]
            xsT = sbt("xsT", [128, 4, 512], BF16); xsT_b = [Buf("xsT") for _ in range(4)]
            dtpre2 = sbt("dtpre", [128, 2, 4, 4], F32); dt_b2 = [Buf("dt0"), Buf("dt1")]
            dtv2 = sbt("dtv", [128, 2, 4, 4], F32)
            dtA2 = sbt("dtA", [128, 2, 4, 4], F32)
            xBtok = sbt("xBtok", [128, 2, 384], BF16); xBtok_b = [Buf("xBt0"), Buf("xBt1")]
            Rall = sbt("Rall", [128, 4, 128], F32); Rall_b = Buf("Rall")
            LT = sbt("LT", [128, 4, 128], F32); LT_b = Buf("LT")
            Eall = sbt("Eall", [128, 4, 128], F32); Eall_b = Buf("Eall")
            CBm = sbt("CBm", [128, 128], F32); CBm_b = Buf("CBm")
            MT = sbt("MT", [128, 4, 128], BF16); MT_b = Buf("MT")
            CeT = sbt("CeT", [128, 4, 128], BF16); CeT_b = Buf("CeT")
            wst = sbt("wst", [128, 4], F32); wst_b = Buf("wst")
            xdtp = sbt("xdtp", [128, 4, 128], BF16); xdtp_b = Buf("xdtp")
            xdtd = sbt("xdtd", [128, 4, 64], BF16); xdtd_b = Buf("xdtd")
            hst = sbt("hst", [128, 4, 64], F32); hst_b = Buf("hst")
            hstt = sbt("hstt", [128, 4, 64], F32); hstt_b = Buf("hstt")
            hstp = sbt("hstp", [128, 4, 128], BF16); hstp_b = Buf("hstp")
            yv = sbt("yv", [128, 2, 512], F32); yv_b = [Buf("yv0"), Buf("yv1")]
            Pb = sbt("Pb", [128, 2, 4, 512], BF16); P_b = [[Buf("P") for _ in range(4)] for _ in range(2)]
            rec = sbt("rec", [128, 2, 512], F32); rec_b = [Buf("rec0"), Buf("rec1")]
            recp = sbt("recp", [128, 512], F32); recp_b = Buf("recp")
            o12 = sbt("o12", [128, 2, 512], F32); o12_b = [Buf("o1"), Buf("o2")]
            of = sbt("of", [128, 512], F32); of_b = Buf("of")
            mixT = sbt("mixT", [128, 2, 4, 512], BF16); mixT_b = [[Buf("mx") for _ in range(4)] for _ in range(2)]
            block = sb.enter_context(nc.Block())

            identf = cst_f[:, 0, :]
            onesf = cst_f[:, 1, :]
            triLE = cst_f[:, 3, :]
            SUf = cst_f[:, 4, :]

            for hk in range(2):
                K.dma("sp", lambda h, hk=hk: h.dma_start(out=winb[:, hk * 4:(hk + 1) * 4, :], in_=win_bf.ap()[:, hk * 4:(hk + 1) * 4, :]),
                      reads=b_winbf, writes=b_win)
            K.dma("sp", lambda h: h.dma_start(out=mk[:], in_=mk_bf.ap().rearrange("p (a b) -> p a b", a=4)),
                  reads=[b_mkbf], writes=[b_const])
            for src, c0, n in ((convw, 0, 16), (convb, 16, 4), (dtb, 20, 4), (alog, 24, 4), (dsk, 28, 2), (qkw, 30, 2)):
                K.dma("sp", lambda h, src=src, c0=c0, n=n: h.dma_start(out=prm[:, c0:c0 + n], in_=src), writes=[b_prm])
            K.dma("sp", lambda h: h.dma_start(out=prm2[:, 0:1], in_=subw), writes=[b_prm])
            K.dma("sp", lambda h: h.dma_start(out=lam_t[:], in_=lamv.rearrange("p (a b) -> p a b", a=4)), writes=[b_prm])
            convw_s = prm[:, 0:16]; convb_s = prm[:, 16:20]; dtb_s = prm[:, 20:24]; alog_s = prm[:, 24:28]
            dsk_s = prm[:, 28:30]; qkw_s = prm[:, 30:32]
            A_row = prm2[:, 1:5]; lamneg = prm2[:, 5:6]; sub08 = prm2[:, 6:7]
            K.op("act", lambda h: h.activation(out=A_row, in_=alog_s, func=AF.Exp), reads=[b_prm], writes=[b_prm])
            K.op("dve", lambda h: h.tensor_scalar(out=A_row, in0=A_row, scalar1=-1.0, scalar2=None, op0=ALU.mult),
                 reads=[b_prm], writes=[b_prm])
            K.op("dve", lambda h: h.tensor_tensor(out=lam_t[:, 0, :], in0=lam_t[:, 0, :], in1=lam_t[:, 1, :], op=ALU.mult),
                 reads=[b_prm], writes=[b_prm])
            K.op("dve", lambda h: h.tensor_tensor(out=lam_t[:, 2, :], in0=lam_t[:, 2, :], in1=lam_t[:, 3, :], op=ALU.mult),
                 reads=[b_prm], writes=[b_prm])
            K.op("dve", lambda h: h.reduce_sum(out=lam_s[:, 0:1], in_=lam_t[:, 0, :], axis=AX.X), reads=[b_prm], writes=[b_prm])
            K.op("dve", lambda h: h.reduce_sum(out=lam_s[:, 1:2], in_=lam_t[:, 2, :], axis=AX.X), reads=[b_prm], writes=[b_prm])
            K.op("act", lambda h: h.activation(out=lam_s[:, 2:4], in_=lam_s[:, 0:2], func=AF.Exp), reads=[b_prm], writes=[b_prm])
            K.op("dve", lambda h: h.tensor_tensor(out=lamneg, in0=lam_s[:, 3:4], in1=lam_s[:, 2:3], op=ALU.subtract),
                 reads=[b_prm], writes=[b_prm])
            K.op("dve", lambda h: h.tensor_scalar(out=lamneg, in0=lamneg, scalar1=-0.2, scalar2=None, op0=ALU.add),
                 reads=[b_prm], writes=[b_prm])
            K.op("dve", lambda h: h.tensor_scalar(out=sub08, in0=prm2[:, 0:1], scalar1=0.8, scalar2=None, op0=ALU.mult),
                 reads=[b_prm], writes=[b_prm])
            K.op("pool", lambda h: h.memset(hst[:], 0.0), writes=[hst_b])
            K.op("pool", lambda h: h.memset(hstp[:], 0.0), writes=[hstp_b])
            K.op("pool", lambda h: h.memset(xdtp[:], 0.0), writes=[xdtp_b])
            K.op("pool", lambda h: h.memset(xbc[:], 0.0), writes=xbc_b)

            O1, O2, LB = 3, 4, 5
            GP = (6, 7)
            GPA = (0, 1, 2)
            s_rr = [0]

            def load_h(i):
                slot = i % 2
                src = hfull.ap()[i % 4][(i // 4) * D:(i // 4 + 1) * D, :].rearrange("(kt p) t -> p kt t", p=128)
                K.dma("sp", lambda h, slot=slot, src=src: h.dma_start(out=hB[:, slot, :, :], in_=src),
                      reads=[b_hfull[i % 4]], writes=[hB_b[slot]])

            bankA = [0]

            def gp_bank():
                return next_bank(GP)

            def proj_tile(i, ct):
                slot = i % 2
                bk = gp_bank()
                for kt in range(8):
                    mm(bank_ap(bk), winb[:, kt, ct * 128:(ct + 1) * 128], hB[:, slot, kt, :], kt == 0, kt == 7,
                       [b_win[ct // 2], hB_b[slot]], bk)
                return bk

            def st_qk(i, ct):
                def f():
                    t0 = i * 512
                    qs = i % 2
                    bk = proj_tile(i, ct)
                    rs = ct % 2
                    K.op("dve", lambda h: h.tensor_copy(out=qk_raw[:, rs, :], in_=bank_ap(bk)),
                         reads=[banks[bk]], writes=[qk_raw_b[rs]])
                    K.op("dve", lambda h: h.tensor_tensor(out=sqb[:], in0=qk_raw[:, rs, :], in1=qk_raw[:, rs, :], op=ALU.mult),
                         reads=[qk_raw_b[rs]], writes=[sqb_b])
                    bk2 = gp_bank()
                    mm(bank_ap(bk2), bones_bf[:], sqb[:], True, True, [sqb_b, b_const], bk2)
                    rstd_from_ss(bk2, 64, rstd[:], rstd_b, ntmp[:], ntmp_b)
                    if ct < 2:
                        dst, dstb, wc = qn[:, qs, ct, :], qn_b[qs][ct], qkw_s[:, 0:1]
                    else:
                        dst, dstb, wc = KT[:, ct - 2, t0:t0 + 512], KT_b[ct - 2][i], qkw_s[:, 1:2]
                    K.op("dve", lambda h: h.scalar_tensor_tensor(
                        out=dst, in0=qk_raw[:, rs, :], scalar=wc, in1=rstd[:], op0=ALU.mult, op1=ALU.mult),
                        reads=[qk_raw_b[rs], rstd_b, b_prm], writes=[dstb])
                return f

            def st_z(i, pz):
                def f():
                    bk = proj_tile(i, 4 + pz)
                    K.op("dve", lambda h: h.tensor_copy(out=zs[:, pz, :], in_=bank_ap(bk)),
                         reads=[banks[bk]], writes=[zs_b[pz]])
                return f

            def st_xbc(i, c):
                def f():
                    bk = proj_tile(i, 6 + c)
                    K.op("dve", lambda h: h.tensor_copy(out=xbc[:, c, 3:515], in_=bank_ap(bk)),
                         reads=[banks[bk]], writes=[xbc_b[c]])
                return f

            def st_v(i, c):
                def f():
                    slot = i % 2
                    dtpre = dtpre2[:, i % 2]
                    dt_b = dt_b2[i % 2]
                    bk = gp_bank()
                    for kt in range(8):
                        mm(bank_ap(bk, 260), hB[:, slot, kt, c * 128:(c + 1) * 128], winb[:, kt, 1280:1540], kt == 0, kt == 7,
                           [b_win[5], hB_b[slot]], bk)
                    for hv in range(2):
                        K.op("dve", lambda h, hv=hv: h.tensor_copy(
                            out=Vt[:, hv, 4 * i + c, :], in_=ps[:, bk * 512 + hv * 128:bk * 512 + (hv + 1) * 128]),
                            reads=[banks[bk]], writes=[V_b[i]])
                    K.op("dve", lambda h: h.tensor_tensor(
                        out=dtpre[:, c, :], in0=ps[:, bk * 512 + 256:bk * 512 + 260], in1=dtb_s, op=ALU.add),
                        reads=[banks[bk], b_prm], writes=[dt_b])
                return f

            def st_dt(i):
                def f():
                    dtpre = dtpre2[:, i % 2]; dtv = dtv2[:, i % 2]; dtA = dtA2[:, i % 2]
                    dt_b = dt_b2[i % 2]
                    K.op("act", lambda h: h.activation(out=dtv[:], in_=dtpre[:], func=AF.Exp), reads=[dt_b], writes=[dt_b])
                    K.op("act", lambda h: h.activation(out=dtv[:], in_=dtv[:], func=AF.Ln, bias=1.0), reads=[dt_b], writes=[dt_b])
                    K.op("dve", lambda h: h.tensor_tensor(out=dtA[:], in0=dtv[:], in1=A_row.unsqueeze(1).to_broadcast([128, 4, 4]),
                                                          op=ALU.mult), reads=[dt_b, b_prm], writes=[dt_b])
                return f

            def st_conv(i, c):
                def f():
                    K.op("dve", lambda h: h.tensor_scalar(out=cacc[:, c, :], in0=xbc[:, c, 0:512],
                                                          scalar1=convw_s[:, c * 4:c * 4 + 1], scalar2=None, op0=ALU.mult),
                         reads=[xbc_b[c], b_prm], writes=[cacc_b[c]])
                    for k in range(1, 4):
                        K.op("dve", lambda h, k=k: h.scalar_tensor_tensor(
                            out=cacc[:, c, :], in0=xbc[:, c, k:k + 512], scalar=convw_s[:, c * 4 + k:c * 4 + k + 1],
                            in1=cacc[:, c, :], op0=ALU.mult, op1=ALU.add),
                            reads=[xbc_b[c], b_prm, cacc_b[c]], writes=[cacc_b[c]])
                    K.op("dve", lambda h: h.tensor_copy(out=xbc[:, c, 0:3], in_=xbc[:, c, 512:515]),
                         reads=[xbc_b[c]], writes=[xbc_b[c]])
                return f

            def st_silu(i):
                def f():
                    for c in range(4):
                        K.op("act", lambda h, c=c: h.activation(out=xsT[:, c, :], in_=cacc[:, c, :], func=AF.Silu,
                                                                bias=convb_s[:, c:c + 1]),
                             reads=[cacc_b[c], b_prm], writes=[xsT_b[c]])
                    for pz in range(2):
                        K.op("act", lambda h, pz=pz: h.activation(out=zs[:, pz, :], in_=zs[:, pz, :], func=AF.Silu),
                             reads=[zs_b[pz]], writes=[zs_b[pz]])
                return f

            def prelude_stages(i):
                L = [st_qk(i, ct) for ct in range(4)] + [st_v(i, c) for c in range(4)] + [st_dt(i)]
                for c in range(4):
                    L += [st_xbc(i, c), st_conv(i, c)]
                L += [st_z(i, pz) for pz in range(2)]
                L += [st_silu(i)]
                return L

            def ssd_stages(i):
                L = []
                ms = i % 2
                dtv = dtv2[:, i % 2]; dtA = dtA2[:, i % 2]
                dt_b = dt_b2[i % 2]
                for c in range(4):
                    cs = slice(c * 128, (c + 1) * 128)
                    ts_ = c % 2
                    st = {}

                    def s1(c=c, cs=cs, ts_=ts_, st=st):
                        bk = gp_bank()
                        for j in range(3):
                            mm(ps[:, bk * 512 + j * 128:bk * 512 + (j + 1) * 128], xsT[:, j, cs], ident_bf[:], True, True,
                               [xsT_b[j], b_const], bk)
                        K.op("dve", lambda h: h.tensor_copy(out=xBtok[:, ts_, :], in_=bank_ap(bk, 384)),
                             reads=[banks[bk]], writes=[xBtok_b[ts_]])
                        K.op("dve", lambda h: h.tensor_tensor(
                            out=Rall[:], in0=triLE.unsqueeze(1).to_broadcast([128, 4, 128]),
                            in1=dtA[:, c, :].unsqueeze(2).to_broadcast([128, 4, 128]), op=ALU.mult),
                            reads=[b_const, dt_b], writes=[Rall_b])

                    def s2(c=c, cs=cs, ts_=ts_, st=st):
                        Rflat = Rall[:].rearrange("p a b -> p (a b)")
                        bseg = gp_bank()
                        mm(bank_ap(bseg), SUf, Rflat, True, True, [b_const, Rall_b], bseg)
                        K.op("act", lambda h: h.activation(out=LT[:].rearrange("p a b -> p (a b)"), in_=bank_ap(bseg), func=AF.Exp),
                             reads=[banks[bseg]], writes=[LT_b])
                        bacs = gp_bank()
                        mm(bank_ap(bacs), onesf, Rflat, True, True, [b_const, Rall_b], bacs)
                        K.op("act", lambda h: h.activation(out=Eall[:].rearrange("p a b -> p (a b)"), in_=bank_ap(bacs), func=AF.Exp),
                             reads=[banks[bacs]], writes=[Eall_b])

                    def s2b(c=c, cs=cs, ts_=ts_, st=st):
                        bcb = gp_bank()
                        mm(bank_ap(bcb, 128), xsT[:, 2, cs], xsT[:, 3, cs], True, True, [xsT_b[2], xsT_b[3]], bcb)
                        K.op("dve", lambda h: h.tensor_tensor(out=CBm[:], in0=bank_ap(bcb, 128), in1=triLE, op=ALU.mult),
                             reads=[banks[bcb], b_const], writes=[CBm_b])

                    def s3(c=c, cs=cs, ts_=ts_, st=st):
                        K.op("dve", lambda h: h.tensor_tensor(out=MT[:], in0=LT[:], in1=CBm[:].unsqueeze(1).to_broadcast([128, 4, 128]),
                                                              op=ALU.mult), reads=[LT_b, CBm_b], writes=[MT_b])
                        K.op("dve", lambda h: h.tensor_tensor(out=CeT[:], in0=Eall[:],
                                                               in1=xsT[:, 3, cs].unsqueeze(1).to_broadcast([128, 4, 128]),
                                                               op=ALU.mult), reads=[Eall_b, xsT_b[3]], writes=[CeT_b])
                        K.op("dve", lambda h: h.tensor_tensor(out=wst[:], in0=dtv[:, c, :], in1=LT[:, :, 127], op=ALU.mult),
                             reads=[dt_b, LT_b], writes=[wst_b])
                        xt22 = xBtok[:, ts_, 0:256].rearrange("p (a b c) -> p a b c", a=2, b=2)
                        dt22 = dtv[:, c, :].rearrange("p (a b) -> p a b", a=2)
                        xdtp4 = xdtp[:].rearrange("p (a b) c -> p a b c", a=2)
                        xt4 = xBtok[:, ts_, 0:256].rearrange("p (a b) -> p a b", a=4)
                        for half in range(2):
                            K.op("dve", lambda h, half=half: h.tensor_tensor(
                                out=xdtp4[:, :, half, half * 64:(half + 1) * 64], in0=xt22[:, :, half, :],
                                in1=dt22[:, :, half:half + 1].to_broadcast([128, 2, 64]), op=ALU.mult),
                                reads=[xBtok_b[ts_], dt_b], writes=[xdtp_b])
                        K.op("dve", lambda h: h.tensor_tensor(out=xdtd[:], in0=xt4,
                                                              in1=wst[:].unsqueeze(2).to_broadcast([128, 4, 64]), op=ALU.mult),
                             reads=[xBtok_b[ts_], wst_b], writes=[xdtd_b])

                    def s4(c=c, cs=cs, ts_=ts_, st=st):
                        by = gp_bank()
                        st["by"] = by
                        for pr in range(2):
                            yo = ps[:, by * 512 + pr * 128:by * 512 + (pr + 1) * 128]
                            for hh in range(2):
                                h4 = 2 * pr + hh
                                mm(yo, xdtp[:, h4, :], MT[:, h4, :], hh == 0, False, [xdtp_b, MT_b], by)
                            for hh in range(2):
                                h4 = 2 * pr + hh
                                mm(yo, hstp[:, h4, :], CeT[:, h4, :], False, hh == 1, [hstp_b, CeT_b], by)
                        bst = gp_bank()
                        st["bst"] = bst
                        mm(bank_ap(bst, 256), xBtok[:, ts_, 256:384], xdtd[:].rearrange("p a b -> p (a b)"), True, True,
                           [xBtok_b[ts_], xdtd_b], bst)

                    def s5(c=c, cs=cs, ts_=ts_, st=st):
                        by, bst = st["by"], st["bst"]
                        for pr in range(2):
                            K.op("dve", lambda h, pr=pr: h.scalar_tensor_tensor(
                                out=yv[:, pr, cs], in0=xsT[:, pr, cs], scalar=dsk_s[:, pr:pr + 1],
                                in1=ps[:, by * 512 + pr * 128:by * 512 + (pr + 1) * 128], op0=ALU.mult, op1=ALU.add),
                                reads=[xsT_b[pr], banks[by], b_prm], writes=[yv_b[pr]])
                        K.op("dve", lambda h: h.tensor_tensor(out=hstt[:], in0=hst[:], in1=Eall[:, :, 127:128].to_broadcast([128, 4, 64]),
                                                              op=ALU.mult), reads=[hst_b, Eall_b], writes=[hstt_b])
                        K.op("dve", lambda h: h.tensor_tensor(out=hst[:], in0=hstt[:],
                                                              in1=bank_ap(bst, 256).rearrange("p (a b) -> p a b", a=4), op=ALU.add),
                             reads=[hstt_b, banks[bst]], writes=[hst_b])
                        hst22 = hst[:].rearrange("p (a b) c -> p a b c", a=2)
                        hstp4 = hstp[:].rearrange("p (a b) c -> p a b c", a=2)
                        for half in range(2):
                            K.op("dve", lambda h, half=half: h.tensor_copy(
                                out=hstp4[:, :, half, half * 64:(half + 1) * 64], in_=hst22[:, :, half, :]),
                                reads=[hst_b], writes=[hstp_b])

                    def s45(s4=s4, s5=s5):
                        s4()
                        s5()

                    L += [s1, s2, s2b, s3, s45]

                def gate():
                    for pr in range(2):
                        K.op("dve", lambda h, pr=pr: h.tensor_tensor(out=mixT[:, ms, pr, :], in0=yv[:, pr, :], in1=zs[:, pr, :],
                                                                      op=ALU.mult),
                             reads=[yv_b[pr], zs_b[pr]], writes=[mixT_b[ms][pr]])
                L.append(gate)
                return L

            def attention(i, side):
                nkt = 4 * i + 4
                qs = i % 2
                ms = i % 2
                npairs = 2 * nkt
                per = -(-len(side) // npairs) if side else 0
                for hd in range(2):
                    sbank = {}

                    def emit_S(kt):
                        diag = kt >= 4 * i
                        kti = kt // 4
                        pslot = kt % 4
                        for mp, lo in ((0, 0), (1, 64)):
                            sb_ = GPA[s_rr[0] % 3]
                            s_rr[0] += 1
                            mm(bank_ap(sb_), KT[lo:lo + 64, hd, kt * 128:(kt + 1) * 128], qn[lo:lo + 64, qs, hd, :], True, not diag,
                               [KT_b[hd][kti], qn_b[qs][hd]], sb_)
                            if diag:
                                mm(bank_ap(sb_), ident_bf[:], mk[:, kt - 4 * i, :], False, True, [b_const], sb_)
                            K.op("act", lambda h, sb_=sb_, mp=mp, pslot=pslot: h.activation(
                                out=Pb[:, mp, pslot, :], in_=bank_ap(sb_), func=AF.Exp, scale=0.125),
                                reads=[banks[sb_]], writes=[P_b[mp][pslot]])

                    def emit_PV(kt):
                        kti = kt // 4
                        pslot = kt % 4
                        for mp, ob in ((0, O1), (1, O2)):
                            mm(bank_ap(ob), Vt[:, hd, kt, :], Pb[:, mp, pslot, :], kt == 0, kt == nkt - 1,
                               [V_b[kti], P_b[mp][pslot]], ob)
                        if kt % 2 == 1:
                            j = 0
                            for k2 in (kt - 1, kt):
                                for mp in range(2):
                                    mm(ps[32 * j:32 * j + 32, LB * 512:(LB + 1) * 512], ones_bf[:, 0:32], Pb[:, mp, k2 % 4, :],
                                       k2 < 2, k2 >= nkt - 2, [b_const, P_b[mp][k2 % 4]], LB, tp=(0, 32 * j))
                                    j += 1

                    emit_S(0)
                    for kt in range(nkt):
                        if kt + 1 < nkt:
                            emit_S(kt + 1)
                        emit_PV(kt)
                        for _ in range(per):
                            if side:
                                side.pop(0)()
                    K.op("act", lambda h: h.activation(out=o12[:, 0, :], in_=bank_ap(O1), func=AF.Copy),
                         reads=[banks[O1]], writes=[o12_b[0]])
                    K.op("dve", lambda h: h.tensor_copy(out=o12[:, 1, :], in_=bank_ap(O2)), reads=[banks[O2]], writes=[o12_b[1]])
                    K.op("act", lambda h: h.activation(out=recp[:], in_=bank_ap(LB), func=AF.Copy), reads=[banks[LB]], writes=[recp_b])

                    def e1():
                        pass

                    def e2(hd=hd):
                        st = {}
                        for mp in range(2):
                            bb = gp_bank()
                            mm(bank_ap(bb), cst_f[:, 7 + mp, :], recp[:], True, True, [b_const, recp_b], bb)
                            K.op("act", lambda h, mp=mp, bb=bb: h.activation(out=rec[:, mp, :], in_=bank_ap(bb), func=AF.Ln),
                                 reads=[banks[bb]], writes=[rec_b[mp]])
                            K.op("act", lambda h, mp=mp: h.activation(out=rec[:, mp, :], in_=rec[:, mp, :], func=AF.Exp, scale=-1.0),
                                 reads=[rec_b[mp]], writes=[rec_b[mp]])
                            K.op("dve", lambda h, mp=mp: h.tensor_tensor(out=o12[:, mp, :], in0=o12[:, mp, :], in1=rec[:, mp, :],
                                                                         op=ALU.mult),
                                 reads=[rec_b[mp], o12_b[mp]], writes=[o12_b[mp]])

                    def e3():
                        K.op("dve", lambda h: h.scalar_tensor_tensor(out=of[:], in0=o12[:, 1, :], scalar=lamneg, in1=o12[:, 0, :],
                                                                     op0=ALU.mult, op1=ALU.add),
                             reads=[o12_b[0], o12_b[1], b_prm], writes=[of_b])
                        K.op("dve", lambda h: h.tensor_tensor(out=sqb2[:], in0=of[:], in1=of[:], op=ALU.mult),
                             reads=[of_b], writes=[sqb2_b])

                    def e4():
                        bk2 = gp_bank()
                        mm(bank_ap(bk2), ones_bf[:], sqb2[:], True, True, [sqb2_b, b_const], bk2)
                        rstd_from_ss(bk2, 128, rstd2[:], rstd2_b, ntmp2[:], ntmp2_b)

                    def e5(hd=hd):
                        K.op("dve", lambda h: h.scalar_tensor_tensor(
                            out=mixT[:, ms, 2 + hd, :], in0=of[:], scalar=sub08, in1=rstd2[:], op0=ALU.mult, op1=ALU.mult),
                            reads=[of_b, rstd2_b, b_prm], writes=[mixT_b[ms][2 + hd]])

                    ep = [e1, e2, e3, e4, e5]
                    if hd == 0:
                        side[0:0] = ep
                    else:
                        carry.extend(ep)
                while side:
                    side.pop(0)()

            carry = []

            def issue_ag(k):
                K.dma("pool", lambda h, k=k: h.collective_compute(
                    "AllGather", ALU.bypass, replica_groups=GROUPS,
                    ins=[mloc.ap()[k].opt()], outs=[mfull.ap()[k].opt()]),
                    reads=[b_mloc[k], hB_b[0], hB_b[1]], writes=[b_mfull[k]], inc=1, grp="cc2")

            def st_store(i):
                def f():
                    ms = i % 2
                    K.dma("sp", lambda h: h.dma_start(
                        out=mloc.ap()[i // 2][:, (i % 2) * 512:(i % 2 + 1) * 512].rearrange("(j p) t -> p j t", p=128),
                        in_=mixT[:, ms, :, :]),
                        reads=mixT_b[ms], writes=[b_mloc[i // 2]])
                    if i % 2 == 1:
                        issue_ag(i // 2)
                return f

            load_h(0)
            load_h(1)
            issue_pending(extra=list(b_win) + [b_const, b_prm, hB_b[0], hB_b[1]])
            for f in prelude_stages(0):
                f()
            for rr in range(4):
                for part, base in ((0, 256 * rr), (1, D + 256 * rr)):
                    K.dma("pool", lambda h, rr=rr, part=part, base=base: h.dma_start(
                        out=wout_bf.ap()[:, rr * 4 + part * 2:rr * 4 + part * 2 + 2, :],
                        in_=wout[base:base + 256, :].rearrange("(j p) c -> p j c", p=128)),
                        writes=[b_woutbf[rr]])
            for i in range(16):
                side = list(carry)
                del carry[:]
                if i >= 1:
                    side.append(st_store(i - 1))
                sl = ssd_stages(i)
                pl = prelude_stages(i + 1) if i + 1 < 16 else []
                while sl:
                    side += sl[:2]
                    del sl[:2]
                    if len(pl) > 3:
                        side.append(pl.pop(0))
                side += pl
                attention(i, side)
                if i + 2 < 16:
                    load_h(i + 2)
            for f in carry:
                f()
            st_store(15)()
            if debug:
                evs = []
                for k in range(8):
                    evs.append(K.dma("sp", lambda h, k=k: h.dma_start(out=dbg["m"][k], in_=mloc.ap()[k]),
                                     reads=[b_mloc[k]], writes=[Buf("d3")]))
                for ev in evs + tap_evs:
                    K.wait_event("sp", ev)
            K.flush(block)

        if stop_after == "B":
            return nc
        with ExitStack() as sc:
            xres = sc.enter_context(nc.sbuf_tensor("xresC", [128, 8, T], F32))
            xres_b = [[Buf("x") for _ in range(4)] for _ in range(8)]
            with ExitStack() as sc1:
                mx = sc1.enter_context(nc.sbuf_tensor("mx", [128, 16, T], BF16))
                mx_b = [[Buf("m") for _ in range(4)] for _ in range(16)]
                woutb = sc1.enter_context(nc.sbuf_tensor("woutb", [128, 16, D], BF16))
                wout_b = [Buf("wo") for _ in range(16)]
                sq4 = sc1.enter_context(nc.sbuf_tensor("sq4", [128, 2, 4, 512], BF16)); sq4_b = [Buf("sq4"), Buf("sq4")]
                rstd = sc1.enter_context(nc.sbuf_tensor("rstdC", [128, 2, 512], F32)); rstd_b = [Buf("r"), Buf("r")]
                ntmp = sc1.enter_context(nc.sbuf_tensor("ntmpC", [128, 2, 512], F32)); ntmp_b = [Buf("t"), Buf("t")]
                block = sc1.enter_context(nc.Block())
                def load_mx(h, tt):
                    pid = h.partition_id()
                    bq = pid % 4
                    src = mfull.ap().rearrange("(b k2) (rj p) (hh t) -> p b k2 rj hh t", b=4, p=128, hh=2)
                    return h.dma_start(out=mx[:, :, tt * 512:(tt + 1) * 512],
                                       in_=src[:, bass.ds(bq, 1), tt // 2, :, tt % 2, :])

                def issue_mx(tt):
                    K.dma("sp", lambda h, tt=tt: load_mx(h, tt), reads=(b_mfull[0:7] if tt < 2 else b_mfull),
                          writes=[mx_b[rj][tt] for rj in range(16)])

                issue_mx(0)
                issue_mx(1)
                for q4 in range(4):
                    K.dma("sp", lambda h, q4=q4: h.dma_start(out=woutb[:, q4 * 4:(q4 + 1) * 4, :], in_=wout_bf.ap()[:, q4 * 4:(q4 + 1) * 4, :]),
                          reads=b_woutbf, writes=wout_b[q4 * 4:(q4 + 1) * 4])
                for kt in range(8):
                    K.dma("sp", lambda h, kt=kt: h.dma_start(out=xres[:, kt, :], in_=xsave.ap()[kt * 128:(kt + 1) * 128, :]),
                          reads=[b_xsave], writes=xres_b[kt])
                issue_mx(2)
                issue_mx(3)
                for tt in range(4):
                    tsl = slice(tt * 512, (tt + 1) * 512)
                    for g in range(2):
                        tiles = [8 * g + 0, 8 * g + 1, 8 * g + 4, 8 * g + 5]
                        for a, rj in enumerate(tiles):
                            K.op("pool", lambda h, a=a, rj=rj, tsl=tsl, g=g: h.tensor_tensor(out=sq4[:, g, a, :], in0=mx[:, rj, tsl],
                                                                                            in1=mx[:, rj, tsl], op=ALU.mult),
                                 reads=[mx_b[rj][tt]], writes=[sq4_b[g]])
                        bk = next_bank()
                        for a in range(4):
                            mm(bank_ap(bk), ones_bf[:], sq4[:, g, a, :], a == 0, a == 3, [sq4_b[g], b_const], bk)
                        rstd_from_ss(bk, 512, rstd[:, g, :], rstd_b[g], ntmp[:, g, :], ntmp_b[g])
                        for a, rj in enumerate(tiles):
                            rr, j = rj // 4, rj % 4
                            wk = 2 * rr + j
                            K.op("dve", lambda h, rj=rj, tsl=tsl, wk=wk, g=g: h.scalar_tensor_tensor(
                                out=mx[:, rj, tsl], in0=mx[:, rj, tsl], scalar=ncols[:, 3, wk:wk + 1], in1=rstd[:, g, :],
                                op0=ALU.mult, op1=ALU.mult),
                                reads=[mx_b[rj][tt], rstd_b[g], b_ncols], writes=[mx_b[rj][tt]])
                    for d in range(8):
                        bk = next_bank()
                        for rj in range(16):
                            mm(bank_ap(bk), woutb[:, rj, d * 128:(d + 1) * 128], mx[:, rj, tt * 512:(tt + 1) * 512], rj == 0, rj == 15,
                               [wout_b[rj], mx_b[rj][tt]], bk)
                        K.op("dve", lambda h, bk=bk, d=d, tt=tt: h.tensor_tensor(
                            out=xres[:, d, tt * 512:(tt + 1) * 512], in0=bank_ap(bk), in1=xres[:, d, tt * 512:(tt + 1) * 512], op=ALU.add),
                            reads=[banks[bk], xres_b[d][tt]], writes=[xres_b[d][tt]])
                K.flush(block)
            with ExitStack() as sc2:
                fb = alloc_ffn_bufs(sc2, "F2")
                block = sc2.enter_context(nc.Block())
                ffn(xres, xres_b, ncols[:, 2, :], b_ncols, w2g, w2u, w2d, fb)
                b_out = Buf("out")
                evs = []
                for kt in range(8):
                    evs.append(K.dma("sp", lambda h, kt=kt: h.dma_start(out=outT[kt * 128:(kt + 1) * 128, :], in_=xres[:, kt, :]),
                                     reads=xres_b[kt], writes=[b_out]))
                for ev in evs:
                    K.wait_event("sp", ev)
                K.flush(block)
    return nc


def _consts():
    p = np.arange(128)
    ident = np.eye(128, dtype=np.float32)
    ones = np.ones((128, 128), np.float32)
    bones = (p[:, None] // 64 == p[None, :] // 64).astype(np.float32)
    triLE = (p[:, None] <= p[None, :]).astype(np.float32)
    SU = (p[:, None] > p[None, :]).astype(np.float32)
    onesA = np.zeros((128, 128), np.float32); onesA[:, :64] = 1.0
    onesB = np.zeros((128, 128), np.float32); onesB[:, 64:] = 1.0
    selA = np.zeros((128, 128), np.float32); selA[0, :] = 1.0; selA[64, :] = 1.0
    selB = np.zeros((128, 128), np.float32); selB[32, :] = 1.0; selB[96, :] = 1.0
    cst = np.concatenate([ident, ones, bones, triLE, SU, onesA, onesB, selA, selB], axis=1)
    q = np.arange(512)
    mk = np.zeros((128, 4, 512), np.float32)
    for r in range(4):
        mk[:, r, :] = np.where(q[None, :] >= 128 * r + p[:, None], 0.0, NEG)
    return np.ascontiguousarray(cst), np.ascontiguousarray(mk.reshape(128, 2048))


def _col(v):
    return np.ascontiguousarray(np.asarray(v, np.float32).reshape(-1, 128).T)


def make_in_maps(inp):
    f = lambda a: np.asarray(a, np.float32)
    x = f(inp["x"])
    cst, mk = _consts()
    w_in = f(inp["w_in"])[0]
    conv_w = f(inp["conv_w"])[0]
    conv_b = f(inp["conv_b"])[0]
    shared = dict(
        w1g=f(inp["ffn1_w_gate"])[0], w1u=f(inp["ffn1_w_up"])[0], w1d=f(inp["ffn1_w_down"])[0],
        w2g=f(inp["ffn2_w_gate"])[0], w2u=f(inp["ffn2_w_up"])[0], w2d=f(inp["ffn2_w_down"])[0],
        wout=f(inp["w_out"])[0],
        n1w=_col(inp["ffn1_norm_w"][0]), nmw=_col(inp["mix_norm_w"][0]), n2w=_col(inp["ffn2_norm_w"][0]),
        ssdw=_col(inp["ssd_norm_w"][0]),
        qkw=np.ascontiguousarray(np.stack([np.tile(f(inp["q_norm_w"])[0], 2), np.tile(f(inp["k_norm_w"])[0], 2)], axis=1)),
        subw=np.ascontiguousarray(f(inp["attn_subln_w"])[0].reshape(128, 1)),
        lamv=np.ascontiguousarray(np.broadcast_to(np.concatenate(
            [f(inp["lambda_q1"])[0], f(inp["lambda_k1"])[0], f(inp["lambda_q2"])[0], f(inp["lambda_k2"])[0]])[None, :], (128, 256))),
        cst=cst, maskadd=mk,
    )
    maps = []
    for c in range(8):
        s, r = c // 4, c % 4
        g = r // 2
        cols = np.concatenate([
            np.arange(2576 + 256 * r, 2576 + 256 * r + 256),
            np.arange(3600 + 256 * r, 3600 + 256 * r + 256),
            np.arange(256 * r, 256 * r + 256),
            np.arange(1024 + 256 * r, 1024 + 256 * r + 256),
            np.arange(2048 + 128 * g, 2048 + 128 * g + 128),
            np.arange(2304 + 128 * g, 2304 + 128 * g + 128),
            np.arange(4624 + 256 * r, 4624 + 256 * r + 256),
            np.arange(2560 + 4 * r, 2560 + 4 * r + 4),
        ])
        cch = np.concatenate([np.arange(256 * r, 256 * r + 256), np.arange(1024 + 128 * g, 1024 + 128 * g + 128),
                              np.arange(1280 + 128 * g, 1280 + 128 * g + 128)])
        cw = conv_w[:, cch]
        convw = np.ascontiguousarray(cw.reshape(4, 4, 128).transpose(2, 1, 0).reshape(128, 16))
        convb = np.ascontiguousarray(conv_b[cch].reshape(4, 128).T)
        hs = slice(4 * r, 4 * r + 4)
        m = dict(shared)
        m.update(
            xT=np.ascontiguousarray(x[s, r * T:(r + 1) * T, :].T),
            win=np.ascontiguousarray(w_in[:, cols]),
            convw=convw, convb=convb,
            dtb=np.ascontiguousarray(np.broadcast_to(f(inp["dt_bias"])[0, hs][None, :], (128, 4))),
            alog=np.ascontiguousarray(np.broadcast_to(f(inp["a_log"])[0, hs][None, :], (128, 4))),
            dsk=np.ascontiguousarray(np.repeat(f(inp["d_skip"])[0, hs].reshape(2, 2), 64, axis=1).reshape(2, 128).T),
        )
        maps.append(m)
    return maps


_NC_CACHE = {}


def kernel(**inputs):
    if "nc" not in _NC_CACHE:
        _NC_CACHE["nc"] = build_program()
    nc = _NC_CACHE["nc"]
    maps = make_in_maps(inputs)
    res = run_bass_kernel_spmd(nc, maps, core_ids=list(range(8)))
    out = np.empty((2, S, D), np.float32)
    for c in range(8):
        s, r = c // 4, c % 4
        out[s, r * T:(r + 1) * T, :] = np.asarray(res.results[c]["outT"]).T
    return out
```

```python
import numpy as np
from contextlib import ExitStack
import concourse.bass as bass
import concourse.mybir as mybir
from concourse.bass_utils import run_bass_kernel_spmd

F32 = mybir.dt.float32
BF16 = mybir.dt.bfloat16
ALU = mybir.AluOpType
AF = mybir.ActivationFunctionType
AX = mybir.AxisListType

D = 1024
FF = 2816
NF = FF // 128
T = 2048
S = 8192
EPS = 1e-6
WIN_C = 1540
GROUPS = [[0, 1, 2, 3], [4, 5, 6, 7]]
NEG = -30000.0


class Buf:
    __slots__ = ("name", "w", "rc", "rd")

    def __init__(self, name):
        self.name = name
        self.w = None
        self.rc = {}
        self.rd = []


class Sched:
    CE = ("pe", "act", "dve", "pool")

    def __init__(self, nc, sems, dma_sems):
        self.nc = nc
        self.sem = sems
        self.dma_sems = dma_sems
        self.dma_n = {k: 0 for k in dma_sems}
        self.ops = {e: [] for e in ("pe", "act", "dve", "pool", "sp")}
        self.seq = {e: 0 for e in self.ops}
        self.flushed = {e: 0 for e in self.ops}
        self.inc_total = {e: 0 for e in self.CE}
        self.phase_end_rank = {e: 0 for e in self.CE}
        self.waited = {e: {} for e in self.ops}

    def _deps_and_mark(self, ev, reads, writes):
        deps = []
        for b in reads:
            if b.w is not None:
                deps.append(b.w)
        for b in writes:
            if b.w is not None:
                deps.append(b.w)
            for e, s in b.rc.items():
                deps.append(("c", e, s))
            deps.extend(b.rd)
        for b in reads:
            if ev[0] == "c":
                if b.rc.get(ev[1], 0) < ev[2]:
                    b.rc[ev[1]] = ev[2]
            else:
                b.rd.append(ev)
        for b in writes:
            b.w = ev
            b.rc = {}
            b.rd = []
        return deps

    def op(self, eng, fn, reads=(), writes=()):
        self.seq[eng] += 1
        ev = ("c", eng, self.seq[eng])
        deps = self._deps_and_mark(ev, reads, writes)
        self.ops[eng].append(dict(seq=self.seq[eng], fn=fn, deps=deps, kind="c"))
        return ev

    def dma(self, eng, fn, reads=(), writes=(), inc=16, grp=None):
        grp = grp or eng
        sems = self.dma_sems[grp]
        n = self.dma_n[grp]
        self.dma_n[grp] += 1
        k = n % len(sems)
        val = inc * (n // len(sems) + 1)
        ev = ("d", (grp, k), val)
        self.seq[eng] += 1
        deps = self._deps_and_mark(ev, reads, writes)
        if val > inc:
            deps.append(("d", (grp, k), val - inc))
        self.ops[eng].append(dict(seq=self.seq[eng], fn=fn, deps=deps, kind="d", sem=sems[k], inc=inc))
        return ev

    def wait_event(self, eng, ev):
        self.seq[eng] += 1
        self.ops[eng].append(dict(seq=self.seq[eng], fn=None, deps=[ev], kind="w"))

    def flush(self, block):
        need_inc = {e: set() for e in self.CE}
        for e, lst in self.ops.items():
            for o in lst:
                nd = []
                for d in o["deps"]:
                    if d[0] == "c":
                        e2, s2 = d[1], d[2]
                        if e2 == e:
                            if e == "pe" or o["seq"] - s2 > (10 if e == "pool" else 3):
                                continue
                        if s2 > self.flushed[e2]:
                            need_inc[e2].add(s2)
                    nd.append(d)
                o["deps"] = nd
        rank = {}
        for e in self.CE:
            lst = self.ops[e]
            lc = [o["seq"] for o in lst if o["kind"] == "c"]
            if lc:
                need_inc[e].add(lc[-1])
            r = self.inc_total[e]
            for o in lst:
                if o["kind"] == "c" and o["seq"] in need_inc[e]:
                    r += 1
                    rank[(e, o["seq"])] = r
            self.inc_total[e] = r
        new_phase_end = {e: self.inc_total[e] for e in self.CE}

        sched = self

        def emit(e, handle):
            waited = sched.waited[e]
            for o in sched.ops[e]:
                for d in o["deps"]:
                    if d[0] == "c":
                        e2, s2 = d[1], d[2]
                        if s2 <= sched.flushed[e2]:
                            val = sched.phase_end_rank[e2]
                        else:
                            val = rank[(e2, s2)]
                        key = ("c", e2)
                        semh = sched.sem[e2]
                    else:
                        key = ("d",) + d[1]
                        semh = sched.dma_sems[d[1][0]][d[1][1]]
                        val = d[2]
                    if waited.get(key, 0) < val:
                        handle.wait_ge(semh, val)
                        waited[key] = val
                if o["fn"] is None:
                    continue
                inst = o["fn"](handle)
                if o["kind"] == "d":
                    inst.then_inc(o["sem"], o["inc"])
                elif (e, o["seq"]) in rank:
                    inst.then_inc(sched.sem[e], 1)

        if self.ops["pe"]:
            block.tensor(lambda h: emit("pe", h))
        if self.ops["act"]:
            block.scalar(lambda h: emit("act", h))
        if self.ops["dve"]:
            block.vector(lambda h: emit("dve", h))
        if self.ops["pool"]:
            block.gpsimd(lambda h: emit("pool", h))
        if self.ops["sp"]:
            block.sync(lambda h: emit("sp", h))
        for e in self.ops:
            self.flushed[e] = self.seq[e]
            self.ops[e] = []
        self.phase_end_rank = new_phase_end


def build_program(debug=False, stop_after=None):
    nc = bass.Bass("TRN2", target_bir_lowering=False)

    def din(name, shape, dt=F32):
        return nc.dram_tensor(name, list(shape), dt, kind="ExternalInput").ap()

    xT = din("xT", [D, T])
    w1g = din("w1g", [D, FF]); w1u = din("w1u", [D, FF]); w1d = din("w1d", [FF, D])
    w2g = din("w2g", [D, FF]); w2u = din("w2u", [D, FF]); w2d = din("w2d", [FF, D])
    win = din("win", [D, WIN_C])
    wout = din("wout", [2 * D, D])
    n1w = din("n1w", [128, 8]); nmw = din("nmw", [128, 8]); n2w = din("n2w", [128, 8])
    ssdw = din("ssdw", [128, 8])
    convw = din("convw", [128, 16]); convb = din("convb", [128, 4])
    dtb = din("dtb", [128, 4]); alog = din("alog", [128, 4]); dsk = din("dsk", [128, 2])
    qkw = din("qkw", [128, 2]); subw = din("subw", [128, 1])
    lamv = din("lamv", [128, 4 * 64])
    cst = din("cst", [128, 9 * 128])
    maskadd = din("maskadd", [128, 4 * 512])
    outT = nc.dram_tensor("outT", [D, T], F32, kind="ExternalOutput").ap()

    hloc = nc.dram_tensor("hloc", [4, D, 512], BF16)
    hfull = nc.dram_tensor("hfull", [4, 4 * D, 512], BF16)
    mloc = nc.dram_tensor("mloc", [8, 512, 1024], BF16)
    mfull = nc.dram_tensor("mfull", [8, 4 * 512, 1024], BF16)
    xsave = nc.dram_tensor("xsave", [D, T], F32)
    win_bf = nc.dram_tensor("win_bf", [128, 8, WIN_C], BF16)
    wout_bf = nc.dram_tensor("wout_bf", [128, 16, D], BF16)
    mk_bf = nc.dram_tensor("mk_bf", [128, 4 * 512], BF16)
    dbg = {}
    if debug:
        dbg["x1"] = nc.dram_tensor("dbg_x1", [D, T], F32, kind="ExternalOutput").ap()
        dbg["h"] = nc.dram_tensor("dbg_h", [4, D, 512], BF16, kind="ExternalOutput").ap()
        dbg["m"] = nc.dram_tensor("dbg_m", [8, 512, 1024], BF16, kind="ExternalOutput").ap()

    es = ExitStack()
    with es:
        sems = {e: es.enter_context(nc.semaphore("c_" + e)) for e in Sched.CE}
        dma_sems = {
            "sp": [es.enter_context(nc.semaphore(f"dsp{i}")) for i in range(8)],
            "pool": [es.enter_context(nc.semaphore(f"dpl{i}")) for i in range(8)],
        }
        dma_sems["cc1"] = [es.enter_context(nc.semaphore(f"cc1_{i}")) for i in range(2)]
        dma_sems["cc2"] = [es.enter_context(nc.semaphore(f"cc2_{i}")) for i in range(2)]
        K = Sched(nc, sems, dma_sems)
        tap_evs = []

        def tap(name, ap, shape, dt, reads):
            if not debug:
                return
            t = nc.dram_tensor("tap_" + name, list(shape), dt, kind="ExternalOutput").ap()
            tap_evs.append(K.dma("sp", lambda h: h.dma_start(out=t, in_=ap), reads=reads, writes=[Buf("tap")]))

        ps = es.enter_context(nc.psum_tensor("ps", [128, 8 * 512], F32))
        banks = [Buf(f"bank{i}") for i in range(8)]

        def bank_ap(i, n=512):
            return ps[:, i * 512:i * 512 + n]

        bank_rr = [0]

        def next_bank(pool=(0, 1, 2, 3, 4, 5, 6, 7)):
            i = pool[bank_rr[0] % len(pool)]
            bank_rr[0] += 1
            return i

        def mm(out_ap, lhsT, rhs, start, stop, reads, bank, tp=None):
            if tp is None:
                K.op("pe", lambda h, o=out_ap, l=lhsT, r=rhs, s=start, p=stop: h.matmul(o, lhsT=l, rhs=r, start=s, stop=p),
                     reads=reads, writes=[banks[bank]])
            else:
                K.op("pe", lambda h, o=out_ap, l=lhsT, r=rhs, s=start, p=stop, tp=tp: h.matmul(
                    o, lhsT=l, rhs=r, start=s, stop=p, tile_position=tp), reads=reads, writes=[banks[bank]])

        def rstd_from_ss(bank, n_feat, rstd_ap, rstd_buf, tmp_ap, tmp_buf):
            K.op("act", lambda h: h.activation(out=tmp_ap, in_=bank_ap(bank), func=AF.Ln, bias=eps_col[:, 0:1], scale=1.0 / n_feat),
                 reads=[banks[bank], b_const], writes=[tmp_buf])
            K.op("act", lambda h: h.activation(out=rstd_ap, in_=tmp_ap, func=AF.Exp, scale=-0.5),
                 reads=[tmp_buf], writes=[rstd_buf])

        def rmsnorm_tokens(xres, xres_b, wcol, wcol_b, dst_fn, dst_bufs, t0, scr):
            sq, sq_b, rstd, rstd_b, tmp, tmp_b = scr
            tt = t0 // 512
            K.op("act", lambda h: h.activation(out=sq[:], in_=xres[:, :, t0:t0 + 512], func=AF.Square),
                 reads=[xres_b[kt][tt] for kt in range(8)], writes=[sq_b])
            bk = next_bank()
            for kt in range(8):
                mm(bank_ap(bk), ones_bf[:], sq[:, kt, :], kt == 0, kt == 7, [sq_b, b_const], bk)
            rstd_from_ss(bk, D, rstd[:], rstd_b, tmp[:], tmp_b)
            for kt in range(8):
                K.op("dve", lambda h, kt=kt: h.scalar_tensor_tensor(
                    out=dst_fn(kt), in0=xres[:, kt, t0:t0 + 512], scalar=wcol[:, kt:kt + 1], in1=rstd[:],
                    op0=ALU.mult, op1=ALU.mult),
                    reads=[xres_b[kt][tt], rstd_b, wcol_b], writes=dst_bufs)

        def ffn(xres, xres_b, wcol, wcol_b, wg, wu, wd, bufs, after_st=None, mid_st=None):
            (hT, hT_b, actT, actT_b, wgu, wgu_b, wdb, wdb_b, sg, sg_b, scr) = bufs
            for st in range(2):
                for tl in range(2):
                    rmsnorm_tokens(xres, xres_b, wcol, wcol_b,
                                   lambda kt, tl=tl: hT[:, kt, tl * 512:(tl + 1) * 512], [hT_b[tl]],
                                   st * 1024 + tl * 512, scr)

                def load_gu(blk):
                    slot = blk % 2
                    for j, w in enumerate((wg, wu)):
                        src = w[:, blk * 256:(blk + 1) * 256].rearrange("(kt p) c -> p kt c", p=128)
                        K.dma("pool", lambda h, j=j, slot=slot, src=src: h.dma_start(out=wgu[:, slot, j, :, :], in_=src),
                              writes=[wgu_b[slot][j]])

                def load_d(q):
                    slot = q % 2
                    src = wd[:, q * 256:(q + 1) * 256].rearrange("(f p) c -> p f c", p=128)
                    for hf in range(2):
                        K.dma("pool", lambda h, slot=slot, src=src, hf=hf: h.dma_start(
                            out=wdb[:, slot, hf * 11:(hf + 1) * 11, :], in_=src[:, hf * 11:(hf + 1) * 11, :]),
                            writes=[wdb_b[slot][hf]])

                load_gu(0)
                for blk in range(11):
                    if blk + 1 < 11:
                        load_gu(blk + 1)
                    elif True:
                        load_d(0)
                    if blk == 2 and mid_st is not None:
                        mid_st(st)
                    slot = blk % 2
                    for fl in range(2):
                        f = 2 * blk + fl
                        for tl in range(2):
                            bg = next_bank()
                            bu = next_bank()
                            for j, bk in ((0, bg), (1, bu)):
                                for kt in range(8):
                                    mm(bank_ap(bk), wgu[:, slot, j, kt, fl * 128:(fl + 1) * 128],
                                       hT[:, kt, tl * 512:(tl + 1) * 512], kt == 0, kt == 7,
                                       [wgu_b[slot][j], hT_b[tl]], bk)
                            si = (f * 2 + tl) % 2
                            K.op("act", lambda h, bg=bg, si=si: h.activation(out=sg[:, si, :], in_=bank_ap(bg), func=AF.Silu),
                                 reads=[banks[bg]], writes=[sg_b[si]])
                            K.op("dve", lambda h, bu=bu, si=si, f=f, tl=tl: h.tensor_tensor(
                                out=actT[:, f, tl * 512:(tl + 1) * 512], in0=sg[:, si, :], in1=bank_ap(bu), op=ALU.mult),
                                reads=[sg_b[si], banks[bu]], writes=[actT_b[f][tl]])
                for q in range(4):
                    if q + 1 < 4:
                        load_d(q + 1)
                    slot = q % 2
                    for dl in range(2):
                        d = 2 * q + dl
                        for tl in range(2):
                            bk = next_bank()
                            for f in range(NF):
                                mm(bank_ap(bk), wdb[:, slot, f, dl * 128:(dl + 1) * 128],
                                   actT[:, f, tl * 512:(tl + 1) * 512], f == 0, f == NF - 1,
                                   [wdb_b[slot][f // 11], actT_b[f][tl]], bk)
                            t0 = st * 1024 + tl * 512
                            tt = t0 // 512
                            K.op("dve", lambda h, bk=bk, d=d, t0=t0: h.scalar_tensor_tensor(
                                out=xres[:, d, t0:t0 + 512], in0=bank_ap(bk), scalar=0.5, in1=xres[:, d, t0:t0 + 512],
                                op0=ALU.mult, op1=ALU.add),
                                reads=[banks[bk], xres_b[d][tt]], writes=[xres_b[d][tt]])
                if after_st is not None:
                    after_st(st)

        def alloc_ffn_bufs(st_, tg):
            hT = st_.enter_context(nc.sbuf_tensor("hT" + tg, [128, 8, 1024], BF16))
            actT = st_.enter_context(nc.sbuf_tensor("actT" + tg, [128, NF, 1024], BF16))
            wgu = st_.enter_context(nc.sbuf_tensor("wgu" + tg, [128, 2, 2, 8, 256], BF16))
            wdb = st_.enter_context(nc.sbuf_tensor("wdb" + tg, [128, 2, NF, 256], BF16))
            sg = st_.enter_context(nc.sbuf_tensor("sg" + tg, [128, 2, 512], F32))
            sq = st_.enter_context(nc.sbuf_tensor("sq" + tg, [128, 8, 512], BF16))
            rstd = st_.enter_context(nc.sbuf_tensor("rstd" + tg, [128, 512], F32))
            tmp = st_.enter_context(nc.sbuf_tensor("ntmp" + tg, [128, 512], F32))
            return (hT, [Buf("hT0"), Buf("hT1")], actT, [[Buf("a"), Buf("a")] for _ in range(NF)],
                    wgu, [[Buf("w"), Buf("w")] for _ in range(2)], wdb, [[Buf("w"), Buf("w")] for _ in range(2)],
                    sg, [Buf("sg0"), Buf("sg1")], (sq, Buf("sq"), rstd, Buf("rstd"), tmp, Buf("tmp")))

        ones_bf = es.enter_context(nc.sbuf_tensor("ones_bf", [128, 128], BF16))
        ident_bf = es.enter_context(nc.sbuf_tensor("ident_bf", [128, 128], BF16))
        bones_bf = es.enter_context(nc.sbuf_tensor("bones_bf", [128, 128], BF16))
        cst_f = es.enter_context(nc.sbuf_tensor("cst_f", [128, 9, 128], F32))
        onesAB_bf = es.enter_context(nc.sbuf_tensor("onesAB_bf", [128, 2, 128], BF16))
        eps_col = es.enter_context(nc.sbuf_tensor("eps_col", [128, 1], F32))
        ncols = es.enter_context(nc.sbuf_tensor("ncols", [128, 4, 8], F32))
        b_const = Buf("const")
        b_ncols = Buf("ncols")

        with ExitStack() as sa:
            xres = sa.enter_context(nc.sbuf_tensor("xres", [128, 8, T], F32))
            xres_b = [[Buf("x") for _ in range(4)] for _ in range(8)]
            fb = alloc_ffn_bufs(sa, "A")
            actT = fb[2]
            block = sa.enter_context(nc.Block())
            K.dma("sp", lambda h: h.dma_start(out=cst_f[:], in_=cst.rearrange("p (a b) -> p a b", a=9)), writes=[b_const])
            for j, src in enumerate((n1w, nmw, n2w, ssdw)):
                K.dma("sp", lambda h, j=j, src=src: h.dma_start(out=ncols[:, j, :], in_=src), writes=[b_ncols])
            K.op("dve", lambda h: h.tensor_copy(out=ident_bf[:], in_=cst_f[:, 0, :]), reads=[b_const], writes=[b_const])
            K.op("dve", lambda h: h.tensor_copy(out=ones_bf[:], in_=cst_f[:, 1, :]), reads=[b_const], writes=[b_const])
            K.op("dve", lambda h: h.tensor_copy(out=bones_bf[:], in_=cst_f[:, 2, :]), reads=[b_const], writes=[b_const])
            K.op("dve", lambda h: h.tensor_copy(out=onesAB_bf[:], in_=cst_f[:, 5:7, :]), reads=[b_const], writes=[b_const])
            K.op("dve", lambda h: h.memset(eps_col[:], EPS), writes=[b_const])
            for hf in range(2):
                for kt in range(8):
                    K.dma("sp", lambda h, kt=kt, hf=hf: h.dma_start(
                        out=xres[:, kt, hf * 1024:(hf + 1) * 1024], in_=xT[kt * 128:(kt + 1) * 128, hf * 1024:(hf + 1) * 1024]),
                        writes=[xres_b[kt][2 * hf], xres_b[kt][2 * hf + 1]])
            hmix = sa.enter_context(nc.sbuf_tensor("hmix", [128, 8, 1024], BF16))
            hm_b = [Buf("hm") for _ in range(4)]
            b_xsave = Buf("xsave")
            b_winbf = [Buf("winbf") for _ in range(3)]
            b_woutbf = [Buf("woutbf") for _ in range(4)]
            b_mkbf = Buf("mkbf")
            b_hloc = [Buf("hloc") for _ in range(4)]
            b_hfull = [Buf("hfull") for _ in range(4)]
            pend_ag = []

            def mixnorm_st(st):
                for kt in range(8):
                    K.dma("sp", lambda h, kt=kt, st=st: h.dma_start(
                        out=xsave.ap()[kt * 128:(kt + 1) * 128, st * 1024:(st + 1) * 1024], in_=xres[:, kt, st * 1024:(st + 1) * 1024]),
                        reads=[xres_b[kt][2 * st], xres_b[kt][2 * st + 1]], writes=[b_xsave])
                for tl in range(2):
                    tt = 2 * st + tl
                    rmsnorm_tokens(xres, xres_b, ncols[:, 1, :], b_ncols,
                                   lambda kt, tl=tl: hmix[:, kt, tl * 512:(tl + 1) * 512], [hm_b[tt]],
                                   tt * 512, fb[10])
                    K.dma("sp", lambda h, tl=tl, tt=tt: h.dma_start(
                        out=hloc.ap()[tt].rearrange("(kt p) t -> p kt t", p=128),
                        in_=hmix[:, :, tl * 512:(tl + 1) * 512]),
                        reads=[hm_b[tt]], writes=[b_hloc[tt]])
                    pend_ag.append(tt)

            def issue_pending(st=None, extra=()):
                if st == 0:
                    for j, (c0, c1) in enumerate(((0, 512), (512, 1024), (1024, WIN_C))):
                        K.dma("pool", lambda h, c0=c0, c1=c1: h.dma_start(
                            out=win_bf.ap()[:, :, c0:c1], in_=win[:, c0:c1].rearrange("(kt p) c -> p kt c", p=128)),
                            writes=[b_winbf[j]])
                    K.dma("pool", lambda h: h.dma_start(out=mk_bf.ap(), in_=maskadd), writes=[b_mkbf])
                while pend_ag:
                    tt = pend_ag.pop(0)
                    K.dma("pool", lambda h, tt=tt: h.collective_compute(
                        "AllGather", ALU.bypass, replica_groups=GROUPS,
                        ins=[hloc.ap()[tt].opt()], outs=[hfull.ap()[tt].opt()]),
                        reads=[b_hloc[tt]] + list(extra), writes=[b_hfull[tt]], inc=1, grp="cc1")

            ffn(xres, xres_b, ncols[:, 0, :], b_ncols, w1g, w1u, w1d, fb, after_st=mixnorm_st, mid_st=issue_pending)
            if debug:
                evs = []
                for kt in range(8):
                    sl = slice(kt * 128, (kt + 1) * 128)
                    evs.append(K.dma("sp", lambda h, sl=sl: h.dma_start(out=dbg["x1"][sl, :], in_=xsave.ap()[sl, :]),
                                     reads=[b_xsave], writes=[Buf("d1")]))
                    if kt < 4:
                        evs.append(K.dma("sp", lambda h, kt=kt: h.dma_start(out=dbg["h"][kt], in_=hloc.ap()[kt]),
                                         reads=[b_hloc[kt]], writes=[Buf("d2")]))
                for ev in evs:
                    K.wait_event("sp", ev)
            K.flush(block)

        b_mloc = [Buf("mloc") for _ in range(8)]
        b_mfull = [Buf("mfull") for _ in range(8)]
        if stop_after in ("A", "A1", "A2"):
            return nc
        with ExitStack() as sb:
            def sbt(name, shape, dt):
                return sb.enter_context(nc.sbuf_tensor(name, shape, dt))
            KT = sbt("KT", [128, 2, S], BF16)
            Vt = sbt("Vt", [128, 2, 64, 128], BF16)
            KT_b = [[Buf("k") for _ in range(16)] for _ in range(2)]
            V_b = [Buf("v") for _ in range(16)]
            winb = sbt("winb", [128, 8, WIN_C], BF16)
            b_win = [Buf("win") for _ in range(6)]
            hB = sbt("hB", [128, 2, 8, 512], BF16)
            hB_b = [Buf("hB0"), Buf("hB1")]
            mk = sbt("mk", [128, 4, 512], BF16)
            prm = sbt("prm", [128, 32], F32)
            prm2 = sbt("prm2", [128, 8], F32)
            lam_t = sbt("lam_t", [128, 4, 64], F32)
            lam_s = sbt("lam_s", [128, 4], F32)
            b_prm = Buf("prm")
            qk_raw = sbt("qk_raw", [128, 2, 512], F32)
            qk_raw_b = [Buf("qkr") for _ in range(2)]
            sqb2 = sbt("sqb2", [128, 512], BF16); sqb2_b = Buf("sqb2")
            rstd2 = sbt("rstdB2", [128, 512], F32); rstd2_b = Buf("rstdB2")
            ntmp2 = sbt("ntmpB2", [128, 512], F32); ntmp2_b = Buf("ntmpB2")
            sqb = sbt("sqb", [128, 512], BF16); sqb_b = Buf("sqb")
            rstd = sbt("rstdB", [128, 512], F32); rstd_b = Buf("rstdB")
            ntmp = sbt("ntmpB", [128, 512], F32); ntmp_b = Buf("ntmpB")
            qn = sbt("qn", [128, 2, 2, 512], BF16); qn_b = [[Buf("qn0"), Buf("qn1")] for _ in range(2)]
            zs = sbt("zs", [128, 2, 512], F32); zs_b = [Buf("zs0"), Buf("zs1")]
            xbc = sbt("xbc", [128, 4, 515], F32); xbc_b = [Buf("xbc") for _ in range(4)]
            cacc = sbt("cacc", [128, 4, 512], F32); cacc_b = [Buf("cacc") for _ in range(4)]
            xsT = sbt("xsT", [128, 4, 512], BF16); xsT_b = [Buf("xsT") for _ in range(4)]
            dtpre2 = sbt("dtpre", [128, 2, 4, 4], F32); dt_b2 = [Buf("dt0"), Buf("dt1")]
            dtv2 = sbt("dtv", [128, 2, 4, 4], F32)
            dtA2 = sbt("dtA", [128, 2, 4, 4], F32)
            xBtok = sbt("xBtok", [128, 2, 384], BF16); xBtok_b = [Buf("xBt0"), Buf("xBt1")]
            Rall = sbt("Rall", [128, 4, 128], F32); Rall_b = Buf("Rall")
            LT = sbt("LT", [128, 4, 128], F32); LT_b = Buf("LT")
            Eall = sbt("Eall", [128, 4, 128], F32); Eall_b = Buf("Eall")
            CBm = sbt("CBm", [128, 128], F32); CBm_b = Buf("CBm")
            MT = sbt("MT", [128, 4, 128], BF16); MT_b = Buf("MT")
            CeT = sbt("CeT", [128, 4, 128], BF16); CeT_b = Buf("CeT")
            wst = sbt("wst", [128, 4], F32); wst_b = Buf("wst")
            xdtp = sbt("xdtp", [128, 4, 128], BF16); xdtp_b = Buf("xdtp")
            xdtd = sbt("xdtd", [128, 4, 64], BF16); xdtd_b = Buf("xdtd")
            hst = sbt("hst", [128, 4, 64], F32); hst_b = Buf("hst")
            hstt = sbt("hstt", [128, 4, 64], F32); hstt_b = Buf("hstt")
            hstp = sbt("hstp", [128, 4, 128], BF16); hstp_b = Buf("hstp")
            yv = sbt("yv", [128, 2, 512], F32); yv_b = [Buf("yv0"), Buf("yv1")]
            Pb = sbt("Pb", [128, 2, 4, 512], BF16); P_b = [[Buf("P") for _ in range(4)] for _ in range(2)]
            rec = sbt("rec", [128, 2, 512], F32); rec_b = [Buf("rec0"), Buf("rec1")]
            recp = sbt("recp", [128, 512], F32); recp_b = Buf("recp")
            o12 = sbt("o12", [128, 2, 512], F32); o12_b = [Buf("o1"), Buf("o2")]
            of = sbt("of", [128, 512], F32); of_b = Buf("of")
            mixT = sbt("mixT", [128, 2, 4, 512], BF16); mixT_b = [[Buf("mx") for _ in range(4)] for _ in range(2)]
            block = sb.enter_context(nc.Block())

            identf = cst_f[:, 0, :]
            onesf = cst_f[:, 1, :]
            triLE = cst_f[:, 3, :]
            SUf = cst_f[:, 4, :]

            for hk in range(2):
                K.dma("sp", lambda h, hk=hk: h.dma_start(out=winb[:, hk * 4:(hk + 1) * 4, :], in_=win_bf.ap()[:, hk * 4:(hk + 1) * 4, :]),
                      reads=b_winbf, writes=b_win)
            K.dma("sp", lambda h: h.dma_start(out=mk[:], in_=mk_bf.ap().rearrange("p (a b) -> p a b", a=4)),
                  reads=[b_mkbf], writes=[b_const])
            for src, c0, n in ((convw, 0, 16), (convb, 16, 4), (dtb, 20, 4), (alog, 24, 4), (dsk, 28, 2), (qkw, 30, 2)):
                K.dma("sp", lambda h, src=src, c0=c0, n=n: h.dma_start(out=prm[:, c0:c0 + n], in_=src), writes=[b_prm])
            K.dma("sp", lambda h: h.dma_start(out=prm2[:, 0:1], in_=subw), writes=[b_prm])
            K.dma("sp", lambda h: h.dma_start(out=lam_t[:], in_=lamv.rearrange("p (a b) -> p a b", a=4)), writes=[b_prm])
            convw_s = prm[:, 0:16]; convb_s = prm[:, 16:20]; dtb_s = prm[:, 20:24]; alog_s = prm[:, 24:28]
            dsk_s = prm[:, 28:30]; qkw_s = prm[:, 30:32]
            A_row = prm2[:, 1:5]; lamneg = prm2[:, 5:6]; sub08 = prm2[:, 6:7]
            K.op("act", lambda h: h.activation(out=A_row, in_=alog_s, func=AF.Exp), reads=[b_prm], writes=[b_prm])
            K.op("dve", lambda h: h.tensor_scalar(out=A_row, in0=A_row, scalar1=-1.0, scalar2=None, op0=ALU.mult),
                 reads=[b_prm], writes=[b_prm])
            K.op("dve", lambda h: h.tensor_tensor(out=lam_t[:, 0, :], in0=lam_t[:, 0, :], in1=lam_t[:, 1, :], op=ALU.mult),
                 reads=[b_prm], writes=[b_prm])
            K.op("dve", lambda h: h.tensor_tensor(out=lam_t[:, 2, :], in0=lam_t[:, 2, :], in1=lam_t[:, 3, :], op=ALU.mult),
                 reads=[b_prm], writes=[b_prm])
            K.op("dve", lambda h: h.reduce_sum(out=lam_s[:, 0:1], in_=lam_t[:, 0, :], axis=AX.X), reads=[b_prm], writes=[b_prm])
            K.op("dve", lambda h: h.reduce_sum(out=lam_s[:, 1:2], in_=lam_t[:, 2, :], axis=AX.X), reads=[b_prm], writes=[b_prm])
            K.op("act", lambda h: h.activation(out=lam_s[:, 2:4], in_=lam_s[:, 0:2], func=AF.Exp), reads=[b_prm], writes=[b_prm])
            K.op("dve", lambda h: h.tensor_tensor(out=lamneg, in0=lam_s[:, 3:4], in1=lam_s[:, 2:3], op=ALU.subtract),
                 reads=[b_prm], writes=[b_prm])
            K.op("dve", lambda h: h.tensor_scalar(out=lamneg, in0=lamneg, scalar1=-0.2, scalar2=None, op0=ALU.add),
                 reads=[b_prm], writes=[b_prm])
            K.op("dve", lambda h: h.tensor_scalar(out=sub08, in0=prm2[:, 0:1], scalar1=0.8, scalar2=None, op0=ALU.mult),
                 reads=[b_prm], writes=[b_prm])
            K.op("pool", lambda h: h.memset(hst[:], 0.0), writes=[hst_b])
            K.op("pool", lambda h: h.memset(hstp[:], 0.0), writes=[hstp_b])
            K.op("pool", lambda h: h.memset(xdtp[:], 0.0), writes=[xdtp_b])
            K.op("pool", lambda h: h.memset(xbc[:], 0.0), writes=xbc_b)

            O1, O2, LB = 3, 4, 5
            GP = (6, 7)
            GPA = (0, 1, 2)
            s_rr = [0]

            def load_h(i):
                slot = i % 2
                src = hfull.ap()[i % 4][(i // 4) * D:(i // 4 + 1) * D, :].rearrange("(kt p) t -> p kt t", p=128)
                K.dma("sp", lambda h, slot=slot, src=src: h.dma_start(out=hB[:, slot, :, :], in_=src),
                      reads=[b_hfull[i % 4]], writes=[hB_b[slot]])

            bankA = [0]

            def gp_bank():
                return next_bank(GP)

            def proj_tile(i, ct):
                slot = i % 2
                bk = gp_bank()
                for kt in range(8):
                    mm(bank_ap(bk), winb[:, kt, ct * 128:(ct + 1) * 128], hB[:, slot, kt, :], kt == 0, kt == 7,
                       [b_win[ct // 2], hB_b[slot]], bk)
                return bk

            def st_qk(i, ct):
                def f():
                    t0 = i * 512
                    qs = i % 2
                    bk = proj_tile(i, ct)
                    rs = ct % 2
                    K.op("dve", lambda h: h.tensor_copy(out=qk_raw[:, rs, :], in_=bank_ap(bk)),
                         reads=[banks[bk]], writes=[qk_raw_b[rs]])
                    K.op("dve", lambda h: h.tensor_tensor(out=sqb[:], in0=qk_raw[:, rs, :], in1=qk_raw[:, rs, :], op=ALU.mult),
                         reads=[qk_raw_b[rs]], writes=[sqb_b])
                    bk2 = gp_bank()
                    mm(bank_ap(bk2), bones_bf[:], sqb[:], True, True, [sqb_b, b_const], bk2)
                    rstd_from_ss(bk2, 64, rstd[:], rstd_b, ntmp[:], ntmp_b)
                    if ct < 2:
                        dst, dstb, wc = qn[:, qs, ct, :], qn_b[qs][ct], qkw_s[:, 0:1]
                    else:
                        dst, dstb, wc = KT[:, ct - 2, t0:t0 + 512], KT_b[ct - 2][i], qkw_s[:, 1:2]
                    K.op("dve", lambda h: h.scalar_tensor_tensor(
                        out=dst, in0=qk_raw[:, rs, :], scalar=wc, in1=rstd[:], op0=ALU.mult, op1=ALU.mult),
                        reads=[qk_raw_b[rs], rstd_b, b_prm], writes=[dstb])
                return f

            def st_z(i, pz):
                def f():
                    bk = proj_tile(i, 4 + pz)
                    K.op("dve", lambda h: h.tensor_copy(out=zs[:, pz, :], in_=bank_ap(bk)),
                         reads=[banks[bk]], writes=[zs_b[pz]])
                return f

            def st_xbc(i, c):
                def f():
                    bk = proj_tile(i, 6 + c)
                    K.op("dve", lambda h: h.tensor_copy(out=xbc[:, c, 3:515], in_=bank_ap(bk)),
                         reads=[banks[bk]], writes=[xbc_b[c]])
                return f

            def st_v(i, c):
                def f():
                    slot = i % 2
                    dtpre = dtpre2[:, i % 2]
                    dt_b = dt_b2[i % 2]
                    bk = gp_bank()
                    for kt in range(8):
                        mm(bank_ap(bk, 260), hB[:, slot, kt, c * 128:(c + 1) * 128], winb[:, kt, 1280:1540], kt == 0, kt == 7,
                           [b_win[5], hB_b[slot]], bk)
                    for hv in range(2):
                        K.op("dve", lambda h, hv=hv: h.tensor_copy(
                            out=Vt[:, hv, 4 * i + c, :], in_=ps[:, bk * 512 + hv * 128:bk * 512 + (hv + 1) * 128]),
                            reads=[banks[bk]], writes=[V_b[i]])
                    K.op("dve", lambda h: h.tensor_tensor(
                        out=dtpre[:, c, :], in0=ps[:, bk * 512 + 256:bk * 512 + 260], in1=dtb_s, op=ALU.add),
                        reads=[banks[bk], b_prm], writes=[dt_b])
                return f

            def st_dt(i):
                def f():
                    dtpre = dtpre2[:, i % 2]; dtv = dtv2[:, i % 2]; dtA = dtA2[:, i % 2]
                    dt_b = dt_b2[i % 2]
                    K.op("act", lambda h: h.activation(out=dtv[:], in_=dtpre[:], func=AF.Exp), reads=[dt_b], writes=[dt_b])
                    K.op("act", lambda h: h.activation(out=dtv[:], in_=dtv[:], func=AF.Ln, bias=1.0), reads=[dt_b], writes=[dt_b])
                    K.op("dve", lambda h: h.tensor_tensor(out=dtA[:], in0=dtv[:], in1=A_row.unsqueeze(1).to_broadcast([128, 4, 4]),
                                                          op=ALU.mult), reads=[dt_b, b_prm], writes=[dt_b])
                return f

            def st_conv(i, c):
                def f():
                    K.op("dve", lambda h: h.tensor_scalar(out=cacc[:, c, :], in0=xbc[:, c, 0:512],
                                                          scalar1=convw_s[:, c * 4:c * 4 + 1], scalar2=None, op0=ALU.mult),
                         reads=[xbc_b[c], b_prm], writes=[cacc_b[c]])
                    for k in range(1, 4):
                        K.op("dve", lambda h, k=k: h.scalar_tensor_tensor(
                            out=cacc[:, c, :], in0=xbc[:, c, k:k + 512], scalar=convw_s[:, c * 4 + k:c * 4 + k + 1],
                            in1=cacc[:, c, :], op0=ALU.mult, op1=ALU.add),
                            reads=[xbc_b[c], b_prm, cacc_b[c]], writes=[cacc_b[c]])
                    K.op("dve", lambda h: h.tensor_copy(out=xbc[:, c, 0:3], in_=xbc[:, c, 512:515]),
                         reads=[xbc_b[c]], writes=[xbc_b[c]])
                return f

            def st_silu(i):
                def f():
                    for c in range(4):
                        K.op("act", lambda h, c=c: h.activation(out=xsT[:, c, :], in_=cacc[:, c, :], func=AF.Silu,
                                                                bias=convb_s[:, c:c + 1]),
                             reads=[cacc_b[c], b_prm], writes=[xsT_b[c]])
                    for pz in range(2):
                        K.op("act", lambda h, pz=pz: h.activation(out=zs[:, pz, :], in_=zs[:, pz, :], func=AF.Silu),
                             reads=[zs_b[pz]], writes=[zs_b[pz]])
                return f

            def prelude_stages(i):
                L = [st_qk(i, ct) for ct in range(4)] + [st_v(i, c) for c in range(4)] + [st_dt(i)]
                for c in range(4):
                    L += [st_xbc(i, c), st_conv(i, c)]
                L += [st_z(i, pz) for pz in range(2)]
                L += [st_silu(i)]
                return L

            def ssd_stages(i):
                L = []
                ms = i % 2
                dtv = dtv2[:, i % 2]; dtA = dtA2[:, i % 2]
                dt_b = dt_b2[i % 2]
                for c in range(4):
                    cs = slice(c * 128, (c + 1) * 128)
                    ts_ = c % 2
                    st = {}

                    def s1(c=c, cs=cs, ts_=ts_, st=st):
                        bk = gp_bank()
                        for j in range(3):
                            mm(ps[:, bk * 512 + j * 128:bk * 512 + (j + 1) * 128], xsT[:, j, cs], ident_bf[:], True, True,
                               [xsT_b[j], b_const], bk)
                        K.op("dve", lambda h: h.tensor_copy(out=xBtok[:, ts_, :], in_=bank_ap(bk, 384)),
                             reads=[banks[bk]], writes=[xBtok_b[ts_]])
                        K.op("dve", lambda h: h.tensor_tensor(
                            out=Rall[:], in0=triLE.unsqueeze(1).to_broadcast([128, 4, 128]),
                            in1=dtA[:, c, :].unsqueeze(2).to_broadcast([128, 4, 128]), op=ALU.mult),
                            reads=[b_const, dt_b], writes=[Rall_b])

                    def s2(c=c, cs=cs, ts_=ts_, st=st):
                        Rflat = Rall[:].rearrange("p a b -> p (a b)")
                        bseg = gp_bank()
                        mm(bank_ap(bseg), SUf, Rflat, True, True, [b_const, Rall_b], bseg)
                        K.op("act", lambda h: h.activation(out=LT[:].rearrange("p a b -> p (a b)"), in_=bank_ap(bseg), func=AF.Exp),
                             reads=[banks[bseg]], writes=[LT_b])
                        bacs = gp_bank()
                        mm(bank_ap(bacs), onesf, Rflat, True, True, [b_const, Rall_b], bacs)
                        K.op("act", lambda h: h.activation(out=Eall[:].rearrange("p a b -> p (a b)"), in_=bank_ap(bacs), func=AF.Exp),
                             reads=[banks[bacs]], writes=[Eall_b])

                    def s2b(c=c, cs=cs, ts_=ts_, st=st):
                        bcb = gp_bank()
                        mm(bank_ap(bcb, 128), xsT[:, 2, cs], xsT[:, 3, cs], True, True, [xsT_b[2], xsT_b[3]], bcb)
                        K.op("dve", lambda h: h.tensor_tensor(out=CBm[:], in0=bank_ap(bcb, 128), in1=triLE, op=ALU.mult),
                             reads=[banks[bcb], b_const], writes=[CBm_b])

                    def s3(c=c, cs=cs, ts_=ts_, st=st):
                        K.op("dve", lambda h: h.tensor_tensor(out=MT[:], in0=LT[:], in1=CBm[:].unsqueeze(1).to_broadcast([128, 4, 128]),
                                                              op=ALU.mult), reads=[LT_b, CBm_b], writes=[MT_b])
                        K.op("dve", lambda h: h.tensor_tensor(out=CeT[:], in0=Eall[:],
                                                               in1=xsT[:, 3, cs].unsqueeze(1).to_broadcast([128, 4, 128]),
                                                               op=ALU.mult), reads=[Eall_b, xsT_b[3]], writes=[CeT_b])
                        K.op("dve", lambda h: h.tensor_tensor(out=wst[:], in0=dtv[:, c, :], in1=LT[:, :, 127], op=ALU.mult),
                             reads=[dt_b, LT_b], writes=[wst_b])
                        xt22 = xBtok[:, ts_, 0:256].rearrange("p (a b c) -> p a b c", a=2, b=2)
                        dt22 = dtv[:, c, :].rearrange("p (a b) -> p a b", a=2)
                        xdtp4 = xdtp[:].rearrange("p (a b) c -> p a b c", a=2)
                        xt4 = xBtok[:, ts_, 0:256].rearrange("p (a b) -> p a b", a=4)
                        for half in range(2):
                            K.op("dve", lambda h, half=half: h.tensor_tensor(
                                out=xdtp4[:, :, half, half * 64:(half + 1) * 64], in0=xt22[:, :, half, :],
                                in1=dt22[:, :, half:half + 1].to_broadcast([128, 2, 64]), op=ALU.mult),
                                reads=[xBtok_b[ts_], dt_b], writes=[xdtp_b])
                        K.op("dve", lambda h: h.tensor_tensor(out=xdtd[:], in0=xt4,
                                                              in1=wst[:].unsqueeze(2).to_broadcast([128, 4, 64]), op=ALU.mult),
                             reads=[xBtok_b[ts_], wst_b], writes=[xdtd_b])

                    def s4(c=c, cs=cs, ts_=ts_, st=st):
                        by = gp_bank()
                        st["by"] = by
                        for pr in range(2):
                            yo = ps[:, by * 512 + pr * 128:by * 512 + (pr + 1) * 128]
                            for hh in range(2):
                                h4 = 2 * pr + hh
                                mm(yo, xdtp[:, h4, :], MT[:, h4, :], hh == 0, False, [xdtp_b, MT_b], by)
                            for hh in range(2):
                                h4 = 2 * pr + hh
                                mm(yo, hstp[:, h4, :], CeT[:, h4, :], False, hh == 1, [hstp_b, CeT_b], by)
                        bst = gp_bank()
                        st["bst"] = bst
                        mm(bank_ap(bst, 256), xBtok[:, ts_, 256:384], xdtd[:].rearrange("p a b -> p (a b)"), True, True,
                           [xBtok_b[ts_], xdtd_b], bst)

                    def s5(c=c, cs=cs, ts_=ts_, st=st):
                        by, bst = st["by"], st["bst"]
                        for pr in range(2):
                            K.op("dve", lambda h, pr=pr: h.scalar_tensor_tensor(
                                out=yv[:, pr, cs], in0=xsT[:, pr, cs], scalar=dsk_s[:, pr:pr + 1],
                                in1=ps[:, by * 512 + pr * 128:by * 512 + (pr + 1) * 128], op0=ALU.mult, op1=ALU.add),
                                reads=[xsT_b[pr], banks[by], b_prm], writes=[yv_b[pr]])
                        K.op("dve", lambda h: h.tensor_tensor(out=hstt[:], in0=hst[:], in1=Eall[:, :, 127:128].to_broadcast([128, 4, 64]),
                                                              op=ALU.mult), reads=[hst_b, Eall_b], writes=[hstt_b])
                        K.op("dve", lambda h: h.tensor_tensor(out=hst[:], in0=hstt[:],
                                                              in1=bank_ap(bst, 256).rearrange("p (a b) -> p a b", a=4), op=ALU.add),
                             reads=[hstt_b, banks[bst]], writes=[hst_b])
                        hst22 = hst[:].rearrange("p (a b) c -> p a b c", a=2)
                        hstp4 = hstp[:].rearrange("p (a b) c -> p a b c", a=2)
                        for half in range(2):
                            K.op("dve", lambda h, half=half: h.tensor_copy(
                                out=hstp4[:, :, half, half * 64:(half + 1) * 64], in_=hst22[:, :, half, :]),
                                reads=[hst_b], writes=[hstp_b])

                    def s45(s4=s4, s5=s5):
                        s4()
                        s5()

                    L += [s1, s2, s2b, s3, s45]

                def gate():
                    for pr in range(2):
                        K.op("dve", lambda h, pr=pr: h.tensor_tensor(out=mixT[:, ms, pr, :], in0=yv[:, pr, :], in1=zs[:, pr, :],
                                                                      op=ALU.mult),
                             reads=[yv_b[pr], zs_b[pr]], writes=[mixT_b[ms][pr]])
                L.append(gate)
                return L

            def attention(i, side):
                nkt = 4 * i + 4
                qs = i % 2
                ms = i % 2
                npairs = 2 * nkt
                per = -(-len(side) // npairs) if side else 0
                for hd in range(2):
                    sbank = {}

                    def emit_S(kt):
                        diag = kt >= 4 * i
                        kti = kt // 4
                        pslot = kt % 4
                        for mp, lo in ((0, 0), (1, 64)):
                            sb_ = GPA[s_rr[0] % 3]
                            s_rr[0] += 1
                            mm(bank_ap(sb_), KT[lo:lo + 64, hd, kt * 128:(kt + 1) * 128], qn[lo:lo + 64, qs, hd, :], True, not diag,
                               [KT_b[hd][kti], qn_b[qs][hd]], sb_)
                            if diag:
                                mm(bank_ap(sb_), ident_bf[:], mk[:, kt - 4 * i, :], False, True, [b_const], sb_)
                            K.op("act", lambda h, sb_=sb_, mp=mp, pslot=pslot: h.activation(
                                out=Pb[:, mp, pslot, :], in_=bank_ap(sb_), func=AF.Exp, scale=0.125),
                                reads=[banks[sb_]], writes=[P_b[mp][pslot]])

                    def emit_PV(kt):
                        kti = kt // 4
                        pslot = kt % 4
                        for mp, ob in ((0, O1), (1, O2)):
                            mm(bank_ap(ob), Vt[:, hd, kt, :], Pb[:, mp, pslot, :], kt == 0, kt == nkt - 1,
                               [V_b[kti], P_b[mp][pslot]], ob)
                        if kt % 2 == 1:
                            j = 0
                            for k2 in (kt - 1, kt):
                                for mp in range(2):
                                    mm(ps[32 * j:32 * j + 32, LB * 512:(LB + 1) * 512], ones_bf[:, 0:32], Pb[:, mp, k2 % 4, :],
                                       k2 < 2, k2 >= nkt - 2, [b_const, P_b[mp][k2 % 4]], LB, tp=(0, 32 * j))
                                    j += 1

                    emit_S(0)
                    for kt in range(nkt):
                        if kt + 1 < nkt:
                            emit_S(kt + 1)
                        emit_PV(kt)
                        for _ in range(per):
                            if side:
                                side.pop(0)()
                    K.op("act", lambda h: h.activation(out=o12[:, 0, :], in_=bank_ap(O1), func=AF.Copy),
                         reads=[banks[O1]], writes=[o12_b[0]])
                    K.op("dve", lambda h: h.tensor_copy(out=o12[:, 1, :], in_=bank_ap(O2)), reads=[banks[O2]], writes=[o12_b[1]])
                    K.op("act", lambda h: h.activation(out=recp[:], in_=bank_ap(LB), func=AF.Copy), reads=[banks[LB]], writes=[recp_b])

                    def e1():
                        pass

                    def e2(hd=hd):
                        st = {}
                        for mp in range(2):
                            bb = gp_bank()
                            mm(bank_ap(bb), cst_f[:, 7 + mp, :], recp[:], True, True, [b_const, recp_b], bb)
                            K.op("act", lambda h, mp=mp, bb=bb: h.activation(out=rec[:, mp, :], in_=bank_ap(bb), func=AF.Ln),
                                 reads=[banks[bb]], writes=[rec_b[mp]])
                            K.op("act", lambda h, mp=mp: h.activation(out=rec[:, mp, :], in_=rec[:, mp, :], func=AF.Exp, scale=-1.0),
                                 reads=[rec_b[mp]], writes=[rec_b[mp]])
                            K.op("dve", lambda h, mp=mp: h.tensor_tensor(out=o12[:, mp, :], in0=o12[:, mp, :], in1=rec[:, mp, :],
                                                                         op=ALU.mult),
                                 reads=[rec_b[mp], o12_b[mp]], writes=[o12_b[mp]])

                    def e3():
                        K.op("dve", lambda h: h.scalar_tensor_tensor(out=of[:], in0=o12[:, 1, :], scalar=lamneg, in1=o12[:, 0, :],
                                                                     op0=ALU.mult, op1=ALU.add),
                             reads=[o12_b[0], o12_b[1], b_prm], writes=[of_b])
                        K.op("dve", lambda h: h.tensor_tensor(out=sqb2[:], in0=of[:], in1=of[:], op=ALU.mult),
                             reads=[of_b], writes=[sqb2_b])

                    def e4():
                        bk2 = gp_bank()
                        mm(bank_ap(bk2), ones_bf[:], sqb2[:], True, True, [sqb2_b, b_const], bk2)
                        rstd_from_ss(bk2, 128, rstd2[:], rstd2_b, ntmp2[:], ntmp2_b)

                    def e5(hd=hd):
                        K.op("dve", lambda h: h.scalar_tensor_tensor(
                            out=mixT[:, ms, 2 + hd, :], in0=of[:], scalar=sub08, in1=rstd2[:], op0=ALU.mult, op1=ALU.mult),
                            reads=[of_b, rstd2_b, b_prm], writes=[mixT_b[ms][2 + hd]])

                    ep = [e1, e2, e3, e4, e5]
                    if hd == 0:
                        side[0:0] = ep
                    else:
                        carry.extend(ep)
                while side:
                    side.pop(0)()

            carry = []

            def issue_ag(k):
                K.dma("pool", lambda h, k=k: h.collective_compute(
                    "AllGather", ALU.bypass, replica_groups=GROUPS,
                    ins=[mloc.ap()[k].opt()], outs=[mfull.ap()[k].opt()]),
                    reads=[b_mloc[k], hB_b[0], hB_b[1]], writes=[b_mfull[k]], inc=1, grp="cc2")

            def st_store(i):
                def f():
                    ms = i % 2
                    K.dma("sp", lambda h: h.dma_start(
                        out=mloc.ap()[i // 2][:, (i % 2) * 512:(i % 2 + 1) * 512].rearrange("(j p) t -> p j t", p=128),
                        in_=mixT[:, ms, :, :]),
                        reads=mixT_b[ms], writes=[b_mloc[i // 2]])
                    if i % 2 == 1:
                        issue_ag(i // 2)
                return f

            load_h(0)
            load_h(1)
            issue_pending(extra=list(b_win) + [b_const, b_prm, hB_b[0], hB_b[1]])
            for f in prelude_stages(0):
                f()
            for rr in range(4):
                for part, base in ((0, 256 * rr), (1, D + 256 * rr)):
                    K.dma("pool", lambda h, rr=rr, part=part, base=base: h.dma_start(
                        out=wout_bf.ap()[:, rr * 4 + part * 2:rr * 4 + part * 2 + 2, :],
                        in_=wout[base:base + 256, :].rearrange("(j p) c -> p j c", p=128)),
                        writes=[b_woutbf[rr]])
            for i in range(16):
                side = list(carry)
                del carry[:]
                if i >= 1:
                    side.append(st_store(i - 1))
                sl = ssd_stages(i)
                pl = prelude_stages(i + 1) if i + 1 < 16 else []
                while sl:
                    side += sl[:2]
                    del sl[:2]
                    if len(pl) > 3:
                        side.append(pl.pop(0))
                side += pl
                attention(i, side)
                if i + 2 < 16:
                    load_h(i + 2)
            for f in carry:
                f()
            st_store(15)()
            if debug:
                evs = []
                for k in range(8):
                    evs.append(K.dma("sp", lambda h, k=k: h.dma_start(out=dbg["m"][k], in_=mloc.ap()[k]),
                                     reads=[b_mloc[k]], writes=[Buf("d3")]))
                for ev in evs + tap_evs:
                    K.wait_event("sp", ev)
            K.flush(block)

        if stop_after == "B":
            return nc
        with ExitStack() as sc:
            xres = sc.enter_context(nc.sbuf_tensor("xresC", [128, 8, T], F32))
            xres_b = [[Buf("x") for _ in range(4)] for _ in range(8)]
            with ExitStack() as sc1:
                mx = sc1.enter_context(nc.sbuf_tensor("mx", [128, 16, T], BF16))
                mx_b = [[Buf("m") for _ in range(4)] for _ in range(16)]
                woutb = sc1.enter_context(nc.sbuf_tensor("woutb", [128, 16, D], BF16))
                wout_b = [Buf("wo") for _ in range(16)]
                sq4 = sc1.enter_context(nc.sbuf_tensor("sq4", [128, 2, 4, 512], BF16)); sq4_b = [Buf("sq4"), Buf("sq4")]
                rstd = sc1.enter_context(nc.sbuf_tensor("rstdC", [128, 2, 512], F32)); rstd_b = [Buf("r"), Buf("r")]
                ntmp = sc1.enter_context(nc.sbuf_tensor("ntmpC", [128, 2, 512], F32)); ntmp_b = [Buf("t"), Buf("t")]
                block = sc1.enter_context(nc.Block())
                def load_mx(h, tt):
                    pid = h.partition_id()
                    bq = pid % 4
                    src = mfull.ap().rearrange("(b k2) (rj p) (hh t) -> p b k2 rj hh t", b=4, p=128, hh=2)
                    return h.dma_start(out=mx[:, :, tt * 512:(tt + 1) * 512],
                                       in_=src[:, bass.ds(bq, 1), tt // 2, :, tt % 2, :])

                def issue_mx(tt):
                    K.dma("sp", lambda h, tt=tt: load_mx(h, tt), reads=(b_mfull[0:7] if tt < 2 else b_mfull),
                          writes=[mx_b[rj][tt] for rj in range(16)])

                issue_mx(0)
                issue_mx(1)
                for q4 in range(4):
                    K.dma("sp", lambda h, q4=q4: h.dma_start(out=woutb[:, q4 * 4:(q4 + 1) * 4, :], in_=wout_bf.ap()[:, q4 * 4:(q4 + 1) * 4, :]),
                          reads=b_woutbf, writes=wout_b[q4 * 4:(q4 + 1) * 4])
                for kt in range(8):
                    K.dma("sp", lambda h, kt=kt: h.dma_start(out=xres[:, kt, :], in_=xsave.ap()[kt * 128:(kt + 1) * 128, :]),
                          reads=[b_xsave], writes=xres_b[kt])
                issue_mx(2)
                issue_mx(3)
                for tt in range(4):
                    tsl = slice(tt * 512, (tt + 1) * 512)
                    for g in range(2):
                        tiles = [8 * g + 0, 8 * g + 1, 8 * g + 4, 8 * g + 5]
                        for a, rj in enumerate(tiles):
                            K.op("dve", lambda h, a=a, rj=rj, tsl=tsl, g=g: h.tensor_tensor(out=sq4[:, g, a, :], in0=mx[:, rj, tsl],
                                                                                            in1=mx[:, rj, tsl], op=ALU.mult),
                                 reads=[mx_b[rj][tt]], writes=[sq4_b[g]])
                        bk = next_bank()
                        for a in range(4):
                            mm(bank_ap(bk), ones_bf[:], sq4[:, g, a, :], a == 0, a == 3, [sq4_b[g], b_const], bk)
                        rstd_from_ss(bk, 512, rstd[:, g, :], rstd_b[g], ntmp[:, g, :], ntmp_b[g])
                        for a, rj in enumerate(tiles):
                            rr, j = rj // 4, rj % 4
                            wk = 2 * rr + j
                            K.op("dve", lambda h, rj=rj, tsl=tsl, wk=wk, g=g: h.scalar_tensor_tensor(
                                out=mx[:, rj, tsl], in0=mx[:, rj, tsl], scalar=ncols[:, 3, wk:wk + 1], in1=rstd[:, g, :],
                                op0=ALU.mult, op1=ALU.mult),
                                reads=[mx_b[rj][tt], rstd_b[g], b_ncols], writes=[mx_b[rj][tt]])
                    for d in range(8):
                        bk = next_bank()
                        for rj in range(16):
                            mm(bank_ap(bk), woutb[:, rj, d * 128:(d + 1) * 128], mx[:, rj, tt * 512:(tt + 1) * 512], rj == 0, rj == 15,
                               [wout_b[rj], mx_b[rj][tt]], bk)
                        K.op("dve", lambda h, bk=bk, d=d, tt=tt: h.tensor_tensor(
                            out=xres[:, d, tt * 512:(tt + 1) * 512], in0=bank_ap(bk), in1=xres[:, d, tt * 512:(tt + 1) * 512], op=ALU.add),
                            reads=[banks[bk], xres_b[d][tt]], writes=[xres_b[d][tt]])
                K.flush(block)
            with ExitStack() as sc2:
                fb = alloc_ffn_bufs(sc2, "F2")
                block = sc2.enter_context(nc.Block())
                ffn(xres, xres_b, ncols[:, 2, :], b_ncols, w2g, w2u, w2d, fb)
                b_out = Buf("out")
                evs = []
                for kt in range(8):
                    evs.append(K.dma("sp", lambda h, kt=kt: h.dma_start(out=outT[kt * 128:(kt + 1) * 128, :], in_=xres[:, kt, :]),
                                     reads=xres_b[kt], writes=[b_out]))
                for ev in evs:
                    K.wait_event("sp", ev)
                K.flush(block)
    return nc


def _consts():
    p = np.arange(128)
    ident = np.eye(128, dtype=np.float32)
    ones = np.ones((128, 128), np.float32)
    bones = (p[:, None] // 64 == p[None, :] // 64).astype(np.float32)
    triLE = (p[:, None] <= p[None, :]).astype(np.float32)
    SU = (p[:, None] > p[None, :]).astype(np.float32)
    onesA = np.zeros((128, 128), np.float32); onesA[:, :64] = 1.0
    onesB = np.zeros((128, 128), np.float32); onesB[:, 64:] = 1.0
    selA = np.zeros((128, 128), np.float32); selA[0, :] = 1.0; selA[64, :] = 1.0
    selB = np.zeros((128, 128), np.float32); selB[32, :] = 1.0; selB[96, :] = 1.0
    cst = np.concatenate([ident, ones, bones, triLE, SU, onesA, onesB, selA, selB], axis=1)
    q = np.arange(512)
    mk = np.zeros((128, 4, 512), np.float32)
    for r in range(4):
        mk[:, r, :] = np.where(q[None, :] >= 128 * r + p[:, None], 0.0, NEG)
    return np.ascontiguousarray(cst), np.ascontiguousarray(mk.reshape(128, 2048))


def _col(v):
    return np.ascontiguousarray(np.asarray(v, np.float32).reshape(-1, 128).T)


def make_in_maps(inp):
    f = lambda a: np.asarray(a, np.float32)
    x = f(inp["x"])
    cst, mk = _consts()
    w_in = f(inp["w_in"])[0]
    conv_w = f(inp["conv_w"])[0]
    conv_b = f(inp["conv_b"])[0]
    shared = dict(
        w1g=f(inp["ffn1_w_gate"])[0], w1u=f(inp["ffn1_w_up"])[0], w1d=f(inp["ffn1_w_down"])[0],
        w2g=f(inp["ffn2_w_gate"])[0], w2u=f(inp["ffn2_w_up"])[0], w2d=f(inp["ffn2_w_down"])[0],
        wout=f(inp["w_out"])[0],
        n1w=_col(inp["ffn1_norm_w"][0]), nmw=_col(inp["mix_norm_w"][0]), n2w=_col(inp["ffn2_norm_w"][0]),
        ssdw=_col(inp["ssd_norm_w"][0]),
        qkw=np.ascontiguousarray(np.stack([np.tile(f(inp["q_norm_w"])[0], 2), np.tile(f(inp["k_norm_w"])[0], 2)], axis=1)),
        subw=np.ascontiguousarray(f(inp["attn_subln_w"])[0].reshape(128, 1)),
        lamv=np.ascontiguousarray(np.broadcast_to(np.concatenate(
            [f(inp["lambda_q1"])[0], f(inp["lambda_k1"])[0], f(inp["lambda_q2"])[0], f(inp["lambda_k2"])[0]])[None, :], (128, 256))),
        cst=cst, maskadd=mk,
    )
    maps = []
    for c in range(8):
        s, r = c // 4, c % 4
        g = r // 2
        cols = np.concatenate([
            np.arange(2576 + 256 * r, 2576 + 256 * r + 256),
            np.arange(3600 + 256 * r, 3600 + 256 * r + 256),
            np.arange(256 * r, 256 * r + 256),
            np.arange(1024 + 256 * r, 1024 + 256 * r + 256),
            np.arange(2048 + 128 * g, 2048 + 128 * g + 128),
            np.arange(2304 + 128 * g, 2304 + 128 * g + 128),
            np.arange(4624 + 256 * r, 4624 + 256 * r + 256),
            np.arange(2560 + 4 * r, 2560 + 4 * r + 4),
        ])
        cch = np.concatenate([np.arange(256 * r, 256 * r + 256), np.arange(1024 + 128 * g, 1024 + 128 * g + 128),
                              np.arange(1280 + 128 * g, 1280 + 128 * g + 128)])
        cw = conv_w[:, cch]
        convw = np.ascontiguousarray(cw.reshape(4, 4, 128).transpose(2, 1, 0).reshape(128, 16))
        convb = np.ascontiguousarray(conv_b[cch].reshape(4, 128).T)
        hs = slice(4 * r, 4 * r + 4)
        m = dict(shared)
        m.update(
            xT=np.ascontiguousarray(x[s, r * T:(r + 1) * T, :].T),
            win=np.ascontiguousarray(w_in[:, cols]),
            convw=convw, convb=convb,
            dtb=np.ascontiguousarray(np.broadcast_to(f(inp["dt_bias"])[0, hs][None, :], (128, 4))),
            alog=np.ascontiguousarray(np.broadcast_to(f(inp["a_log"])[0, hs][None, :], (128, 4))),
            dsk=np.ascontiguousarray(np.repeat(f(inp["d_skip"])[0, hs].reshape(2, 2), 64, axis=1).reshape(2, 128).T),
        )
        maps.append(m)
    return maps


_NC_CACHE = {}


def kernel(**inputs):
    if "nc" not in _NC_CACHE:
        _NC_CACHE["nc"] = build_program()
    nc = _NC_CACHE["nc"]
    maps = make_in_maps(inputs)
    res = run_bass_kernel_spmd(nc, maps, core_ids=list(range(8)))
    out = np.empty((2, S, D), np.float32)
    for c in range(8):
        s, r = c // 4, c % 4
        out[s, r * T:(r + 1) * T, :] = np.asarray(res.results[c]["outT"]).T
    return out
```
